# Optimizing a Trainium2 kernel written in Bass

```python
import math
import jax, jax.numpy as jnp
from jax import lax
import numpy as np

D_MODEL = 1024
BATCH = 32
SEQ = 256
DEPTH = 4
DEC_BATCH = 4
DEC_SEQ = 2048
PAST_LEN = 512

GRID_W = 64
F_GROUPS = 4
F_GROUP_W = 128
F_W = F_GROUPS * F_GROUP_W
CONV_W = 512
CONV_K = 31
N_HEADS = 8
QK_NOPE = 64
QK_ROPE = 32
V_DIM = 64
QK_DIM = QK_NOPE + QK_ROPE
Q_LORA = 384
KV_LORA = 256
ROPE_BASE = 10000.0
Q_BLOCK = 128
N_BRANCH = 3
IN_COLS = F_W + 2 * CONV_W + Q_LORA + KV_LORA + QK_ROPE
D_FF = 2816
FFN_K = 3
EPS = 1e-6

kernel_name = 'hybrid_fourier_conformer_mla_prefix_dit'


def rms_norm(x, g):
    xf = x.astype(jnp.float32)
    y = xf * lax.rsqrt(jnp.mean(xf * xf, axis=-1, keepdims=True) + EPS)
    return y.astype(x.dtype) * g


def layer_norm(x, g, b):
    xf = x.astype(jnp.float32)
    mu = jnp.mean(xf, axis=-1, keepdims=True)
    var = jnp.mean(jnp.square(xf - mu), axis=-1, keepdims=True)
    y = (xf - mu) * lax.rsqrt(var + EPS)
    return y.astype(x.dtype) * g + b


def adaln(cvec, ada_w, ada_b):
    m = jax.nn.silu(cvec) @ ada_w + ada_b
    return jnp.split(m[:, None, :], 6, axis=-1)


def grid_rope_tables(L):
    rows = L // GRID_W
    row = jnp.repeat(jnp.arange(rows), GRID_W, total_repeat_length=rows * GRID_W).astype(jnp.float32)
    col = jnp.tile(jnp.arange(GRID_W), rows).astype(jnp.float32)
    half = QK_ROPE // 2
    inv = ROPE_BASE ** (-jnp.arange(0, half, 2, dtype=jnp.float32) / half)
    ang = jnp.stack([row[:, None] * inv, col[:, None] * inv], axis=1)
    return jnp.cos(ang), jnp.sin(ang)


def apply_rope2d(x, cos, sin):
    shp = x.shape
    xr = x.astype(jnp.float32).reshape(shp[:-1] + (2, 2, QK_ROPE // 4))
    x1 = xr[..., 0, :]
    x2 = xr[..., 1, :]
    c = cos[None, :, None]
    s = sin[None, :, None]
    out = jnp.stack([x1 * c - x2 * s, x1 * s + x2 * c], axis=-2)
    return out.reshape(shp).astype(x.dtype)


def mla_queries(q_lat, q_norm_g, w_q_up, qk_q_g, rope):
    B, L, _ = q_lat.shape
    q = (rms_norm(q_lat, q_norm_g) @ w_q_up).reshape(B, L, N_HEADS, QK_DIM)
    q = rms_norm(q, qk_q_g)
    if rope is not None:
        q = jnp.concatenate([q[..., :QK_NOPE], apply_rope2d(q[..., QK_NOPE:], rope[0], rope[1])], axis=-1)
    return q


def mla_keys_values(c_kv, k_rope, w_kv_up, qk_k_g, rope):
    B, L, _ = c_kv.shape
    kv = (c_kv @ w_kv_up).reshape(B, L, N_HEADS, QK_NOPE + V_DIM)
    k_nope = kv[..., :QK_NOPE]
    v = kv[..., QK_NOPE:]
    k_r = jnp.broadcast_to(k_rope[:, :, None, :], (B, L, N_HEADS, QK_ROPE))
    k = rms_norm(jnp.concatenate([k_nope, k_r], axis=-1), qk_k_g)
    if rope is not None:
        k = jnp.concatenate([k[..., :QK_NOPE], apply_rope2d(k[..., QK_NOPE:], rope[0], rope[1])], axis=-1)
    return k, v


def block_attention(q, k, v):
    B, Lq, H, D = q.shape
    nb = Lq // Q_BLOCK
    qb = q.reshape(B, nb, Q_BLOCK, H, D).transpose(1, 0, 2, 3, 4)
    scale = 1.0 / math.sqrt(QK_DIM)

    def one(qblk):
        s = jnp.einsum('bqhd,bkhd->bhqk', qblk, k, preferred_element_type=jnp.float32) * scale
        p = jax.nn.softmax(s, axis=-1).astype(v.dtype)
        return jnp.einsum('bhqk,bkhe->bqhe', p, v)

    o = lax.map(one, qb)
    return o.transpose(1, 0, 2, 3, 4).reshape(B, Lq, H * V_DIM)


def fourier_mix(f_in):
    B, L, _ = f_in.shape
    z = f_in.astype(jnp.float32).reshape(B, L, F_GROUPS, F_GROUP_W)
    y = jnp.fft.fft2(z, axes=(1, 3), norm='ortho').real
    return y.reshape(B, L, F_W).astype(f_in.dtype)


def depthwise_conv(x, w, b):
    C = x.shape[-1]
    pad = w.shape[0] // 2
    y = lax.conv_general_dilated(x, w[:, None, :].astype(x.dtype), (1,), [(pad, pad)],
                                 dimension_numbers=('NWC', 'WIO', 'NWC'), feature_group_count=C)
    return y + b


def conv_module(conv_in, dw, dw_b, ln_g, ln_b, w_out):
    a, g = jnp.split(conv_in, 2, axis=-1)
    u = a * jax.nn.sigmoid(g)
    u = depthwise_conv(u, dw, dw_b)
    u = jax.nn.silu(layer_norm(u, ln_g, ln_b))
    return u @ w_out


def conv_ffn(h, up, dw, dw_b, down):
    u = depthwise_conv(h @ up, dw, dw_b)
    a, b = jnp.split(u, 2, axis=-1)
    return (jax.nn.silu(a) * b) @ down


def trunk_layer(x, cvec, W, l, rope, ctx_cache):
    sh1, sc1, g1, sh2, sc2, g2 = adaln(cvec, W['ada_w'][l], W['ada_b'][l])
    h = rms_norm(x, W['norm1_g'][l]) * (1 + sc1) + sh1
    u = h @ W['w_in'][l]
    cuts = [F_W, F_W + 2 * CONV_W, F_W + 2 * CONV_W + Q_LORA, F_W + 2 * CONV_W + Q_LORA + KV_LORA]
    f_in, conv_in, q_lat, kv_lat, k_rope = jnp.split(u, cuts, axis=-1)
    c_kv = rms_norm(kv_lat, W['kv_norm_g'][l])
    y_f = fourier_mix(f_in) @ W['w_fourier'][l]
    y_c = conv_module(conv_in, W['conv_dw'][l], W['conv_dw_b'][l], W['conv_ln_g'][l],
                      W['conv_ln_b'][l], W['w_conv_out'][l])
    q = mla_queries(q_lat, W['q_norm_g'][l], W['w_q_up'][l], W['qk_q_g'][l], rope)
    k, v = mla_keys_values(c_kv, k_rope, W['w_kv_up'][l], W['qk_k_g'][l], rope)
    if ctx_cache is not None:
        kc, vc = mla_keys_values(ctx_cache[0], ctx_cache[1], W['w_kv_up'][l], W['qk_k_g'][l], None)
        k = jnp.concatenate([kc, k], axis=1)
        v = jnp.concatenate([vc, v], axis=1)
    y_a = block_attention(q, k, v) @ W['w_mla_out'][l]
    gates = jax.nn.sigmoid(h @ W['w_gate'][l] + W['b_gate'][l])
    gf, gc, ga = jnp.split(gates, N_BRANCH, axis=-1)
    x = x + g1 * ((gf * y_f + gc * y_c + ga * y_a) @ W['w_out'][l])
    h2 = rms_norm(x, W['norm2_g'][l]) * (1 + sc2) + sh2
    x = x + g2 * conv_ffn(h2, W['ffn_up'][l], W['ffn_dw'][l], W['ffn_dw_b'][l], W['ffn_down'][l])
    return x, c_kv, k_rope


def setup_inputs(seed: int = 0) -> dict:
    key = jax.random.key(seed)
    ks = jax.random.split(key, 40)
    f32 = jnp.float32

    def nrm(k, shape, scale):
        return jax.random.normal(k, shape, f32) * scale

    def gain(k, shape):
        return 1.0 + 0.02 * jax.random.normal(k, shape, f32)

    L = DEPTH
    return {
        'x_prompt': nrm(ks[0], (BATCH, SEQ, D_MODEL), 1.0),
        'x_sample': nrm(ks[1], (DEC_BATCH, DEC_SEQ, D_MODEL), 1.0),
        'cache_ckv': nrm(ks[2], (DEC_BATCH, DEPTH, PAST_LEN, KV_LORA), 1.0),
        'cache_krope': nrm(ks[3], (DEC_BATCH, DEPTH, PAST_LEN, QK_ROPE), 1.0),
        'c': nrm(ks[4], (DEC_BATCH, D_MODEL), 1.0),
        'c_ctx': nrm(ks[5], (D_MODEL,), 1.0),
        'ada_w': nrm(ks[6], (L, D_MODEL, 6 * D_MODEL), D_MODEL ** -0.5),
        'ada_b': nrm(ks[7], (L, 6 * D_MODEL), 0.02),
        'norm1_g': gain(ks[8], (L, D_MODEL)),
        'norm2_g': gain(ks[9], (L, D_MODEL)),
        'w_in': nrm(ks[10], (L, D_MODEL, IN_COLS), D_MODEL ** -0.5),
        'w_gate': nrm(ks[11], (L, D_MODEL, N_BRANCH * D_MODEL), D_MODEL ** -0.5),
        'b_gate': nrm(ks[12], (L, N_BRANCH * D_MODEL), 0.02),
        'w_fourier': nrm(ks[13], (L, F_W, D_MODEL), F_W ** -0.5),
        'conv_dw': nrm(ks[14], (L, CONV_K, CONV_W), CONV_K ** -0.5),
        'conv_dw_b': nrm(ks[15], (L, CONV_W), 0.02),
        'conv_ln_g': gain(ks[16], (L, CONV_W)),
        'conv_ln_b': nrm(ks[17], (L, CONV_W), 0.02),
        'w_conv_out': nrm(ks[18], (L, CONV_W, D_MODEL), CONV_W ** -0.5),
        'q_norm_g': gain(ks[19], (L, Q_LORA)),
        'w_q_up': nrm(ks[20], (L, Q_LORA, N_HEADS * QK_DIM), Q_LORA ** -0.5),
        'kv_norm_g': gain(ks[21], (L, KV_LORA)),
        'w_kv_up': nrm(ks[22], (L, KV_LORA, N_HEADS * (QK_NOPE + V_DIM)), KV_LORA ** -0.5),
        'qk_q_g': gain(ks[23], (L, QK_DIM)),
        'qk_k_g': gain(ks[24], (L, QK_DIM)),
        'w_mla_out': nrm(ks[25], (L, N_HEADS * V_DIM, D_MODEL), (N_HEADS * V_DIM) ** -0.5),
        'w_out': nrm(ks[26], (L, D_MODEL, D_MODEL), D_MODEL ** -0.5),
        'ffn_up': nrm(ks[27], (L, D_MODEL, 2 * D_FF), D_MODEL ** -0.5),
        'ffn_dw': nrm(ks[28], (L, FFN_K, 2 * D_FF), FFN_K ** -0.5),
        'ffn_dw_b': nrm(ks[29], (L, 2 * D_FF), 0.02),
        'ffn_down': nrm(ks[30], (L, D_FF, D_MODEL), D_FF ** -0.5),
    }


def reference(x_prompt, x_sample, cache_ckv, cache_krope, c, c_ctx, ada_w, ada_b, norm1_g, norm2_g,
              w_in, w_gate, b_gate, w_fourier, conv_dw, conv_dw_b, conv_ln_g, conv_ln_b, w_conv_out,
              q_norm_g, w_q_up, kv_norm_g, w_kv_up, qk_q_g, qk_k_g, w_mla_out, w_out,
              ffn_up, ffn_dw, ffn_dw_b, ffn_down):
    W = {
        'ada_w': ada_w, 'ada_b': ada_b, 'norm1_g': norm1_g, 'norm2_g': norm2_g,
        'w_in': w_in, 'w_gate': w_gate, 'b_gate': b_gate, 'w_fourier': w_fourier,
        'conv_dw': conv_dw, 'conv_dw_b': conv_dw_b, 'conv_ln_g': conv_ln_g, 'conv_ln_b': conv_ln_b,
        'w_conv_out': w_conv_out, 'q_norm_g': q_norm_g, 'w_q_up': w_q_up, 'kv_norm_g': kv_norm_g,
        'w_kv_up': w_kv_up, 'qk_q_g': qk_q_g, 'qk_k_g': qk_k_g, 'w_mla_out': w_mla_out,
        'w_out': w_out, 'ffn_up': ffn_up, 'ffn_dw': ffn_dw, 'ffn_dw_b': ffn_dw_b, 'ffn_down': ffn_down,
    }
    ctx_vec = c_ctx[None, :]
    xp = x_prompt
    ckv_list = []
    krope_list = []
    for l in range(DEPTH):
        xp, ckv, kr = trunk_layer(xp, ctx_vec, W, l, None, None)
        ckv_list.append(ckv)
        krope_list.append(kr)
    new_ckv = jnp.stack(ckv_list, axis=1)
    new_krope = jnp.stack(krope_list, axis=1)
    rope = grid_rope_tables(x_sample.shape[1])
    xs = x_sample
    for l in range(DEPTH):
        xs, _, _ = trunk_layer(xs, c, W, l, rope, (cache_ckv[:, l], cache_krope[:, l]))
    return (xp, xs, new_ckv, new_krope)
```

```python
import math
import numpy as np
import ml_dtypes
from contextlib import ExitStack
import concourse.bass as bass
import concourse.mybir as mybir
from concourse.bass_utils import run_bass_kernel_spmd

F32 = mybir.dt.float32
BF16 = mybir.dt.bfloat16
ALU = mybir.AluOpType
AF = mybir.ActivationFunctionType

ENGS = ("pe", "act", "dve", "pool", "sp")
EPOCH = 20000
NDMASEM = 6

DEPTH = 4
EPS = 1e-6


class Res:
    __slots__ = ("name", "writers", "readers")

    def __init__(self, name=""):
        self.name = name
        self.writers = []
        self.readers = []


class Op:
    __slots__ = ("eng", "idx", "fn", "waits", "needed", "is_dma", "ev", "dma_slot")

    def __init__(self, eng, idx, fn, is_dma):
        self.eng = eng
        self.idx = idx
        self.fn = fn
        self.waits = []
        self.needed = False
        self.is_dma = is_dma
        self.ev = None
        self.dma_slot = None


class Prog:
    def __init__(self, nc):
        self.nc = nc
        self.ops = {e: [] for e in ENGS}
        self.seen = {e: {} for e in ENGS}
        self.seen_dma = {e: set() for e in ENGS}
        self.dma_count = {e: 0 for e in ENGS}
        self.dma_last = {e: {} for e in ENGS}
        self.last_compute = {e: None for e in ENGS}
        self.stack = ExitStack()

    def _dep(self, op, prod, same_ok):
        if prod is None or prod is op:
            return
        F = op.eng
        if prod.is_dma:
            if id(prod) in self.seen_dma[F]:
                return
            self.seen_dma[F].add(id(prod))
            op.waits.append(prod)
            prod.needed = True
            return
        if prod.eng == F and same_ok:
            return
        if self.seen[F].get(prod.eng, -1) >= prod.idx:
            return
        self.seen[F][prod.eng] = prod.idx
        op.waits.append(prod)
        prod.needed = True

    def op(self, eng, fn, reads=(), writes=(), is_dma=False):
        lst = self.ops[eng]
        o = Op(eng, len(lst), fn, is_dma)
        if is_dma:
            k = self.dma_count[eng]
            self.dma_count[eng] = k + 1
            o.dma_slot = k % NDMASEM
            prev = self.dma_last[eng].get(o.dma_slot)
            self.dma_last[eng][o.dma_slot] = o
            if prev is not None:
                self._dep(o, prev, same_ok=False)
        cand = {}

        def add(p, same_ok):
            if p is None or p is o:
                return
            if p.is_dma:
                self._dep(o, p, same_ok)
                return
            if p.eng == eng and same_ok:
                return
            c = cand.get(p.eng)
            if c is None or c.idx < p.idx:
                cand[p.eng] = p

        for r in reads:
            for w in r.writers:
                add(w, eng == "pe")
        for w in writes:
            for ww in w.writers:
                add(ww, True)
            for rd in w.readers:
                add(rd, True)
        for p in cand.values():
            self._dep(o, p, same_ok=False)
        for r in reads:
            r.readers.append(o)
        for w in writes:
            if w.readers:
                w.writers = [o]
                w.readers = []
            else:
                w.writers.append(o)
                if len(w.writers) > 64:
                    w.writers = w.writers[-64:] if all(x.eng == o.eng and not x.is_dma for x in w.writers) else w.writers
        lst.append(o)
        if not is_dma:
            self.last_compute[eng] = o
        return o

    def barrier(self):
        lasts = [self.last_compute[e] for e in ENGS if self.last_compute[e] is not None]
        dmas = [o for e in ENGS for o in self.dma_last[e].values()]
        o = Op("sp", len(self.ops["sp"]), (lambda e: e.nop()), False)
        for p in lasts:
            self._dep(o, p, same_ok=True)
        for p in dmas:
            self._dep(o, p, same_ok=False)
        self.ops["sp"].append(o)
        self.last_compute["sp"] = o
        for F in ENGS:
            if F == "sp":
                continue
            b = Op(F, len(self.ops[F]), None, False)
            self._dep(b, o, same_ok=True)
            self.ops[F].append(b)
            for p in lasts:
                if self.seen[F].get(p.eng, -1) < p.idx:
                    self.seen[F][p.eng] = p.idx
            for p in dmas:
                self.seen_dma[F].add(id(p))

    def emit(self):
        nc = self.nc
        st = self.stack
        sems = {}
        for e in ENGS:
            cnt = 0
            ep = 0
            for o in self.ops[e]:
                if o.is_dma or o.fn is None:
                    continue
                if o.needed:
                    if cnt >= EPOCH:
                        ep += 1
                        cnt = 0
                    cnt += 1
                    key = (e, ep)
                    if key not in sems:
                        sems[key] = st.enter_context(nc.semaphore(f"s_{e}_{ep}"))
                    o.ev = (sems[key], cnt)
        dsem = {}
        for e in ENGS:
            per_slot = {}
            for o in self.ops[e]:
                if not o.is_dma:
                    continue
                key = (e, o.dma_slot)
                if key not in dsem:
                    dsem[key] = st.enter_context(nc.semaphore(f"d_{e}_{o.dma_slot}"))
                per_slot[o.dma_slot] = per_slot.get(o.dma_slot, 0) + 16
                o.ev = (dsem[key], per_slot[o.dma_slot])

        def run(ename):
            def body(eng):
                for o in self.ops[ename]:
                    for p in o.waits:
                        eng.wait_ge(p.ev[0], p.ev[1])
                    if o.fn is None:
                        continue
                    ins = o.fn(eng)
                    if o.is_dma:
                        ins.then_inc(o.ev[0], 16)
                    elif o.needed:
                        ins.then_inc(o.ev[0], 1)
                for slot, last in self.dma_last[ename].items():
                    eng.wait_ge(last.ev[0], last.ev[1])
            return body

        with nc.Block() as block:
            block.tensor(run("pe"))
            block.scalar(run("act"))
            block.vector(run("dve"))
            block.gpsimd(run("pool"))
            block.sync(run("sp"))

    def close(self):
        self.stack.close()


VOFF = {}
_o = 0
for _n, _w in (("n1g", 8), ("n2g", 8), ("bg", 24), ("cw", 124), ("cb", 4), ("lng", 4), ("lnb", 4),
               ("qng", 3), ("kvg", 2), ("fw", 132), ("fb", 44), ("qkq", 1), ("qkk", 1)):
    VOFF[_n] = _o
    _o += _w
NV = _o

WIN_COLS = 2176 + 96
ARENA_BYTES = 190 * 1024


def build_program(depth=DEPTH, passes=(0, 1)):
    nc = bass.Bass("TRN2", target_bir_lowering=False)
    P = Prog(nc)

    def din(name, shape, dt=F32):
        return nc.dram_tensor(name, list(shape), dt, kind="ExternalInput").ap()

    def dout(name, shape, dt=F32):
        return nc.dram_tensor(name, list(shape), dt, kind="ExternalOutput").ap()

    xp = din("xp", [128, 8, 1024])
    xs = din("xs", [128, 8, 2048])
    cvd = din("cv", [128, 8, 2])
    adaw = din("adaw", [DEPTH, 128, 8, 6144])
    adab = din("adab", [DEPTH, 128, 48])
    vecd = din("vec", [DEPTH, 128, NV])
    wind = din("win", [DEPTH, 128, 8, WIN_COLS])
    wmgd = din("wmg", [DEPTH, 8, 128, 4608])
    wqd = din("wq", [DEPTH, 128, 3, 768])
    wknd = din("wkn", [DEPTH, 128, 2, 512])
    wkvd = din("wkv", [DEPTH, 128, 2, 512])
    wod = din("wo", [DEPTH, 128, 8, 1024])
    fud = din("fu", [DEPTH, 22, 128, 8, 256])
    fdd = din("fd", [DEPTH, 8, 128, 22, 128])
    cckd = din("cck", [DEPTH, 128, 2, 512])
    ckrd = din("ckr", [DEPTH, 32, 512])
    onesfd = din("onesf", [128, 128])
    identd = din("ident", [128, 128])
    cscd = din("csc", [128, 256])
    dftpd = din("dftp", [128, 2, 2, 256], BF16)
    dftsd = din("dfts", [4, 128, 9, 2, 512], BF16)
    ropecd = din("ropec", [96, 2048])
    ropesd = din("ropes", [96, 2048])
    rmatd = din("rmat", [96, 96])
    seld = din("sel", [65, 64])
    yp = dout("yp", [128, 8, 1024])
    ys = dout("ys", [128, 8, 2048])
    ockv = dout("ockv", [DEPTH, 128, 2, 1024])
    okr = dout("okr", [DEPTH, 32, 1024])

    sb = lambda name, shape, dt: P.stack.enter_context(nc.sbuf_tensor("sb_" + name, list(shape), dt))
    resmap = {}

    def R(*key):
        r = resmap.get(key)
        if r is None:
            r = Res(str(key))
            resmap[key] = r
        return r

    def MM(out, lhsT, rhs, st, sp, rd, wr):
        P.op("pe", lambda e: e.matmul(out, lhsT, rhs, start=st, stop=sp), reads=rd, writes=wr)

    def ACT(out, in_, func, rd, wr, bias=None, scale=None):
        kw = {}
        if bias is not None:
            kw["bias"] = bias
        if scale is not None:
            kw["scale"] = scale
        P.op("act", lambda e: e.activation(out, in_, func, **kw), reads=rd, writes=wr)

    def TS(eng, out, in0, s1, s2, op0, op1, rd, wr):
        if s2 is None:
            P.op(eng, lambda e: e.tensor_scalar(out, in0, s1, None, op0), reads=rd, writes=wr)
        else:
            P.op(eng, lambda e: e.tensor_scalar(out, in0, s1, s2, op0, op1), reads=rd, writes=wr)

    def TT(eng, out, in0, in1, op, rd, wr):
        P.op(eng, lambda e: e.tensor_tensor(out, in0, in1, op), reads=rd, writes=wr)

    def STT(eng, out, in0, scalar, in1, op0, op1, rd, wr):
        P.op(eng, lambda e: e.scalar_tensor_tensor(out, in0, scalar, in1, op0, op1), reads=rd, writes=wr)

    def CP(eng, out, in_, rd, wr):
        if eng == "act":
            P.op("act", lambda e: e.activation(out, in_, AF.Copy), reads=rd, writes=wr)
        else:
            P.op(eng, lambda e: e.tensor_copy(out, in_), reads=rd, writes=wr)

    def RECIP(out, in_, rd, wr):
        P.op("dve", lambda e: e.reciprocal(out, in_), reads=rd, writes=wr)

    def MEMSET(eng, ap, val, wr):
        P.op(eng, lambda e: e.memset(ap, val), writes=wr)

    def DMA(q, out, in_, rd, wr):
        P.op(q, lambda e: e.dma_start(out=out, in_=in_), reads=rd, writes=wr, is_dma=True)

    banks = []
    for i in range(8):
        t = P.stack.enter_context(nc.psum_tensor(f"bank{i}", [128, 512], F32))
        banks.append((t, Res(f"bank{i}")))
    bstate = {"i": 0}

    def bank():
        b = banks[bstate["i"] % 6]
        bstate["i"] += 1
        return b

    ostate = {"i": 0}

    def obank():
        b = banks[6 + ostate["i"] % 2]
        ostate["i"] += 1
        return b

    onesf = sb("onesf", [128, 128], F32)
    onesb = sb("onesb", [128, 128], BF16)
    identb = sb("identb", [128, 128], BF16)
    cscb = sb("cscb", [128, 256], BF16)
    dftp = sb("dftp", [128, 2, 2, 256], BF16)
    rmat = sb("rmat", [96, 96], BF16)
    sel = sb("sel", [65, 64], F32)
    epsc = sb("epsc", [128, 1], F32)
    vec = sb("vecs", [128, DEPTH, NV], F32)
    modv = sb("modv", [128, DEPTH, 2, 48], F32)
    cvt = sb("cvt", [128, 8, 2], F32)
    scv = sb("scv", [128, 8, 2], BF16)
    adabt = sb("adabt", [128, DEPTH, 48], F32)
    RC = R("consts")
    DMA("sp", onesf[:], onesfd, [], [RC])
    DMA("pool", onesb[:], onesfd, [], [RC])
    DMA("pool", identb[:], identd, [], [RC])
    DMA("pool", cscb[:], cscd, [], [RC])
    DMA("sp", dftp[:], dftpd, [], [RC])
    DMA("pool", rmat[:], rmatd, [], [RC])
    DMA("sp", sel[:], seld, [], [RC])
    DMA("sp", cvt[:], cvd, [], [RC])
    for l in range(DEPTH):
        DMA("sp", vec[:, l, :], vecd[l], [], [RC])
        DMA("sp", adabt[:, l, :], adab[l], [], [RC])
    MEMSET("dve", epsc[:], EPS, [RC])

    def V(l, name, j=0, n=1, rows=128):
        o = VOFF[name] + j
        return vec[0:rows, l, o:o + n]

    arena = sb("arena", [128, ARENA_BYTES // 4], F32)
    ast = {"off": 0, "max": 0}

    def alloc(nbytes_pp, dt, shape_free, parts=128):
        off = ast["off"]
        nb = (nbytes_pp + 31) // 32 * 32
        assert off + nb <= ARENA_BYTES, f"arena overflow {off + nb}"
        ast["off"] = off + nb
        ast["max"] = max(ast["max"], ast["off"])
        ap = arena[0:parts, off // 4:(off + nb) // 4]
        esz = 4 if dt == F32 else 2
        nel = 1
        for s in shape_free:
            nel *= s
        assert nel * esz <= nb
        if dt != F32:
            ap = ap.bitcast(dt)
        ap = ap[:, 0:nel]
        if len(shape_free) == 2:
            ap = ap.rearrange("p (a b) -> p a b", a=shape_free[0])
        elif len(shape_free) == 3:
            ap = ap.rearrange("p (a b c) -> p a b c", a=shape_free[0], b=shape_free[1])
        elif len(shape_free) == 4:
            ap = ap.rearrange("p (a b c d) -> p a b c d", a=shape_free[0], b=shape_free[1], c=shape_free[2])
        return ap

    def alloc_at(off, nbytes_pp, dt, shape_free, parts=128):
        save = ast["off"]
        ast["off"] = off
        ap = alloc(nbytes_pp, dt, shape_free, parts)
        ast["off"] = save
        return ap

    def mark():
        return ast["off"]

    def release(m):
        ast["off"] = m

    ACT(scv[:], cvt[:], AF.Silu, [RC], [R("scv")])
    m0 = mark()
    wb = [alloc(8 * 1024 * 2, BF16, [8, 1024]) for _ in range(2)]
    for l in range(DEPTH):
        bk, br_ = bank()
        for v in range(6):
            w = wb[(l * 6 + v) % 2]
            rw = R("adawbuf", (l * 6 + v) % 2)
            DMA("pool", w, adaw[l, :, :, v * 1024:(v + 1) * 1024], [], [rw])
            for j in range(8):
                col = v * 8 + j
                for kc in range(8):
                    MM(bk[:, 2 * col:2 * col + 2], w[:, kc, j * 128:(j + 1) * 128], scv[:, kc, :],
                       kc == 0, kc == 7, [rw, R("scv")], [br_])
        bk3 = bk[:, 0:96].rearrange("p (a b) -> p a b", b=2)
        for ps in range(2):
            TT("dve", modv[:, l, ps, :], bk3[:, :, ps], adabt[:, l, :], ALU.add, [br_, RC], [R("modraw", l, ps)])
    MOD = sb("MOD", [128, DEPTH, 2, 48], F32)
    for l in range(DEPTH):
        for ps in range(2):
            rr = [R("modraw", l, ps), RC]
            wr = [R("MOD")]
            mv = modv[:, l, ps, :]
            STT("dve", MOD[:, l, ps, 0:8], mv[:, 8:16], 1.0, V(l, "n1g", 0, 8), ALU.add, ALU.mult, rr, wr)
            CP("dve", MOD[:, l, ps, 8:16], mv[:, 0:8], rr, wr)
            CP("dve", MOD[:, l, ps, 16:24], mv[:, 16:24], rr, wr)
            STT("dve", MOD[:, l, ps, 24:32], mv[:, 32:40], 1.0, V(l, "n2g", 0, 8), ALU.add, ALU.mult, rr, wr)
            CP("dve", MOD[:, l, ps, 32:40], mv[:, 24:32], rr, wr)
            CP("dve", MOD[:, l, ps, 40:48], mv[:, 40:48], rr, wr)
    P.barrier()
    release(m0)
    RMOD = R("MOD")

    def run_pass(ps, T, nseq, L, xin, yout, sample):
        nkctx = 512 if sample else 0
        Lk = L + nkctx
        TN = 512
        tiles = []
        for tg in range(0, T, TN):
            pcs = []
            off = 0
            while off < TN:
                s_, p0 = divmod(tg + off, L)
                ln_ = min(L - p0, TN - off)
                pcs.append((s_, p0, ln_, off))
                off += ln_
            tiles.append((tg, TN, pcs))
        HTW = nseq * (L + 2)
        UW = nseq * (L + 30)
        KW = nseq * Lk
        release(0)
        HT = alloc(8 * HTW * 2, BF16, [8, HTW])
        ZY = alloc(4 * T * 2, BF16, [4, T])
        YC = alloc(4 * T * 2, BF16, [4, T])
        OT2 = alloc(4 * T * 2, BF16, [4, T])
        m_long = mark()
        MEMSET("pool", HT, 0.0, [R("HTall")])
        P.barrier()

        def rHT(ti):
            return R("HT", ps, ti)

        def norm_to_HT(l, xt, rxt, ti, n, dst, aoff, tmpn):
            sq, rs, tmp = tmpn
            rsq, rrs, rtmp = R("sq"), R("rs"), R("ntmp")
            ACT(sq[:, :, 0:n], xt[:, :, 0:n], AF.Square, [rxt], [rsq])
            bk, rb = bank()
            for kc in range(8):
                MM(bk[:, 0:n], onesb[:], sq[:, kc, 0:n], kc == 0, kc == 7, [rsq, RC], [rb])
            ACT(rs[:, 0:n], bk[:, 0:n], AF.Ln, [rb, RC], [rrs], bias=epsc[:, 0:1], scale=1.0 / 1024)
            ACT(rs[:, 0:n], rs[:, 0:n], AF.Exp, [rrs], [rrs], scale=-0.5)
            nsl = tmp.shape[1]
            for kc in range(8):
                sl = kc % nsl
                rtmp = R("ntmp", sl)
                TT("dve", tmp[:, sl, 0:n], xt[:, kc, 0:n], rs[:, 0:n], ALU.mult, [rxt, rrs], [rtmp])
                for (off, ln_, hc) in dst:
                    ACT(HT[:, kc, hc:hc + ln_], tmp[:, sl, off:off + ln_], AF.Identity, [rtmp, RMOD], [rHT(ti), R("HTall")],
                        bias=MOD[:, l, ps, aoff + 8 + kc:aoff + 9 + kc], scale=MOD[:, l, ps, aoff + kc:aoff + kc + 1])

        for l in range(depth):
            xsrc = xin if l == 0 else yout
            release(m_long)
            QN = alloc(3 * T * 2, BF16, [3, T])
            CKV = alloc(2 * KW * 2, BF16, [2, KW])
            KR96 = alloc(KW * 4, F32, [KW], parts=96)
            m_u = mark()
            U = alloc(4 * UW * 2, BF16, [4, UW])
            m_s1 = mark()
            MEMSET("pool", U, 0.0, [R("U")])
            xts = [alloc(8 * TN * 4, F32, [8, TN]) for _ in range(2)]
            sq = alloc(8 * TN * 2, BF16, [8, TN])
            rs = alloc(TN * 4, F32, [TN])
            tmp = alloc(8 * TN * 4, F32, [8, TN])
            for ti, (tg, n, pcs) in enumerate(tiles):
                if l > 0:
                    break
                xt = xts[ti % 2]
                rxt = R("xt", ti % 2)
                DMA("sp", xt[:, :, 0:n], xsrc[:, :, tg:tg + n], [R("Y", ps, ti)], [rxt])
                norm_to_HT(l, xt, rxt, ti, n, [(0, n, tg)], 0, (sq, rs, tmp))
            P.barrier()
            release(m_s1)
            WIN = alloc(8 * WIN_COLS * 2, BF16, [8, WIN_COLS])
            wpieces = [(0, 512), (512, 1024), (1024, 1536), (1536, WIN_COLS)]
            for pi_, (a_, b_) in enumerate(wpieces):
                DMA("pool", WIN[:, :, a_:b_], wind[l, :, :, a_:b_], [], [R("WIN", pi_)])
            qf = alloc(3 * TN * 4, F32, [3, TN])
            kvf = alloc(2 * TN * 4, F32, [2, TN])
            sqq = alloc(3 * TN * 2, BF16, [3, TN])
            rs = alloc(TN * 4, F32, [TN])
            sg = alloc(TN * 4, F32, [TN])
            if sample:
                DMA("pool", CKV[:, :, 0:512], cckd[l], [], [R("CKV")])
                DMA("sp", KR96[64:96, 0:512], ckrd[l], [], [R("KR96")])
            for ti, (tg, n, pcs) in enumerate(tiles):
                rhp = [[rHT(ti), R("WIN", pi_)] for pi_ in range(4)]
                hsl = lambda kc: HT[:, kc, tg:tg + n]
                kcol = nkctx + tg
                for c in range(4):
                    bk, rb = bank()
                    for kc in range(8):
                        MM(bk[:, 0:n], WIN[:, kc, c * 128:(c + 1) * 128], hsl(kc), kc == 0, kc == 7, rhp[0], [rb])
                    CP("act", ZY[:, c, tg:tg + n], bk[:, 0:n], [rb], [R("ZY", ps)])
                for c in range(4):
                    ba, rba = bank()
                    bg, rbg = bank()
                    for kc in range(8):
                        MM(ba[:, 0:n], WIN[:, kc, 512 + c * 128:512 + (c + 1) * 128], hsl(kc), kc == 0, kc == 7, rhp[1], [rba])
                    for kc in range(8):
                        MM(bg[:, 0:n], WIN[:, kc, 1024 + c * 128:1024 + (c + 1) * 128], hsl(kc), kc == 0, kc == 7, rhp[2], [rbg])
                    ACT(sg[:, 0:n], bg[:, 0:n], AF.Sigmoid, [rbg], [R("sg")])
                    for (s_, p0, ln_, off) in pcs:
                        ucol = s_ * (L + 30) + 15 + p0
                        TT("dve", U[:, c, ucol:ucol + ln_], ba[:, off:off + ln_], sg[:, off:off + ln_], ALU.mult,
                           [rba, R("sg")], [R("U")])
                for c in range(3):
                    bk, rb = bank()
                    for kc in range(8):
                        MM(bk[:, 0:n], WIN[:, kc, 1536 + c * 128:1536 + (c + 1) * 128], hsl(kc), kc == 0, kc == 7, rhp[3], [rb])
                    CP("dve", qf[:, c, 0:n], bk[:, 0:n], [rb], [R("qf")])
                ACT(sqq[:, :, 0:n], qf[:, :, 0:n], AF.Square, [R("qf")], [R("sqq")])
                bk, rb = bank()
                for c in range(3):
                    MM(bk[:, 0:n], onesb[:], sqq[:, c, 0:n], c == 0, c == 2, [R("sqq"), RC], [rb])
                ACT(rs[:, 0:n], bk[:, 0:n], AF.Ln, [rb, RC], [R("rs1")], bias=epsc[:, 0:1], scale=1.0 / 384)
                ACT(rs[:, 0:n], rs[:, 0:n], AF.Exp, [R("rs1")], [R("rs1")], scale=-0.5)
                for c in range(3):
                    STT("dve", QN[:, c, tg:tg + n], qf[:, c, 0:n], V(l, "qng", c), rs[:, 0:n], ALU.mult, ALU.mult,
                        [R("qf"), R("rs1"), RC], [R("QN")])
                for c in range(2):
                    bk, rb = bank()
                    for kc in range(8):
                        MM(bk[:, 0:n], WIN[:, kc, 1920 + c * 128:1920 + (c + 1) * 128], hsl(kc), kc == 0, kc == 7, rhp[3], [rb])
                    CP("dve", kvf[:, c, 0:n], bk[:, 0:n], [rb], [R("kvf")])
                ACT(sqq[:, 0:2, 0:n], kvf[:, :, 0:n], AF.Square, [R("kvf")], [R("sqq")])
                bk, rb = bank()
                for c in range(2):
                    MM(bk[:, 0:n], onesb[:], sqq[:, c, 0:n], c == 0, c == 1, [R("sqq"), RC], [rb])
                ACT(rs[:, 0:n], bk[:, 0:n], AF.Ln, [rb, RC], [R("rs1")], bias=epsc[:, 0:1], scale=1.0 / 256)
                ACT(rs[:, 0:n], rs[:, 0:n], AF.Exp, [R("rs1")], [R("rs1")], scale=-0.5)
                for c in range(2):
                    STT("dve", kvf[:, c, 0:n], kvf[:, c, 0:n], V(l, "kvg", c), rs[:, 0:n], ALU.mult, ALU.mult,
                        [R("kvf"), R("rs1"), RC], [R("kvf")])
                CP("act", CKV[:, :, kcol:kcol + n], kvf[:, :, 0:n], [R("kvf")], [R("CKV")])
                if not sample:
                    DMA("sp", ockv[l, :, :, tg:tg + n], kvf[:, :, 0:n], [R("kvf")], [R("ockv")])
                bk, rb = bank()
                for kc in range(8):
                    MM(bk[0:96, 0:n], WIN[:, kc, 2176:2272], hsl(kc), kc == 0, kc == 7, rhp[3], [rb])
                CP("dve", KR96[64:96, kcol:kcol + n], bk[64:96, 0:n], [rb], [R("KR96")])
                if not sample:
                    DMA("sp", okr[l, :, tg:tg + n], KR96[64:96, kcol:kcol + n], [R("KR96")], [R("okr")])
            P.barrier()
            release(m_s1)
            diag = alloc(124 * 128 * 2, BF16, [124, 128])
            cvs = [alloc(4 * 512 * 4, F32, [4, 512]) for _ in range(2)]
            cvb = alloc(4 * 512 * 2, BF16, [4, 512])
            sqb = alloc(4 * 512 * 2, BF16, [4, 512])
            mu = alloc(512 * 4, F32, [512])
            var = alloc(512 * 4, F32, [512])
            tcv = alloc(512 * 4, F32, [512])
            diag4 = diag.rearrange("p (k c) m -> p k c m", c=4)
            cw0 = VOFF["cw"]
            cw4 = vec[:, l, cw0:cw0 + 124].rearrange("p (k c) -> p k c", c=4)
            for cc in range(4):
                TT("dve", diag4[:, :, cc, :], identb[:].unsqueeze(1).broadcast_to([128, 31, 128]),
                   cw4[:, :, cc].unsqueeze(2).broadcast_to([128, 31, 128]), ALU.mult, [RC], [R("diag", cc)])
            cot = []
            for s in range(nseq):
                for l0 in range(0, L, 512):
                    cot.append((s * (L + 30), l0, min(512, L - l0), s * L + l0))

            def conv_mm(i):
                ub, l0, ln, tq = cot[i]
                cv = cvs[i % 2]
                for cc in range(4):
                    bk, rb = bank()
                    for k in range(31):
                        MM(bk[:, 0:ln], diag[:, k * 4 + cc, :], U[:, cc, ub + l0 + k:ub + l0 + k + ln], k == 0, k == 30,
                           [R("diag", cc), R("U")], [rb])
                    ACT(cv[:, cc, 0:ln], bk[:, 0:ln], AF.Identity, [rb, RC], [R("cv", i % 2)], bias=V(l, "cb", cc), scale=1.0)

            def conv_stats(i):
                ub, l0, ln, tq = cot[i]
                cv = cvs[i % 2]
                rcv = R("cv", i % 2)
                ACT(cvb[:, :, 0:ln], cv[:, :, 0:ln], AF.Copy, [rcv], [R("cvb")])
                TT("dve", sqb[:, :, 0:ln], cv[:, :, 0:ln], cv[:, :, 0:ln], ALU.mult, [rcv], [R("sqb")])
                bm, rbm = bank()
                bq, rbq = bank()
                for cc in range(4):
                    MM(bm[:, 0:ln], onesb[:], cvb[:, cc, 0:ln], cc == 0, cc == 3, [R("cvb"), RC], [rbm])
                for cc in range(4):
                    MM(bq[:, 0:ln], onesb[:], sqb[:, cc, 0:ln], cc == 0, cc == 3, [R("sqb"), RC], [rbq])
                TS("dve", mu[:, 0:ln], bm[:, 0:ln], 1.0 / 512, None, ALU.mult, None, [rbm], [R("mu")])
                TT("dve", var[:, 0:ln], mu[:, 0:ln], mu[:, 0:ln], ALU.mult, [R("mu")], [R("var")])
                STT("dve", var[:, 0:ln], bq[:, 0:ln], 1.0 / 512, var[:, 0:ln], ALU.mult, ALU.subtract,
                    [rbq, R("var")], [R("var")])
                ACT(var[:, 0:ln], var[:, 0:ln], AF.Ln, [R("var"), RC], [R("var")], bias=epsc[:, 0:1], scale=1.0)
                ACT(var[:, 0:ln], var[:, 0:ln], AF.Exp, [R("var")], [R("var")], scale=-0.5)
                for cc in range(4):
                    TT("dve", tcv[:, 0:ln], cv[:, cc, 0:ln], mu[:, 0:ln], ALU.subtract, [rcv, R("mu")], [R("tcv")])
                    TT("dve", tcv[:, 0:ln], tcv[:, 0:ln], var[:, 0:ln], ALU.mult, [R("tcv"), R("var")], [R("tcv")])
                    ACT(YC[:, cc, tq:tq + ln], tcv[:, 0:ln], AF.Silu, [R("tcv"), RC], [R("YC")],
                        bias=V(l, "lnb", cc), scale=V(l, "lng", cc))

            conv_mm(0)
            for i in range(len(cot)):
                if i + 1 < len(cot):
                    conv_mm(i + 1)
                conv_stats(i)
            P.barrier()
            release(m_u)
            wq = alloc(3 * 768 * 2, BF16, [3, 768])
            wkn = alloc(2 * 512 * 2, BF16, [2, 512])
            wkv = alloc(2 * 512 * 2, BF16, [2, 512])
            rAW = R("attw")
            DMA("pool", wq, wqd[l], [], [rAW])
            DMA("pool", wkn, wknd[l], [], [rAW])
            DMA("pool", wkv, wkvd[l], [], [rAW])
            nkb = Lk // 128
            KTs = [alloc(Lk * 2, BF16, [Lk], parts=96) for _ in range(2)]
            QTs = [alloc(512 * 2, BF16, [512], parts=96) for _ in range(2)]
            VTs = [alloc(nkb * 65 * 2, BF16, [nkb, 65]) for _ in range(2)]
            PTs = [alloc(512 * 2, BF16, [512]) for _ in range(4)]
            tset = {}
            for nm in ("K", "Q"):
                tset[nm] = dict(
                    kf=alloc(512 * 4, F32, [512], parts=96), kg=alloc(512 * 2, BF16, [512], parts=96),
                    sqk=alloc(512 * 2, BF16, [512], parts=96), rsk=alloc(512 * 4, F32, [512], parts=96),
                    t1=alloc(512 * 4, F32, [512], parts=96), t2=alloc(512 * 4, F32, [512], parts=96))
            osb = alloc(512 * 4, F32, [512], parts=65)
            rd_ = alloc(512 * 4, F32, [512], parts=64)
            obf = alloc(512 * 2, BF16, [512], parts=64)
            if sample:
                ropec = alloc(2048 * 4, F32, [2048], parts=96)
                ropes = alloc(2048 * 4, F32, [2048], parts=96)
                DMA("sp", ropec, ropecd, [], [R("rope")])
                DMA("sp", ropes, ropesd, [], [R("rope")])
            for i_ in range(2):
                MEMSET("pool", VTs[i_][:, :, 64:65], 1.0, [R("VT", i_)])
            pti = [0]
            deferred = [None]

            def norm_steps(nm, n, gname, pos0, dest, rdest, pre):
                t = tset[nm]
                kf, kg, sqk, rsk, t1, t2 = t["kf"], t["kg"], t["sqk"], t["rsk"], t["t1"], t["t2"]
                rk = lambda x: R(x, nm)
                st = []
                hold = {}

                def s1():
                    pre(kf, rk("kf"))
                    TT("dve", sqk[:, 0:n], kf[:, 0:n], kf[:, 0:n], ALU.mult, [rk("kf")], [rk("sqk")])
                st.append(s1)

                def s2():
                    b2, rb2 = bank()
                    hold["b2"] = (b2, rb2)
                    MM(b2[0:96, 0:n], onesb[0:96, 0:96], sqk[:, 0:n], True, True, [rk("sqk"), RC], [rb2])
                    ACT(rsk[:, 0:n], b2[0:96, 0:n], AF.Ln, [rb2, RC], [rk("rsk")], bias=epsc[0:96, 0:1], scale=1.0 / 96)
                    ACT(rsk[:, 0:n], rsk[:, 0:n], AF.Exp, [rk("rsk")], [rk("rsk")], scale=-0.5)
                    if pos0 is None:
                        STT("dve", dest, kf[:, 0:n], V(l, gname, 0, 1, 96), rsk[:, 0:n], ALU.mult, ALU.mult,
                            [rk("kf"), rk("rsk"), RC], [rdest])
                    else:
                        STT("dve", kg[:, 0:n], kf[:, 0:n], V(l, gname, 0, 1, 96), rsk[:, 0:n], ALU.mult, ALU.mult,
                            [rk("kf"), rk("rsk"), RC], [rk("kg")])
                        TT("dve", t1[:, 0:n], kg[:, 0:n], ropec[:, pos0:pos0 + n], ALU.mult, [rk("kg"), R("rope")], [rk("t1")])
                st.append(s2)
                if pos0 is not None:
                    def s3():
                        b3, rb3 = bank()
                        MM(b3[0:96, 0:n], rmat[:], kg[:, 0:n], True, True, [rk("kg"), RC], [rb3])
                        TT("dve", t2[:, 0:n], b3[0:96, 0:n], ropes[:, pos0:pos0 + n], ALU.mult, [rb3, R("rope")], [rk("t2")])
                        TT("dve", dest, t1[:, 0:n], t2[:, 0:n], ALU.add, [rk("t1"), rk("t2")], [rdest])
                    st.append(s3)
                return st

            def k_steps(s, h, bi):
                KT, VT = KTs[bi], VTs[bi]
                rKT, rVT = R("KT", bi), R("VT", bi)
                kc0 = s * Lk
                st = []
                for k0 in range(0, Lk, 512):
                    kn = min(512, Lk - k0)

                    def pre(kf, rkf, k0=k0, kn=kn):
                        bk, rb = bank()
                        for c in range(2):
                            MM(bk[0:64, 0:kn], wkn[:, c, h * 64:(h + 1) * 64], CKV[:, c, kc0 + k0:kc0 + k0 + kn],
                               c == 0, c == 1, [rAW, R("CKV")], [rb])
                        CP("dve", kf[0:64, 0:kn], bk[0:64, 0:kn], [rb], [rkf])
                        CP("pool", kf[64:96, 0:kn], KR96[64:96, kc0 + k0:kc0 + k0 + kn], [R("KR96")], [rkf])
                    pos0 = (k0 - nkctx) if (sample and k0 >= nkctx) else None
                    st += norm_steps("K", kn, "qkk", pos0, KT[:, k0:k0 + kn], rKT, pre)
                for kb0 in range(0, nkb, 8):
                    nb = min(8, nkb - kb0)

                    def vstep(kb0=kb0, nb=nb):
                        bk, rb = bank()
                        for i in range(nb):
                            kb = kb0 + i
                            for c in range(2):
                                MM(bk[:, i * 64:(i + 1) * 64], CKV[:, c, kc0 + kb * 128:kc0 + (kb + 1) * 128],
                                   wkv[:, c, h * 64:(h + 1) * 64], c == 0, c == 1, [rAW, R("CKV")], [rb])
                        CP("dve", VT[:, kb0:kb0 + nb, 0:64], bk[:, 0:nb * 64].rearrange("p (a b) -> p a b", b=64),
                           [rb], [rVT])
                    st.append(vstep)
                return st

            def q_steps(s, h, q0, qi_):
                qn = min(512, L - q0)
                tq = s * L + q0
                QT = QTs[qi_ % 2]
                rQT = R("QT", qi_ % 2)

                def pre(kf, rkf):
                    bk, rb = bank()
                    for c in range(3):
                        MM(bk[0:96, 0:qn], wq[:, c, h * 96:(h + 1) * 96], QN[:, c, tq:tq + qn], c == 0, c == 2,
                           [rAW, R("QN")], [rb])
                    CP("dve", kf[:, 0:qn], bk[0:96, 0:qn], [rb], [rkf])
                return norm_steps("Q", qn, "qkq", q0 if sample else None, QT[:, 0:qn], rQT, pre)

            def merge(a, b):
                out = []
                ia = ib = 0
                while ia < len(a) or ib < len(b):
                    if ia < len(a):
                        out.append(a[ia]); ia += 1
                    if ib < len(b):
                        out.append(b[ib]); ib += 1
                return out

            def attend(s, h, q0, qi_, bi, pending):
                qn = min(512, L - q0)
                tq = s * L + q0
                QT = QTs[qi_ % 2]
                rQT = R("QT", qi_ % 2)
                KT, VT = KTs[bi], VTs[bi]
                rKT, rVT = R("KT", bi), R("VT", bi)
                bo, rbo = obank()
                pendq = []
                LA = 3
                per = -(-len(pending) // nkb) if pending else 0
                stride = max(1, nkb // max(1, len(pending)))
                for kb in range(nkb):
                    bs_, rbs = bank()
                    MM(bs_[:, 0:qn], KT[:, kb * 128:(kb + 1) * 128], QT[:, 0:qn], True, True, [rKT, rQT], [rbs])
                    PT = PTs[pti[0] % 4]
                    rPT = R("PT", pti[0] % 4)
                    pti[0] += 1
                    ACT(PT[:, 0:qn], bs_[:, 0:qn], AF.Exp, [rbs], [rPT], scale=1.0 / math.sqrt(96.0))
                    pendq.append((kb, PT, rPT))
                    if len(pendq) > LA:
                        pkb, pPT, prPT = pendq.pop(0)
                        MM(bo[0:65, 0:qn], VT[:, pkb, :], pPT[:, 0:qn], pkb == 0, pkb == nkb - 1, [rVT, prPT], [rbo])
                    if kb == min(2, nkb - 1) and deferred[0] is not None:
                        deferred[0]()
                        deferred[0] = None
                    if kb % stride == 0:
                        for _ in range(per):
                            if pending:
                                pending.pop(0)()
                while pendq:
                    pkb, pPT, prPT = pendq.pop(0)
                    MM(bo[0:65, 0:qn], VT[:, pkb, :], pPT[:, 0:qn], pkb == 0, pkb == nkb - 1, [rVT, prPT], [rbo])
                while pending:
                    pending.pop(0)()
                CP("dve", osb[:, 0:qn], bo[0:65, 0:qn], [rbo], [R("osb")])

                def fin():
                    bd, rbd = bank()
                    MM(bd[0:64, 0:qn], sel[:], osb[:, 0:qn], True, True, [R("osb"), RC], [rbd])
                    ACT(rd_[:, 0:qn], bd[0:64, 0:qn], AF.Ln, [rbd], [R("rd")])
                    ACT(rd_[:, 0:qn], rd_[:, 0:qn], AF.Exp, [R("rd")], [R("rd")], scale=-1.0)
                    if h % 2 == 0:
                        TT("dve", OT2[0:64, h // 2, tq:tq + qn], osb[0:64, 0:qn], rd_[:, 0:qn], ALU.mult,
                           [R("osb"), R("rd")], [R("OT2")])
                    else:
                        TT("dve", obf[:, 0:qn], osb[0:64, 0:qn], rd_[:, 0:qn], ALU.mult, [R("osb"), R("rd")], [R("obf")])
                        DMA("sp", OT2[64:128, h // 2, tq:tq + qn], obf[:, 0:qn], [R("obf")], [R("OT2")])
                deferred[0] = fin

            if sample:
                heads = [(s_, h_) for s_ in range(nseq) for h_ in range(8)]
                qtl = list(range(0, L, 512))
                for f_ in merge(k_steps(heads[0][0], heads[0][1], 0), q_steps(heads[0][0], heads[0][1], qtl[0], 0)):
                    f_()
                qcnt = 0
                for hi, (s_, h_) in enumerate(heads):
                    knext = k_steps(heads[hi + 1][0], heads[hi + 1][1], (hi + 1) % 2) if hi + 1 < len(heads) else []
                    ksh = -(-len(knext) // len(qtl)) if knext else 0
                    for qi, q0 in enumerate(qtl):
                        if qi + 1 < len(qtl):
                            nxt = q_steps(s_, h_, qtl[qi + 1], qcnt + 1)
                        elif hi + 1 < len(heads):
                            nxt = q_steps(heads[hi + 1][0], heads[hi + 1][1], qtl[0], qcnt + 1)
                        else:
                            nxt = []
                        kpart, knext = knext[:ksh], knext[ksh:]
                        attend(s_, h_, q0, qcnt, hi % 2, merge(kpart, nxt))
                        qcnt += 1
                if deferred[0] is not None:
                    deferred[0]()
                    deferred[0] = None
            else:
                KTa = [alloc(8 * 256 * 2, BF16, [8, 256], parts=96) for _ in range(2)]
                QTa = [alloc(8 * 256 * 2, BF16, [8, 256], parts=96) for _ in range(2)]
                VTa = [alloc(2 * 8 * 65 * 2, BF16, [2, 8, 65]) for _ in range(2)]
                for i_ in range(2):
                    MEMSET("pool", VTa[i_][:, :, :, 64:65], 1.0, [R("VTa", i_)])
                bset = {}
                for nm in ("K", "Q"):
                    bset[nm] = dict(kf=alloc(8 * 256 * 4, F32, [8, 256], parts=96),
                                    sq=alloc(8 * 256 * 2, BF16, [8, 256], parts=96),
                                    rs=alloc(8 * 256 * 4, F32, [8, 256], parts=96))
                osb2 = alloc(512 * 4, F32, [512], parts=65)
                rd2 = alloc(512 * 4, F32, [512], parts=64)
                obf2 = alloc(256 * 2, BF16, [256], parts=64)

                def prep_steps(s, bi):
                    c0 = s * L
                    st = []
                    for nm in ("K", "Q"):
                        t = bset[nm]
                        kf, sq, rs = t["kf"], t["sq"], t["rs"]
                        rk = lambda x, nm=nm: R(x + "a", nm)
                        dest = (KTa if nm == "K" else QTa)[bi]
                        rdest = R("KTa" if nm == "K" else "QTa", bi)

                        def s1(nm=nm, kf=kf, sq=sq, rk=rk):
                            for hp in range(4):
                                bk, rb = bank()
                                for i in range(2):
                                    h = 2 * hp + i
                                    if nm == "K":
                                        for c in range(2):
                                            MM(bk[0:64, i * 256:(i + 1) * 256], wkn[:, c, h * 64:(h + 1) * 64],
                                               CKV[:, c, c0:c0 + L], c == 0, c == 1, [rAW, R("CKV")], [rb])
                                    else:
                                        for c in range(3):
                                            MM(bk[0:96, i * 256:(i + 1) * 256], wq[:, c, h * 96:(h + 1) * 96],
                                               QN[:, c, c0:c0 + L], c == 0, c == 2, [rAW, R("QN")], [rb])
                                rows = 64 if nm == "K" else 96
                                CP("dve" if hp % 2 == 0 else "act", kf[0:rows, 2 * hp:2 * hp + 2, :],
                                   bk[0:rows, 0:512].rearrange("p (a b) -> p a b", b=256), [rb], [rk("kf")])
                            if nm == "K":
                                CP("pool", kf[64:96, :, :], KR96[64:96, c0:c0 + L].unsqueeze(1).broadcast_to([32, 8, 256]),
                                   [R("KR96")], [rk("kf")])
                            TT("dve", sq[:, :, :], kf[:, :, :], kf[:, :, :], ALU.mult, [rk("kf")], [rk("sq")])
                        st.append(s1)

                        def s2(nm=nm, kf=kf, sq=sq, rs=rs, rk=rk, dest=dest, rdest=rdest):
                            for hp in range(4):
                                b2, rb2 = bank()
                                MM(b2[0:96, 0:512], onesb[0:96, 0:96], sq[:, 2 * hp:2 * hp + 2, :], True, True, [rk("sq"), RC], [rb2])
                                ACT(rs[:, 2 * hp:2 * hp + 2, :], b2[0:96, 0:512].rearrange("p (a b) -> p a b", b=256), AF.Ln,
                                    [rb2, RC], [rk("rs")], bias=epsc[0:96, 0:1], scale=1.0 / 96)
                            ACT(rs[:, :, :], rs[:, :, :], AF.Exp, [rk("rs")], [rk("rs")], scale=-0.5)
                            STT("dve", dest[:, :, :], kf[:, :, :], V(l, "qkk" if nm == "K" else "qkq", 0, 1, 96), rs[:, :, :],
                                ALU.mult, ALU.mult, [rk("kf"), rk("rs"), RC], [rdest])
                        st.append(s2)

                    def sv():
                        for kb in range(2):
                            bk, rb = bank()
                            for c in range(2):
                                MM(bk[:, 0:512], CKV[:, c, c0 + kb * 128:c0 + (kb + 1) * 128], wkv[:, c, :], c == 0, c == 1,
                                   [rAW, R("CKV")], [rb])
                            CP("act", VTa[bi][:, kb, :, 0:64], bk[:, 0:512].rearrange("p (a b) -> p a b", b=64), [rb], [R("VTa", bi)])
                    st.insert(2, sv)
                    return st

                def attend_seq(s, bi, pending):
                    c0 = s * L
                    KT_, QT_, VT_ = KTa[bi], QTa[bi], VTa[bi]
                    rKT_, rQT_, rVT_ = R("KTa", bi), R("QTa", bi), R("VTa", bi)
                    pend = None
                    bo = rbo = None
                    for h in range(9):
                        if h < 8:
                            bs_, rbs = bank()
                            for kb in range(2):
                                MM(bs_[:, kb * 256:(kb + 1) * 256], KT_[:, h, kb * 128:(kb + 1) * 128], QT_[:, h, :], True, True,
                                   [rKT_, rQT_], [rbs])
                            PT = PTs[pti[0] % 4]
                            rPT = R("PT", pti[0] % 4)
                            pti[0] += 1
                            ACT(PT[:, 0:512], bs_[:, 0:512], AF.Exp, [rbs], [rPT], scale=1.0 / math.sqrt(96.0))
                        if pend is not None:
                            ph, pPT, prPT = pend
                            if ph % 2 == 0:
                                bo, rbo = obank()
                            for kb in range(2):
                                MM(bo[0:65, (ph % 2) * 256:(ph % 2) * 256 + 256], VT_[:, kb, ph, :], pPT[:, kb * 256:(kb + 1) * 256],
                                   kb == 0, kb == 1, [rVT_, prPT], [rbo])
                            if ph % 2 == 1:
                                if deferred[0] is not None:
                                    deferred[0]()
                                    deferred[0] = None
                                CP("dve", osb2[:, :], bo[0:65, 0:512], [rbo], [R("osb2")])

                                def fin(hp=ph // 2):
                                    bd, rbd = bank()
                                    MM(bd[0:64, 0:512], sel[:], osb2[:, :], True, True, [R("osb2"), RC], [rbd])
                                    ACT(rd2[:, :], bd[0:64, 0:512], AF.Ln, [rbd], [R("rd2")])
                                    ACT(rd2[:, :], rd2[:, :], AF.Exp, [R("rd2")], [R("rd2")], scale=-1.0)
                                    TT("dve", OT2[0:64, hp, c0:c0 + L], osb2[0:64, 0:256], rd2[:, 0:256], ALU.mult,
                                       [R("osb2"), R("rd2")], [R("OT2")])
                                    TT("dve", obf2[:, :], osb2[0:64, 256:512], rd2[:, 256:512], ALU.mult,
                                       [R("osb2"), R("rd2")], [R("obf2")])
                                    DMA("sp", OT2[64:128, hp, c0:c0 + L], obf2[:, :], [R("obf2")], [R("OT2")])
                                deferred[0] = fin
                        pend = (h, PT, rPT) if h < 8 else None
                        if pending and h % 2 == 1:
                            pending.pop(0)()
                    while pending:
                        pending.pop(0)()

                for f_ in prep_steps(0, 0):
                    f_()
                for s_ in range(nseq):
                    nxt = prep_steps(s_ + 1, (s_ + 1) % 2) if s_ + 1 < nseq else []
                    attend_seq(s_, s_ % 2, nxt)
                if deferred[0] is not None:
                    deferred[0]()
                    deferred[0] = None
            P.barrier()
            release(m_long)
            ntb = L // 128
            MW_SLOT = ARENA_BYTES - 9216
            MW0 = alloc_at(MW_SLOT, 4608 * 2, BF16, [4608])
            DMA("pool", MW0, wmgd[l, 0], [], [R("MW", 0)])
            if sample:
                H = L // 2
                Zp = alloc(4 * (H + 1) * 2, BF16, [4, H + 1])
                Zm = alloc(4 * (H + 1) * 2, BF16, [4, H + 1])
                AB = alloc(9 * 4 * 256 * 2, BF16, [9, 4, 256])
                DBs = [alloc(9 * 2 * 512 * 2, BF16, [9, 2, 512]) for _ in range(2)]
                assert mark() <= MW_SLOT
                rZ = R("ZY", ps)
                TT("dve", Zp[:, :, 1:H], ZY[:, :, 1:H], ZY[:, :, L - 1:H:-1], ALU.add, [rZ], [R("Zp")])
                TT("dve", Zm[:, :, 1:H], ZY[:, :, 1:H], ZY[:, :, L - 1:H:-1], ALU.subtract, [rZ], [R("Zm")])
                CP("act", Zp[:, :, 0:1], ZY[:, :, 0:1], [rZ], [R("Zp")])
                CP("act", Zp[:, :, H:H + 1], ZY[:, :, H:H + 1], [rZ], [R("Zp")])
                MEMSET("pool", Zm[:, :, 0:1], 0.0, [R("Zm")])
                for tb in range(8):
                    for gp in range(2):
                        bk, rb = bank()
                        for i in range(2):
                            g = gp * 2 + i
                            MM(bk[:, i * 256:i * 256 + 128], Zp[:, g, tb * 128:(tb + 1) * 128], cscb[:, 0:128], True, True,
                               [R("Zp"), RC], [rb])
                            MM(bk[:, i * 256 + 128:(i + 1) * 256], Zm[:, g, tb * 128:(tb + 1) * 128], cscb[:, 128:256], True, True,
                               [R("Zm"), RC], [rb])
                        CP("act" if (tb + gp) % 2 else "dve", AB[:, tb, gp * 2:gp * 2 + 2, :],
                           bk[:, 0:512].rearrange("p (a b) -> p a b", b=256), [rb], [R("AB")])
                bk, rb = bank()
                for g in range(4):
                    MM(bk[0:1, g * 128:(g + 1) * 128], Zp[:, g, H:H + 1], cscb[:, 0:128], True, True, [R("Zp"), RC], [rb])
                CP("dve", AB[0:1, 8, :, 0:128], bk[0:1, 0:512].rearrange("p (a b) -> p a b", b=128), [rb], [R("AB")])
                for lb in range(L // 512):
                    DB = DBs[lb % 2]
                    rDB = R("DB", lb % 2)
                    DMA("sp", DB, dftsd[lb], [], [rDB])
                    for g in range(4):
                        bk, rb = bank()
                        for tb in range(8):
                            MM(bk[:, 0:512], AB[:, tb, g, 0:128], DB[:, tb, 0, :], tb == 0, False, [R("AB"), rDB], [rb])
                            MM(bk[:, 0:512], AB[:, tb, g, 128:256], DB[:, tb, 1, :], False, False, [R("AB"), rDB], [rb])
                        MM(bk[:, 0:512], AB[0:1, 8, g, 0:128], DB[0:1, 8, 0, :], False, True, [R("AB"), rDB], [rb])
                        CP("act" if g % 2 else "dve", ZY[:, g, lb * 512:(lb + 1) * 512], bk[:, 0:512], [rb], [rZ])
            else:
                AB = alloc(ntb * 4 * 256 * 2, BF16, [ntb, 4, 256])
                if sample:
                    DBs = [alloc(16 * 2 * 512 * 2, BF16, [16, 2, 512]) for _ in range(2)]
                assert mark() <= MW_SLOT
                for s in range(nseq):
                    tb0 = s * L
                    for tb in range(ntb):
                        for gp in range(2):
                            bk, rb = bank()
                            for i in range(2):
                                g = gp * 2 + i
                                MM(bk[:, i * 256:(i + 1) * 256], ZY[:, g, tb0 + tb * 128:tb0 + (tb + 1) * 128], cscb[:], True, True,
                                   [R("ZY", ps), RC], [rb])
                            CP("act" if (tb + gp) % 2 else "dve", AB[:, tb, gp * 2:gp * 2 + 2, :],
                               bk[:, 0:512].rearrange("p (a b) -> p a b", b=256), [rb], [R("AB")])
                    LB = 512 if sample else 256
                    for lb in range(L // LB):
                        if sample:
                            DB = DBs[lb % 2]
                            rDB = R("DB", lb % 2)
                            DMA("sp", DB, dftsd[lb], [], [rDB])
                            dsl = lambda tb, cs: DB[:, tb, cs, :]
                        else:
                            rDB = RC
                            dsl = lambda tb, cs: dftp[:, tb, cs, :]
                        bk, rb = bank()
                        ng = 512 // LB
                        for g in range(4):
                            if g % ng == 0 and g > 0:
                                bk, rb = bank()
                            o0 = (g % ng) * LB
                            for tb in range(ntb):
                                MM(bk[:, o0:o0 + LB], AB[:, tb, g, 0:128], dsl(tb, 0), tb == 0, False, [R("AB"), rDB], [rb])
                                MM(bk[:, o0:o0 + LB], AB[:, tb, g, 128:256], dsl(tb, 1), False, tb == ntb - 1, [R("AB"), rDB], [rb])
                            if g % ng == ng - 1:
                                g0 = g - ng + 1
                                CP("act" if lb % 2 else "dve", ZY[:, g0:g0 + ng, tb0 + lb * LB:tb0 + (lb + 1) * LB],
                                   bk[:, 0:512].rearrange("p (a b) -> p a b", b=LB), [rb], [R("ZY", ps)])
            P.barrier()
            release(m_long)
            MIX = alloc(8 * T * 2, BF16, [8, T])
            WO = alloc(8 * 1024 * 2, BF16, [8, 1024])
            DMA("pool", WO, wod[l], [], [R("WO")])
            m_mix = mark()
            assert mark() + 9216 + 6144 + 4096 <= MW_SLOT
            MWs = [MW0, alloc(4608 * 2, BF16, [4608])]
            G = [alloc(512 * 4, F32, [512]) for _ in range(3)]
            m1 = alloc(512 * 4, F32, [512])
            m2 = alloc(512 * 4, F32, [512])
            for j in range(8):
                MW = MWs[j % 2]
                rMW = R("MW", j % 2)
                if j + 1 < 8:
                    DMA("pool", MWs[(j + 1) % 2], wmgd[l, j + 1], [], [R("MW", (j + 1) % 2)])
                for ti, (tg, n, pcs) in enumerate(tiles):
                    for br in range(3):
                        bk, rb = bank()
                        for kc in range(8):
                            o = (kc * 3 + br) * 128
                            MM(bk[:, 0:n], MW[:, o:o + 128], HT[:, kc, tg:tg + n], kc == 0, kc == 7, [rMW, rHT(ti)], [rb])
                        ACT(G[br][:, 0:n], bk[:, 0:n], AF.Sigmoid, [rb, RC], [R("G", br)], bias=V(l, "bg", br * 8 + j), scale=1.0)
                    ybk = []
                    for bi, src, rsrc in ((0, ZY, R("ZY", ps)), (1, YC, R("YC")), (2, OT2, R("OT2"))):
                        bk, rb = bank()
                        for c in range(4):
                            o = 3072 + bi * 512 + c * 128
                            MM(bk[:, 0:n], MW[:, o:o + 128], src[:, c, tg:tg + n], c == 0, c == 3, [rMW, rsrc], [rb])
                        ybk.append((bk, rb))
                    TT("dve", m1[:, 0:n], ybk[0][0][:, 0:n], G[0][:, 0:n], ALU.mult, [ybk[0][1], R("G", 0)], [R("m1")])
                    TT("dve", m2[:, 0:n], ybk[1][0][:, 0:n], G[1][:, 0:n], ALU.mult, [ybk[1][1], R("G", 1)], [R("m2")])
                    TT("pool", m1[:, 0:n], m1[:, 0:n], m2[:, 0:n], ALU.add, [R("m1"), R("m2")], [R("m1")])
                    TT("dve", m2[:, 0:n], ybk[2][0][:, 0:n], G[2][:, 0:n], ALU.mult, [ybk[2][1], R("G", 2), R("m1")], [R("m2")])
                    TT("pool", MIX[:, j, tg:tg + n], m1[:, 0:n], m2[:, 0:n], ALU.add, [R("m1"), R("m2")], [R("MIX")])
            P.barrier()
            release(m_mix)
            xts = [alloc(8 * TN * 4, F32, [8, TN]) for _ in range(2)]
            sq = alloc(8 * TN * 2, BF16, [8, TN])
            rs = alloc(TN * 4, F32, [TN])
            tmp = alloc(8 * TN * 4, F32, [8, TN])
            for s_ in range(nseq):
                MEMSET("pool", HT[:, :, s_ * (L + 2):s_ * (L + 2) + 1], 0.0, [R("HTall")])
                MEMSET("pool", HT[:, :, s_ * (L + 2) + L + 1:s_ * (L + 2) + L + 2], 0.0, [R("HTall")])
            for ti, (tg, n, pcs) in enumerate(tiles):
                xt = xts[ti % 2]
                rxt = R("xt", ti % 2)
                DMA("sp", xt[:, :, 0:n], xsrc[:, :, tg:tg + n], [R("Y", ps, ti)], [rxt])
                for j in range(8):
                    bk, rb = bank()
                    for kc in range(8):
                        MM(bk[:, 0:n], WO[:, kc, j * 128:(j + 1) * 128], MIX[:, kc, tg:tg + n], kc == 0, kc == 7,
                           [R("WO"), R("MIX")], [rb])
                    STT("dve", xt[:, j, 0:n], bk[:, 0:n], MOD[:, l, ps, 16 + j:17 + j], xt[:, j, 0:n], ALU.mult, ALU.add,
                        [rb, rxt, RMOD], [rxt])
                DMA("sp", yout[:, :, tg:tg + n], xt[:, :, 0:n], [rxt], [R("Y", ps, ti)])
                norm_to_HT(l, xt, rxt, ti, n, [(off, ln_, s_ * (L + 2) + 1 + p0) for (s_, p0, ln_, off) in pcs], 24, (sq, rs, tmp))
            P.barrier()
            ast["off"] = (8 * HTW * 2 + 31) // 32 * 32
            ACTT = alloc(22 * T * 2, BF16, [22, T])
            m_f = mark()
            FUs = [alloc(8 * 256 * 2, BF16, [8, 256]) for _ in range(3)]
            upas = [alloc(HTW * 4, F32, [HTW]) for _ in range(2)]
            upbs = [alloc(HTW * 4, F32, [HTW]) for _ in range(2)]
            Wd = HTW - 2
            ta = alloc(Wd * 4, F32, [Wd])
            tb_ = alloc(Wd * 4, F32, [Wd])
            sa = alloc(Wd * 4, F32, [Wd])
            coltiles = [(c0, min(512, HTW - c0)) for c0 in range(0, HTW, 512)]

            def make_chain(c):
                upa, upb = upas[c % 2], upbs[c % 2]
                rua, rub = R("upa", c % 2), R("upb", c % 2)
                cb = 22 + c
                st = []
                st.append(lambda: ACT(ta[:], upa[:, 1:1 + Wd], AF.Identity, [rua, RC], [R("ta")], bias=V(l, "fb", c), scale=V(l, "fw", 44 + c)))
                st.append(lambda: ACT(tb_[:], upb[:, 1:1 + Wd], AF.Identity, [rub, RC], [R("tb")], bias=V(l, "fb", cb), scale=V(l, "fw", 44 + cb)))
                st.append(lambda: STT("dve", ta[:], upa[:, 0:Wd], V(l, "fw", c), ta[:], ALU.mult, ALU.add, [rua, R("ta"), RC], [R("ta")]))
                st.append(lambda: STT("dve", ta[:], upa[:, 2:2 + Wd], V(l, "fw", 88 + c), ta[:], ALU.mult, ALU.add, [rua, R("ta"), RC], [R("ta")]))
                st.append(lambda: ACT(sa[:], ta[:], AF.Silu, [R("ta")], [R("sa")]))
                st.append(lambda: STT("dve", tb_[:], upb[:, 0:Wd], V(l, "fw", cb), tb_[:], ALU.mult, ALU.add, [rub, R("tb"), RC], [R("tb")]))
                st.append(lambda: STT("dve", tb_[:], upb[:, 2:2 + Wd], V(l, "fw", 88 + cb), tb_[:], ALU.mult, ALU.add, [rub, R("tb"), RC], [R("tb")]))

                def mults():
                    for s_ in range(nseq):
                        j0 = s_ * (L + 2)
                        TT("pool", ACTT[:, c, s_ * L:(s_ + 1) * L], sa[:, j0:j0 + L], tb_[:, j0:j0 + L], ALU.mult,
                           [R("sa"), R("tb")], [R("ACTT")])
                st.append(mults)
                return st

            steps = []
            for c in range(22):
                FU = FUs[c % 3]
                rFU = R("FU", c % 3)
                upa, upb = upas[c % 2], upbs[c % 2]
                rua, rub = R("upa", c % 2), R("upb", c % 2)
                if c == 0:
                    DMA("pool", FU, fud[l, 0], [], [rFU])
                    DMA("pool", FUs[1], fud[l, 1], [], [R("FU", 1)])
                if c + 2 < 22:
                    DMA("pool", FUs[(c + 2) % 3], fud[l, c + 2], [], [R("FU", (c + 2) % 3)])
                per = -(-len(steps) // len(coltiles)) if steps else 0
                for (c0, cn) in coltiles:
                    ba, rba = bank()
                    bb, rbb = bank()
                    for kc in range(8):
                        MM(ba[:, 0:cn], FU[:, kc, 0:128], HT[:, kc, c0:c0 + cn], kc == 0, kc == 7, [rFU, R("HTall")], [rba])
                    for kc in range(8):
                        MM(bb[:, 0:cn], FU[:, kc, 128:256], HT[:, kc, c0:c0 + cn], kc == 0, kc == 7, [rFU, R("HTall")], [rbb])
                    CP("act", upa[:, c0:c0 + cn], ba[:, 0:cn], [rba], [rua])
                    CP("dve", upb[:, c0:c0 + cn], bb[:, 0:cn], [rbb], [rub])
                    for _ in range(per):
                        if steps:
                            steps.pop(0)()
                while steps:
                    steps.pop(0)()
                steps = make_chain(c)
            while steps:
                steps.pop(0)()
            P.barrier()
            release(m_f)
            FDs = [alloc(22 * 128 * 2, BF16, [22, 128]) for _ in range(3)]
            xts = [alloc(8 * TN * 4, F32, [8, TN]) for _ in range(2)]
            sq = alloc(8 * TN * 2, BF16, [8, TN])
            rs = alloc(TN * 4, F32, [TN])
            tmp = alloc(2 * TN * 4, F32, [2, TN])
            fcnt = 0
            til = list(enumerate(tiles))
            nfd = 8 * ((len(til) + 1) // 2)
            for p0_ in range(0, len(til), 2):
                pr = til[p0_:p0_ + 2]
                for k_, (ti, (tg, n, pcs)) in enumerate(pr):
                    DMA("sp", xts[k_][:, :, 0:n], yout[:, :, tg:tg + n], [R("Y", ps, ti)], [R("xt", k_)])
                for j in range(8):
                    FD = FDs[fcnt % 3]
                    rFD = R("FD", fcnt % 3)
                    if fcnt == 0:
                        DMA("pool", FD, fdd[l, 0], [], [rFD])
                        DMA("pool", FDs[1], fdd[l, 1], [], [R("FD", 1)])
                    if fcnt + 2 < nfd:
                        DMA("pool", FDs[(fcnt + 2) % 3], fdd[l, (fcnt + 2) % 8], [], [R("FD", (fcnt + 2) % 3)])
                    fcnt += 1
                    for k_, (ti, (tg, n, pcs)) in enumerate(pr):
                        bk, rb = bank()
                        for c in range(22):
                            MM(bk[:, 0:n], FD[:, c, :], ACTT[:, c, tg:tg + n], c == 0, c == 21, [rFD, R("ACTT")], [rb])
                        STT("dve", xts[k_][:, j, 0:n], bk[:, 0:n], MOD[:, l, ps, 40 + j:41 + j], xts[k_][:, j, 0:n],
                            ALU.mult, ALU.add, [rb, R("xt", k_), RMOD], [R("xt", k_)])
                for k_, (ti, (tg, n, pcs)) in enumerate(pr):
                    DMA("sp", yout[:, :, tg:tg + n], xts[k_][:, :, 0:n], [R("xt", k_)], [R("Y", ps, ti)])
                    if l + 1 < depth:
                        norm_to_HT(l + 1, xts[k_], R("xt", k_), ti, n, [(0, n, tg)], 0, (sq, rs, tmp))
            P.barrier()

    if 0 in passes:
        run_pass(0, 1024, 4, 256, xp, yp, False)
    if 1 in passes:
        run_pass(1, 2048, 1, 2048, xs, ys, True)
    P.counts = {e: len(P.ops[e]) for e in ENGS}
    P.nwaits = {e: sum(len(o.waits) for o in P.ops[e]) for e in ENGS}
    build_program.stats = (P.counts, P.nwaits)
    build_program.P = P
    P.emit()
    P.close()
    return nc, ast["max"]


def _km(W, kc):
    K, N = W.shape
    return np.ascontiguousarray(W.reshape(kc, 128, N).transpose(1, 0, 2))


def _cols(v, n):
    return np.ascontiguousarray(v.reshape(n, 128).T)


def _host_consts():
    c = {}
    c["onesf"] = np.ones((128, 128), np.float32)
    c["ident"] = np.eye(128, dtype=np.float32)
    k = np.arange(128)
    ang = 2 * np.pi * np.outer(k, k) / 128.0
    c["csc"] = np.concatenate([np.cos(ang), -np.sin(ang)], axis=1).astype(np.float32) / np.sqrt(128.0)

    def dft(L):
        m = np.arange(L, dtype=np.float64)
        a = 2 * np.pi * (np.outer(m, m) % L) / L
        return np.cos(a) / np.sqrt(L), np.sin(a) / np.sqrt(L)

    C, S = dft(256)
    dp = np.stack([C.reshape(2, 128, 256), S.reshape(2, 128, 256)], axis=2)
    c["dftp"] = np.ascontiguousarray(dp.transpose(1, 0, 2, 3)).astype(ml_dtypes.bfloat16)
    C, S = dft(2048)
    ds = np.zeros((4, 128, 9, 2, 512), np.float64)
    for tb in range(8):
        ds[:, :, tb, 0, :] = C[tb * 128:(tb + 1) * 128, :].reshape(128, 4, 512).transpose(1, 0, 2)
        ds[:, :, tb, 1, :] = S[tb * 128:(tb + 1) * 128, :].reshape(128, 4, 512).transpose(1, 0, 2)
    ds[:, 0, 8, 0, :] = C[1024, :].reshape(4, 512)
    c["dfts"] = ds.astype(ml_dtypes.bfloat16)
    Ls = 2048
    pos = np.arange(Ls)
    row = (pos // 64).astype(np.float32)
    col = (pos % 64).astype(np.float32)
    half = 16
    inv = (10000.0 ** (-np.arange(0, half, 2, dtype=np.float32) / half)).astype(np.float32)
    rc = np.ones((96, Ls), np.float32)
    rsn = np.zeros((96, Ls), np.float32)
    for axis, pv in enumerate((row, col)):
        a = (pv[None, :] * inv[:, None]).astype(np.float32)
        for hf in range(2):
            r0 = 64 + axis * 16 + hf * 8
            rc[r0:r0 + 8] = np.cos(a)
            rsn[r0:r0 + 8] = np.sin(a)
    c["ropec"] = rc
    c["ropes"] = rsn
    rm = np.zeros((96, 96), np.float32)
    for axis in range(2):
        for f in range(8):
            r1 = 64 + axis * 16 + f
            r2 = r1 + 8
            rm[r2, r1] = -1.0
            rm[r1, r2] = 1.0
    c["rmat"] = rm
    sl = np.zeros((65, 64), np.float32)
    sl[64, :] = 1.0
    c["sel"] = sl
    return c


_CACHE = {}


def kernel(x_prompt, x_sample, cache_ckv, cache_krope, c, c_ctx, ada_w, ada_b, norm1_g, norm2_g,
           w_in, w_gate, b_gate, w_fourier, conv_dw, conv_dw_b, conv_ln_g, conv_ln_b, w_conv_out,
           q_norm_g, w_q_up, kv_norm_g, w_kv_up, qk_q_g, qk_k_g, w_mla_out, w_out,
           ffn_up, ffn_dw, ffn_dw_b, ffn_down):
    f = lambda a: np.asarray(a, dtype=np.float32)
    x_prompt, x_sample, cache_ckv, cache_krope, c, c_ctx = map(f, (x_prompt, x_sample, cache_ckv, cache_krope, c, c_ctx))
    ada_w, ada_b, norm1_g, norm2_g, w_in, w_gate, b_gate = map(f, (ada_w, ada_b, norm1_g, norm2_g, w_in, w_gate, b_gate))
    w_fourier, conv_dw, conv_dw_b, conv_ln_g, conv_ln_b, w_conv_out = map(f, (w_fourier, conv_dw, conv_dw_b, conv_ln_g, conv_ln_b, w_conv_out))
    q_norm_g, w_q_up, kv_norm_g, w_kv_up, qk_q_g, qk_k_g, w_mla_out, w_out = map(f, (q_norm_g, w_q_up, kv_norm_g, w_kv_up, qk_q_g, qk_k_g, w_mla_out, w_out))
    ffn_up, ffn_dw, ffn_dw_b, ffn_down = map(f, (ffn_up, ffn_dw, ffn_dw_b, ffn_down))
    Ld = DEPTH
    if "nc" not in _CACHE:
        _CACHE["nc"] = build_program()[0]
        _CACHE["consts"] = _host_consts()
    nc = _CACHE["nc"]
    consts = _CACHE["consts"]

    sh = dict(consts)
    sh["adaw"] = np.stack([_km(ada_w[l], 8) for l in range(Ld)])
    sh["adab"] = np.stack([_cols(ada_b[l], 48) for l in range(Ld)])
    vecs = np.zeros((Ld, 128, NV), np.float32)
    for l in range(Ld):
        def put(name, arr):
            vecs[l, :arr.shape[0], VOFF[name]:VOFF[name] + arr.shape[1]] = arr
        put("n1g", _cols(norm1_g[l], 8))
        put("n2g", _cols(norm2_g[l], 8))
        put("bg", _cols(b_gate[l], 24))
        put("cw", _cols(conv_dw[l].reshape(-1), 124))
        put("cb", _cols(conv_dw_b[l], 4))
        put("lng", _cols(conv_ln_g[l], 4))
        put("lnb", _cols(conv_ln_b[l], 4))
        put("qng", _cols(q_norm_g[l], 3))
        put("kvg", _cols(kv_norm_g[l], 2))
        put("fw", _cols(ffn_dw[l].reshape(-1), 132))
        put("fb", _cols(ffn_dw_b[l], 44))
        put("qkq", qk_q_g[l].reshape(96, 1))
        put("qkk", qk_k_g[l].reshape(96, 1))
    sh["vec"] = vecs
    win = np.zeros((Ld, 128, 8, WIN_COLS), np.float32)
    for l in range(Ld):
        wk = _km(w_in[l], 8)
        win[l, :, :, 0:2176] = wk[:, :, 0:2176]
        win[l, :, :, 2176 + 64:2176 + 96] = wk[:, :, 2176:2208]
    sh["win"] = win
    wmg = np.zeros((Ld, 8, 128, 4608), np.float32)
    for l in range(Ld):
        g = _km(w_gate[l], 8).reshape(128, 8, 3, 8, 128)
        wf = _km(w_fourier[l], 4).reshape(128, 4, 8, 128)
        wc = _km(w_conv_out[l], 4).reshape(128, 4, 8, 128)
        wm = _km(w_mla_out[l], 4).reshape(128, 4, 8, 128)
        for j in range(8):
            wmg[l, j, :, 0:3072] = g[:, :, :, j, :].reshape(128, 3072)
            wmg[l, j, :, 3072:3584] = wf[:, :, j, :].reshape(128, 512)
            wmg[l, j, :, 3584:4096] = wc[:, :, j, :].reshape(128, 512)
            wmg[l, j, :, 4096:4608] = wm[:, :, j, :].reshape(128, 512)
    sh["wmg"] = wmg
    sh["wq"] = np.stack([_km(w_q_up[l], 3) for l in range(Ld)])
    wkv4 = np.stack([_km(w_kv_up[l], 2) for l in range(Ld)]).reshape(Ld, 128, 2, 8, 128)
    sh["wkn"] = np.ascontiguousarray(wkv4[..., 0:64]).reshape(Ld, 128, 2, 512)
    sh["wkv"] = np.ascontiguousarray(wkv4[..., 64:128]).reshape(Ld, 128, 2, 512)
    sh["wo"] = np.stack([_km(w_out[l], 8) for l in range(Ld)])
    fu = np.zeros((Ld, 22, 128, 8, 256), np.float32)
    for l in range(Ld):
        u = _km(ffn_up[l], 8)
        for cc in range(22):
            fu[l, cc, :, :, 0:128] = u[:, :, cc * 128:(cc + 1) * 128]
            fu[l, cc, :, :, 128:256] = u[:, :, 2816 + cc * 128:2816 + (cc + 1) * 128]
    sh["fu"] = fu
    fd = np.zeros((Ld, 8, 128, 22, 128), np.float32)
    for l in range(Ld):
        d = _km(ffn_down[l], 22).reshape(128, 22, 8, 128)
        fd[l] = d.transpose(2, 0, 1, 3)
    sh["fd"] = fd

    in_maps = []
    for i in range(8):
        b = i // 2
        m = dict(sh)
        xpi = x_prompt[4 * i:4 * i + 4].reshape(1024, 8, 128)
        m["xp"] = np.ascontiguousarray(xpi.transpose(2, 1, 0))
        m["xs"] = np.ascontiguousarray(x_sample[b].reshape(2048, 8, 128).transpose(2, 1, 0))
        cvv = np.stack([c_ctx, c[b]], axis=-1)
        m["cv"] = np.ascontiguousarray(cvv.reshape(8, 128, 2).transpose(1, 0, 2))
        m["cck"] = np.ascontiguousarray(cache_ckv[b].reshape(Ld, 512, 2, 128).transpose(0, 3, 2, 1))
        m["ckr"] = np.ascontiguousarray(cache_krope[b].transpose(0, 2, 1))
        in_maps.append(m)

    if _CACHE.get('prep_only'):
        return in_maps
    res = run_bass_kernel_spmd(nc, in_maps, core_ids=list(range(8)))
    rs = res.results
    y_prompt = np.zeros((32, 256, 1024), np.float32)
    y_sample = np.zeros((4, 2048, 1024), np.float32)
    new_ckv = np.zeros((32, Ld, 256, 256), np.float32)
    new_kr = np.zeros((32, Ld, 256, 32), np.float32)
    for i in range(8):
        r = rs[i]
        y_prompt[4 * i:4 * i + 4] = np.asarray(r["yp"]).transpose(2, 1, 0).reshape(4, 256, 1024)
        if i % 2 == 0:
            y_sample[i // 2] = np.asarray(r["ys"]).transpose(2, 1, 0).reshape(2048, 1024)
        ck = np.asarray(r["ockv"])
        new_ckv[4 * i:4 * i + 4] = ck.transpose(3, 0, 2, 1).reshape(4, 256, Ld, 256).transpose(0, 2, 1, 3)
        kr = np.asarray(r["okr"])
        new_kr[4 * i:4 * i + 4] = kr.transpose(2, 0, 1).reshape(4, 256, Ld, 32).transpose(0, 2, 1, 3)
    return (y_prompt, y_sample, new_ckv, new_kr)
```

```python
import math
import numpy as np
import ml_dtypes
from contextlib import ExitStack
import concourse.bass as bass
import concourse.mybir as mybir
from concourse.bass_utils import run_bass_kernel_spmd

F32 = mybir.dt.float32
BF16 = mybir.dt.bfloat16
ALU = mybir.AluOpType
AF = mybir.ActivationFunctionType

ENGS = ("pe", "act", "dve", "pool", "sp")
EPOCH = 20000
NDMASEM = 6

DEPTH = 4
EPS = 1e-6


class Res:
    __slots__ = ("name", "writers", "readers")

    def __init__(self, name=""):
        self.name = name
        self.writers = []
        self.readers = []


class Op:
    __slots__ = ("eng", "idx", "fn", "waits", "needed", "is_dma", "ev", "dma_slot")

    def __init__(self, eng, idx, fn, is_dma):
        self.eng = eng
        self.idx = idx
        self.fn = fn
        self.waits = []
        self.needed = False
        self.is_dma = is_dma
        self.ev = None
        self.dma_slot = None


class Prog:
    def __init__(self, nc):
        self.nc = nc
        self.ops = {e: [] for e in ENGS}
        self.seen = {e: {} for e in ENGS}
        self.seen_dma = {e: set() for e in ENGS}
        self.dma_count = {e: 0 for e in ENGS}
        self.dma_last = {e: {} for e in ENGS}
        self.last_compute = {e: None for e in ENGS}
        self.stack = ExitStack()

    def _dep(self, op, prod, same_ok):
        if prod is None or prod is op:
            return
        F = op.eng
        if prod.is_dma:
            if id(prod) in self.seen_dma[F]:
                return
            self.seen_dma[F].add(id(prod))
            op.waits.append(prod)
            prod.needed = True
            return
        if prod.eng == F and same_ok:
            return
        if self.seen[F].get(prod.eng, -1) >= prod.idx:
            return
        self.seen[F][prod.eng] = prod.idx
        op.waits.append(prod)
        prod.needed = True

    def op(self, eng, fn, reads=(), writes=(), is_dma=False):
        lst = self.ops[eng]
        o = Op(eng, len(lst), fn, is_dma)
        if is_dma:
            k = self.dma_count[eng]
            self.dma_count[eng] = k + 1
            o.dma_slot = k % NDMASEM
            prev = self.dma_last[eng].get(o.dma_slot)
            self.dma_last[eng][o.dma_slot] = o
            if prev is not None:
                self._dep(o, prev, same_ok=False)
        cand = {}

        def add(p, same_ok):
            if p is None or p is o:
                return
            if p.is_dma:
                self._dep(o, p, same_ok)
                return
            if p.eng == eng and same_ok:
                return
            c = cand.get(p.eng)
            if c is None or c.idx < p.idx:
                cand[p.eng] = p

        for r in reads:
            for w in r.writers:
                add(w, eng == "pe")
        for w in writes:
            for ww in w.writers:
                add(ww, True)
            for rd in w.readers:
                add(rd, True)
        for p in cand.values():
            self._dep(o, p, same_ok=False)
        for r in reads:
            r.readers.append(o)
        for w in writes:
            if w.readers:
                w.writers = [o]
                w.readers = []
            else:
                w.writers.append(o)
                if len(w.writers) > 64:
                    w.writers = w.writers[-64:] if all(x.eng == o.eng and not x.is_dma for x in w.writers) else w.writers
        lst.append(o)
        if not is_dma:
            self.last_compute[eng] = o
        return o

    def barrier(self):
        lasts = [self.last_compute[e] for e in ENGS if self.last_compute[e] is not None]
        dmas = [o for e in ENGS for o in self.dma_last[e].values()]
        o = Op("sp", len(self.ops["sp"]), (lambda e: e.nop()), False)
        for p in lasts:
            self._dep(o, p, same_ok=True)
        for p in dmas:
            self._dep(o, p, same_ok=False)
        self.ops["sp"].append(o)
        self.last_compute["sp"] = o
        for F in ENGS:
            if F == "sp":
                continue
            b = Op(F, len(self.ops[F]), None, False)
            self._dep(b, o, same_ok=True)
            self.ops[F].append(b)
            for p in lasts:
                if self.seen[F].get(p.eng, -1) < p.idx:
                    self.seen[F][p.eng] = p.idx
            for p in dmas:
                self.seen_dma[F].add(id(p))

    def emit(self):
        nc = self.nc
        st = self.stack
        sems = {}
        for e in ENGS:
            cnt = 0
            ep = 0
            for o in self.ops[e]:
                if o.is_dma or o.fn is None:
                    continue
                if o.needed:
                    if cnt >= EPOCH:
                        ep += 1
                        cnt = 0
                    cnt += 1
                    key = (e, ep)
                    if key not in sems:
                        sems[key] = st.enter_context(nc.semaphore(f"s_{e}_{ep}"))
                    o.ev = (sems[key], cnt)
        dsem = {}
        for e in ENGS:
            per_slot = {}
            for o in self.ops[e]:
                if not o.is_dma:
                    continue
                key = (e, o.dma_slot)
                if key not in dsem:
                    dsem[key] = st.enter_context(nc.semaphore(f"d_{e}_{o.dma_slot}"))
                per_slot[o.dma_slot] = per_slot.get(o.dma_slot, 0) + 16
                o.ev = (dsem[key], per_slot[o.dma_slot])

        def run(ename):
            def body(eng):
                for o in self.ops[ename]:
                    for p in o.waits:
                        eng.wait_ge(p.ev[0], p.ev[1])
                    if o.fn is None:
                        continue
                    ins = o.fn(eng)
                    if o.is_dma:
                        ins.then_inc(o.ev[0], 16)
                    elif o.needed:
                        ins.then_inc(o.ev[0], 1)
                for slot, last in self.dma_last[ename].items():
                    eng.wait_ge(last.ev[0], last.ev[1])
            return body

        with nc.Block() as block:
            block.tensor(run("pe"))
            block.scalar(run("act"))
            block.vector(run("dve"))
            block.gpsimd(run("pool"))
            block.sync(run("sp"))

    def close(self):
        self.stack.close()


VOFF = {}
_o = 0
for _n, _w in (("n1g", 8), ("n2g", 8), ("bg", 24), ("cw", 124), ("cb", 4), ("lng", 4), ("lnb", 4),
               ("qng", 3), ("kvg", 2), ("fw", 132), ("fb", 44), ("qkq", 1), ("qkk", 1)):
    VOFF[_n] = _o
    _o += _w
NV = _o

WIN_COLS = 2176 + 96
ARENA_BYTES = 190 * 1024


def build_program(depth=DEPTH, passes=(0, 1)):
    nc = bass.Bass("TRN2", target_bir_lowering=False)
    P = Prog(nc)

    def din(name, shape, dt=F32):
        return nc.dram_tensor(name, list(shape), dt, kind="ExternalInput").ap()

    def dout(name, shape, dt=F32):
        return nc.dram_tensor(name, list(shape), dt, kind="ExternalOutput").ap()

    xp = din("xp", [128, 8, 1024])
    xs = din("xs", [128, 8, 2048])
    cvd = din("cv", [128, 8, 2])
    adaw = din("adaw", [DEPTH, 128, 8, 6144])
    adab = din("adab", [DEPTH, 128, 48])
    vecd = din("vec", [DEPTH, 128, NV])
    wind = din("win", [DEPTH, 128, 8, WIN_COLS])
    wmgd = din("wmg", [DEPTH, 8, 128, 4608])
    wqd = din("wq", [DEPTH, 128, 3, 768])
    wknd = din("wkn", [DEPTH, 128, 2, 512])
    wkvd = din("wkv", [DEPTH, 128, 2, 512])
    wod = din("wo", [DEPTH, 128, 8, 1024])
    fud = din("fu", [DEPTH, 22, 128, 8, 256])
    fdd = din("fd", [DEPTH, 8, 128, 22, 128])
    cckd = din("cck", [DEPTH, 128, 2, 512])
    ckrd = din("ckr", [DEPTH, 32, 512])
    onesfd = din("onesf", [128, 128])
    identd = din("ident", [128, 128])
    cscd = din("csc", [128, 256])
    dftpd = din("dftp", [128, 2, 2, 256], BF16)
    dftsd = din("dfts", [4, 128, 9, 2, 512], BF16)
    ropecd = din("ropec", [96, 2048])
    ropesd = din("ropes", [96, 2048])
    rmatd = din("rmat", [96, 96])
    seld = din("sel", [65, 64])
    yp = dout("yp", [128, 8, 1024])
    ys = dout("ys", [128, 8, 2048])
    ockv = dout("ockv", [DEPTH, 128, 2, 1024])
    okr = dout("okr", [DEPTH, 32, 1024])

    sb = lambda name, shape, dt: P.stack.enter_context(nc.sbuf_tensor("sb_" + name, list(shape), dt))
    resmap = {}

    def R(*key):
        r = resmap.get(key)
        if r is None:
            r = Res(str(key))
            resmap[key] = r
        return r

    def MM(out, lhsT, rhs, st, sp, rd, wr):
        P.op("pe", lambda e: e.matmul(out, lhsT, rhs, start=st, stop=sp), reads=rd, writes=wr)

    def ACT(out, in_, func, rd, wr, bias=None, scale=None):
        kw = {}
        if bias is not None:
            kw["bias"] = bias
        if scale is not None:
            kw["scale"] = scale
        P.op("act", lambda e: e.activation(out, in_, func, **kw), reads=rd, writes=wr)

    def TS(eng, out, in0, s1, s2, op0, op1, rd, wr):
        if s2 is None:
            P.op(eng, lambda e: e.tensor_scalar(out, in0, s1, None, op0), reads=rd, writes=wr)
        else:
            P.op(eng, lambda e: e.tensor_scalar(out, in0, s1, s2, op0, op1), reads=rd, writes=wr)

    def TT(eng, out, in0, in1, op, rd, wr):
        P.op(eng, lambda e: e.tensor_tensor(out, in0, in1, op), reads=rd, writes=wr)

    def STT(eng, out, in0, scalar, in1, op0, op1, rd, wr):
        P.op(eng, lambda e: e.scalar_tensor_tensor(out, in0, scalar, in1, op0, op1), reads=rd, writes=wr)

    def CP(eng, out, in_, rd, wr):
        if eng == "act":
            P.op("act", lambda e: e.activation(out, in_, AF.Copy), reads=rd, writes=wr)
        else:
            P.op(eng, lambda e: e.tensor_copy(out, in_), reads=rd, writes=wr)

    def RECIP(out, in_, rd, wr):
        P.op("dve", lambda e: e.reciprocal(out, in_), reads=rd, writes=wr)

    def MEMSET(eng, ap, val, wr):
        P.op(eng, lambda e: e.memset(ap, val), writes=wr)

    def DMA(q, out, in_, rd, wr):
        P.op(q, lambda e: e.dma_start(out=out, in_=in_), reads=rd, writes=wr, is_dma=True)

    banks = []
    for i in range(8):
        t = P.stack.enter_context(nc.psum_tensor(f"bank{i}", [128, 512], F32))
        banks.append((t, Res(f"bank{i}")))
    bstate = {"i": 0}

    def bank():
        b = banks[bstate["i"] % 6]
        bstate["i"] += 1
        return b

    ostate = {"i": 0}

    def obank():
        b = banks[6 + ostate["i"] % 2]
        ostate["i"] += 1
        return b

    onesf = sb("onesf", [128, 128], F32)
    onesb = sb("onesb", [128, 128], BF16)
    identb = sb("identb", [128, 128], BF16)
    cscb = sb("cscb", [128, 256], BF16)
    dftp = sb("dftp", [128, 2, 2, 256], BF16)
    rmat = sb("rmat", [96, 96], BF16)
    sel = sb("sel", [65, 64], F32)
    epsc = sb("epsc", [128, 1], F32)
    vec = sb("vecs", [128, DEPTH, NV], F32)
    modv = sb("modv", [128, DEPTH, 2, 48], F32)
    cvt = sb("cvt", [128, 8, 2], F32)
    scv = sb("scv", [128, 8, 2], BF16)
    adabt = sb("adabt", [128, DEPTH, 48], F32)
    RC = R("consts")
    DMA("sp", onesf[:], onesfd, [], [RC])
    DMA("pool", onesb[:], onesfd, [], [RC])
    DMA("pool", identb[:], identd, [], [RC])
    DMA("pool", cscb[:], cscd, [], [RC])
    DMA("sp", dftp[:], dftpd, [], [RC])
    DMA("pool", rmat[:], rmatd, [], [RC])
    DMA("sp", sel[:], seld, [], [RC])
    DMA("sp", cvt[:], cvd, [], [RC])
    for l in range(DEPTH):
        DMA("sp", vec[:, l, :], vecd[l], [], [RC])
        DMA("sp", adabt[:, l, :], adab[l], [], [RC])
    MEMSET("dve", epsc[:], EPS, [RC])

    def V(l, name, j=0, n=1, rows=128):
        o = VOFF[name] + j
        return vec[0:rows, l, o:o + n]

    arena = sb("arena", [128, ARENA_BYTES // 4], F32)
    ast = {"off": 0, "max": 0}

    def alloc(nbytes_pp, dt, shape_free, parts=128):
        off = ast["off"]
        nb = (nbytes_pp + 31) // 32 * 32
        assert off + nb <= ARENA_BYTES, f"arena overflow {off + nb}"
        ast["off"] = off + nb
        ast["max"] = max(ast["max"], ast["off"])
        ap = arena[0:parts, off // 4:(off + nb) // 4]
        esz = 4 if dt == F32 else 2
        nel = 1
        for s in shape_free:
            nel *= s
        assert nel * esz <= nb
        if dt != F32:
            ap = ap.bitcast(dt)
        ap = ap[:, 0:nel]
        if len(shape_free) == 2:
            ap = ap.rearrange("p (a b) -> p a b", a=shape_free[0])
        elif len(shape_free) == 3:
            ap = ap.rearrange("p (a b c) -> p a b c", a=shape_free[0], b=shape_free[1])
        elif len(shape_free) == 4:
            ap = ap.rearrange("p (a b c d) -> p a b c d", a=shape_free[0], b=shape_free[1], c=shape_free[2])
        return ap

    def alloc_at(off, nbytes_pp, dt, shape_free, parts=128):
        save = ast["off"]
        ast["off"] = off
        ap = alloc(nbytes_pp, dt, shape_free, parts)
        ast["off"] = save
        return ap

    def mark():
        return ast["off"]

    def release(m):
        ast["off"] = m

    ACT(scv[:], cvt[:], AF.Silu, [RC], [R("scv")])
    m0 = mark()
    wb = [alloc(8 * 1024 * 2, BF16, [8, 1024]) for _ in range(2)]
    for l in range(DEPTH):
        bk, br_ = bank()
        for v in range(6):
            w = wb[(l * 6 + v) % 2]
            rw = R("adawbuf", (l * 6 + v) % 2)
            DMA("pool", w, adaw[l, :, :, v * 1024:(v + 1) * 1024], [], [rw])
            for j in range(8):
                col = v * 8 + j
                for kc in range(8):
                    MM(bk[:, 2 * col:2 * col + 2], w[:, kc, j * 128:(j + 1) * 128], scv[:, kc, :],
                       kc == 0, kc == 7, [rw, R("scv")], [br_])
        bk3 = bk[:, 0:96].rearrange("p (a b) -> p a b", b=2)
        for ps in range(2):
            TT("dve", modv[:, l, ps, :], bk3[:, :, ps], adabt[:, l, :], ALU.add, [br_, RC], [R("modraw", l, ps)])
    MOD = sb("MOD", [128, DEPTH, 2, 48], F32)
    for l in range(DEPTH):
        for ps in range(2):
            rr = [R("modraw", l, ps), RC]
            wr = [R("MOD")]
            mv = modv[:, l, ps, :]
            STT("dve", MOD[:, l, ps, 0:8], mv[:, 8:16], 1.0, V(l, "n1g", 0, 8), ALU.add, ALU.mult, rr, wr)
            CP("dve", MOD[:, l, ps, 8:16], mv[:, 0:8], rr, wr)
            CP("dve", MOD[:, l, ps, 16:24], mv[:, 16:24], rr, wr)
            STT("dve", MOD[:, l, ps, 24:32], mv[:, 32:40], 1.0, V(l, "n2g", 0, 8), ALU.add, ALU.mult, rr, wr)
            CP("dve", MOD[:, l, ps, 32:40], mv[:, 24:32], rr, wr)
            CP("dve", MOD[:, l, ps, 40:48], mv[:, 40:48], rr, wr)
    P.barrier()
    release(m0)
    RMOD = R("MOD")

    def run_pass(ps, T, nseq, L, xin, yout, sample):
        nkctx = 512 if sample else 0
        Lk = L + nkctx
        TN = 512
        tiles = []
        for tg in range(0, T, TN):
            pcs = []
            off = 0
            while off < TN:
                s_, p0 = divmod(tg + off, L)
                ln_ = min(L - p0, TN - off)
                pcs.append((s_, p0, ln_, off))
                off += ln_
            tiles.append((tg, TN, pcs))
        HTW = nseq * (L + 2)
        UW = nseq * (L + 30)
        KW = nseq * Lk
        release(0)
        HT = alloc(8 * HTW * 2, BF16, [8, HTW])
        ZY = alloc(4 * T * 2, BF16, [4, T])
        YC = alloc(4 * T * 2, BF16, [4, T])
        OT2 = alloc(4 * T * 2, BF16, [4, T])
        m_long = mark()
        MEMSET("pool", HT, 0.0, [R("HTall")])
        P.barrier()

        def rHT(ti):
            return R("HT", ps, ti)

        def norm_to_HT(l, xt, rxt, ti, n, dst, aoff, tmpn):
            sq, rs, tmp = tmpn
            rsq, rrs, rtmp = R("sq"), R("rs"), R("ntmp")
            ACT(sq[:, :, 0:n], xt[:, :, 0:n], AF.Square, [rxt], [rsq])
            bk, rb = bank()
            for kc in range(8):
                MM(bk[:, 0:n], onesb[:], sq[:, kc, 0:n], kc == 0, kc == 7, [rsq, RC], [rb])
            ACT(rs[:, 0:n], bk[:, 0:n], AF.Ln, [rb, RC], [rrs], bias=epsc[:, 0:1], scale=1.0 / 1024)
            ACT(rs[:, 0:n], rs[:, 0:n], AF.Exp, [rrs], [rrs], scale=-0.5)
            nsl = tmp.shape[1]
            for kc in range(8):
                sl = kc % nsl
                rtmp = R("ntmp", sl)
                TT("dve", tmp[:, sl, 0:n], xt[:, kc, 0:n], rs[:, 0:n], ALU.mult, [rxt, rrs], [rtmp])
                for (off, ln_, hc) in dst:
                    ACT(HT[:, kc, hc:hc + ln_], tmp[:, sl, off:off + ln_], AF.Identity, [rtmp, RMOD], [rHT(ti), R("HTall")],
                        bias=MOD[:, l, ps, aoff + 8 + kc:aoff + 9 + kc], scale=MOD[:, l, ps, aoff + kc:aoff + kc + 1])

        for l in range(depth):
            xsrc = xin if l == 0 else yout
            release(m_long)
            QN = alloc(3 * T * 2, BF16, [3, T])
            CKV = alloc(2 * KW * 2, BF16, [2, KW])
            KR96 = alloc(KW * 4, F32, [KW], parts=96)
            m_u = mark()
            U = alloc(4 * UW * 2, BF16, [4, UW])
            m_s1 = mark()
            MEMSET("pool", U, 0.0, [R("U")])
            xts = [alloc(8 * TN * 4, F32, [8, TN]) for _ in range(2)]
            sq = alloc(8 * TN * 2, BF16, [8, TN])
            rs = alloc(TN * 4, F32, [TN])
            tmp = alloc(8 * TN * 4, F32, [8, TN])
            for ti, (tg, n, pcs) in enumerate(tiles):
                if l > 0:
                    break
                xt = xts[ti % 2]
                rxt = R("xt", ti % 2)
                DMA("sp", xt[:, :, 0:n], xsrc[:, :, tg:tg + n], [R("Y", ps, ti)], [rxt])
                norm_to_HT(l, xt, rxt, ti, n, [(0, n, tg)], 0, (sq, rs, tmp))
            if l == 0:
                P.barrier()
            release(m_s1)
            WIN = alloc(8 * WIN_COLS * 2, BF16, [8, WIN_COLS])
            wpieces = [(0, 512), (512, 1024), (1024, 1536), (1536, WIN_COLS)]
            for pi_, (a_, b_) in enumerate(wpieces):
                DMA("pool", WIN[:, :, a_:b_], wind[l, :, :, a_:b_], [], [R("WIN", pi_)])
            qf = alloc(3 * TN * 4, F32, [3, TN])
            kvf = alloc(2 * TN * 4, F32, [2, TN])
            sqq = alloc(3 * TN * 2, BF16, [3, TN])
            rs = alloc(TN * 4, F32, [TN])
            sg = alloc(TN * 4, F32, [TN])
            if sample:
                DMA("pool", CKV[:, :, 0:512], cckd[l], [], [R("CKV")])
                DMA("sp", KR96[64:96, 0:512], ckrd[l], [], [R("KR96")])
            for ti, (tg, n, pcs) in enumerate(tiles):
                rhp = [[rHT(ti), R("WIN", pi_)] for pi_ in range(4)]
                hsl = lambda kc: HT[:, kc, tg:tg + n]
                kcol = nkctx + tg
                for c in range(4):
                    bk, rb = bank()
                    for kc in range(8):
                        MM(bk[:, 0:n], WIN[:, kc, c * 128:(c + 1) * 128], hsl(kc), kc == 0, kc == 7, rhp[0], [rb])
                    CP("act", ZY[:, c, tg:tg + n], bk[:, 0:n], [rb], [R("ZY", ps)])
                for c in range(4):
                    ba, rba = bank()
                    bg, rbg = bank()
                    for kc in range(8):
                        MM(ba[:, 0:n], WIN[:, kc, 512 + c * 128:512 + (c + 1) * 128], hsl(kc), kc == 0, kc == 7, rhp[1], [rba])
                    for kc in range(8):
                        MM(bg[:, 0:n], WIN[:, kc, 1024 + c * 128:1024 + (c + 1) * 128], hsl(kc), kc == 0, kc == 7, rhp[2], [rbg])
                    ACT(sg[:, 0:n], bg[:, 0:n], AF.Sigmoid, [rbg], [R("sg")])
                    for (s_, p0, ln_, off) in pcs:
                        ucol = s_ * (L + 30) + 15 + p0
                        TT("dve", U[:, c, ucol:ucol + ln_], ba[:, off:off + ln_], sg[:, off:off + ln_], ALU.mult,
                           [rba, R("sg")], [R("U")])
                for c in range(3):
                    bk, rb = bank()
                    for kc in range(8):
                        MM(bk[:, 0:n], WIN[:, kc, 1536 + c * 128:1536 + (c + 1) * 128], hsl(kc), kc == 0, kc == 7, rhp[3], [rb])
                    CP("dve", qf[:, c, 0:n], bk[:, 0:n], [rb], [R("qf")])
                ACT(sqq[:, :, 0:n], qf[:, :, 0:n], AF.Square, [R("qf")], [R("sqq")])
                bk, rb = bank()
                for c in range(3):
                    MM(bk[:, 0:n], onesb[:], sqq[:, c, 0:n], c == 0, c == 2, [R("sqq"), RC], [rb])
                ACT(rs[:, 0:n], bk[:, 0:n], AF.Ln, [rb, RC], [R("rs1")], bias=epsc[:, 0:1], scale=1.0 / 384)
                ACT(rs[:, 0:n], rs[:, 0:n], AF.Exp, [R("rs1")], [R("rs1")], scale=-0.5)
                for c in range(3):
                    STT("dve", QN[:, c, tg:tg + n], qf[:, c, 0:n], V(l, "qng", c), rs[:, 0:n], ALU.mult, ALU.mult,
                        [R("qf"), R("rs1"), RC], [R("QN")])
                for c in range(2):
                    bk, rb = bank()
                    for kc in range(8):
                        MM(bk[:, 0:n], WIN[:, kc, 1920 + c * 128:1920 + (c + 1) * 128], hsl(kc), kc == 0, kc == 7, rhp[3], [rb])
                    CP("dve", kvf[:, c, 0:n], bk[:, 0:n], [rb], [R("kvf")])
                ACT(sqq[:, 0:2, 0:n], kvf[:, :, 0:n], AF.Square, [R("kvf")], [R("sqq")])
                bk, rb = bank()
                for c in range(2):
                    MM(bk[:, 0:n], onesb[:], sqq[:, c, 0:n], c == 0, c == 1, [R("sqq"), RC], [rb])
                ACT(rs[:, 0:n], bk[:, 0:n], AF.Ln, [rb, RC], [R("rs1")], bias=epsc[:, 0:1], scale=1.0 / 256)
                ACT(rs[:, 0:n], rs[:, 0:n], AF.Exp, [R("rs1")], [R("rs1")], scale=-0.5)
                for c in range(2):
                    STT("dve", kvf[:, c, 0:n], kvf[:, c, 0:n], V(l, "kvg", c), rs[:, 0:n], ALU.mult, ALU.mult,
                        [R("kvf"), R("rs1"), RC], [R("kvf")])
                CP("act", CKV[:, :, kcol:kcol + n], kvf[:, :, 0:n], [R("kvf")], [R("CKV")])
                if not sample:
                    DMA("sp", ockv[l, :, :, tg:tg + n], kvf[:, :, 0:n], [R("kvf")], [R("ockv")])
                bk, rb = bank()
                for kc in range(8):
                    MM(bk[0:96, 0:n], WIN[:, kc, 2176:2272], hsl(kc), kc == 0, kc == 7, rhp[3], [rb])
                CP("dve", KR96[64:96, kcol:kcol + n], bk[64:96, 0:n], [rb], [R("KR96")])
                if not sample:
                    DMA("sp", okr[l, :, tg:tg + n], KR96[64:96, kcol:kcol + n], [R("KR96")], [R("okr")])
            P.barrier()
            release(m_s1)
            diag = alloc(124 * 128 * 2, BF16, [124, 128])
            cv = alloc(4 * 512 * 4, F32, [4, 512])
            sqf = alloc(4 * 512 * 4, F32, [4, 512])
            mu = alloc(512 * 4, F32, [512])
            var = alloc(512 * 4, F32, [512])
            tcv = alloc(512 * 4, F32, [512])
            diag4 = diag.rearrange("p (k c) m -> p k c m", c=4)
            cw0 = VOFF["cw"]
            cw4 = vec[:, l, cw0:cw0 + 124].rearrange("p (k c) -> p k c", c=4)
            for cc in range(4):
                TT("dve", diag4[:, :, cc, :], identb[:].unsqueeze(1).broadcast_to([128, 31, 128]),
                   cw4[:, :, cc].unsqueeze(2).broadcast_to([128, 31, 128]), ALU.mult, [RC], [R("diag", cc)])
            for s in range(nseq):
                ub = s * (L + 30)
                for l0 in range(0, L, 512):
                    ln = min(512, L - l0)
                    tq = s * L + l0
                    for cc in range(4):
                        bk, rb = bank()
                        for k in range(31):
                            MM(bk[:, 0:ln], diag[:, k * 4 + cc, :], U[:, cc, ub + l0 + k:ub + l0 + k + ln], k == 0, k == 30,
                               [R("diag", cc), R("U")], [rb])
                        ACT(cv[:, cc, 0:ln], bk[:, 0:ln], AF.Identity, [rb, RC], [R("cv")], bias=V(l, "cb", cc), scale=1.0)
                    ACT(sqf[:, :, 0:ln], cv[:, :, 0:ln], AF.Square, [R("cv")], [R("sqf")])
                    bm, rbm = bank()
                    bq, rbq = bank()
                    for cc in range(4):
                        MM(bm[:, 0:ln], onesf[:], cv[:, cc, 0:ln], cc == 0, cc == 3, [R("cv"), RC], [rbm])
                    for cc in range(4):
                        MM(bq[:, 0:ln], onesf[:], sqf[:, cc, 0:ln], cc == 0, cc == 3, [R("sqf"), RC], [rbq])
                    TS("dve", mu[:, 0:ln], bm[:, 0:ln], 1.0 / 512, None, ALU.mult, None, [rbm], [R("mu")])
                    TT("dve", var[:, 0:ln], mu[:, 0:ln], mu[:, 0:ln], ALU.mult, [R("mu")], [R("var")])
                    STT("dve", var[:, 0:ln], bq[:, 0:ln], 1.0 / 512, var[:, 0:ln], ALU.mult, ALU.subtract,
                        [rbq, R("var")], [R("var")])
                    ACT(var[:, 0:ln], var[:, 0:ln], AF.Ln, [R("var"), RC], [R("var")], bias=epsc[:, 0:1], scale=1.0)
                    ACT(var[:, 0:ln], var[:, 0:ln], AF.Exp, [R("var")], [R("var")], scale=-0.5)
                    for cc in range(4):
                        TT("dve", tcv[:, 0:ln], cv[:, cc, 0:ln], mu[:, 0:ln], ALU.subtract, [R("cv"), R("mu")], [R("tcv")])
                        TT("dve", tcv[:, 0:ln], tcv[:, 0:ln], var[:, 0:ln], ALU.mult, [R("tcv"), R("var")], [R("tcv")])
                        ACT(YC[:, cc, tq:tq + ln], tcv[:, 0:ln], AF.Silu, [R("tcv"), RC], [R("YC")],
                            bias=V(l, "lnb", cc), scale=V(l, "lng", cc))
            P.barrier()
            release(m_u)
            wq = alloc(3 * 768 * 2, BF16, [3, 768])
            wkn = alloc(2 * 512 * 2, BF16, [2, 512])
            wkv = alloc(2 * 512 * 2, BF16, [2, 512])
            rAW = R("attw")
            DMA("pool", wq, wqd[l], [], [rAW])
            DMA("pool", wkn, wknd[l], [], [rAW])
            DMA("pool", wkv, wkvd[l], [], [rAW])
            nkb = Lk // 128
            KTs = [alloc(Lk * 2, BF16, [Lk], parts=96) for _ in range(2)]
            QTs = [alloc(512 * 2, BF16, [512], parts=96) for _ in range(2)]
            VTs = [alloc(nkb * 65 * 2, BF16, [nkb, 65]) for _ in range(2)]
            PTs = [alloc(512 * 2, BF16, [512]) for _ in range(4)]
            tset = {}
            for nm in ("K", "Q"):
                tset[nm] = dict(
                    kf=alloc(512 * 4, F32, [512], parts=96), kg=alloc(512 * 2, BF16, [512], parts=96),
                    sqk=alloc(512 * 2, BF16, [512], parts=96), rsk=alloc(512 * 4, F32, [512], parts=96),
                    t1=alloc(512 * 4, F32, [512], parts=96), t2=alloc(512 * 4, F32, [512], parts=96))
            osb = alloc(512 * 4, F32, [512], parts=65)
            rd_ = alloc(512 * 4, F32, [512], parts=64)
            obf = alloc(512 * 2, BF16, [512], parts=64)
            if sample:
                ropec = alloc(2048 * 4, F32, [2048], parts=96)
                ropes = alloc(2048 * 4, F32, [2048], parts=96)
                DMA("sp", ropec, ropecd, [], [R("rope")])
                DMA("sp", ropes, ropesd, [], [R("rope")])
            for i_ in range(2):
                MEMSET("pool", VTs[i_][:, :, 64:65], 1.0, [R("VT", i_)])
            pti = [0]
            deferred = [None]

            def norm_steps(nm, n, gname, pos0, dest, rdest, pre):
                t = tset[nm]
                kf, kg, sqk, rsk, t1, t2 = t["kf"], t["kg"], t["sqk"], t["rsk"], t["t1"], t["t2"]
                rk = lambda x: R(x, nm)
                st = []
                hold = {}

                def s1():
                    pre(kf, rk("kf"))
                    TT("dve", sqk[:, 0:n], kf[:, 0:n], kf[:, 0:n], ALU.mult, [rk("kf")], [rk("sqk")])
                st.append(s1)

                def s2():
                    b2, rb2 = bank()
                    hold["b2"] = (b2, rb2)
                    MM(b2[0:96, 0:n], onesb[0:96, 0:96], sqk[:, 0:n], True, True, [rk("sqk"), RC], [rb2])
                    ACT(rsk[:, 0:n], b2[0:96, 0:n], AF.Ln, [rb2, RC], [rk("rsk")], bias=epsc[0:96, 0:1], scale=1.0 / 96)
                    ACT(rsk[:, 0:n], rsk[:, 0:n], AF.Exp, [rk("rsk")], [rk("rsk")], scale=-0.5)
                    if pos0 is None:
                        STT("dve", dest, kf[:, 0:n], V(l, gname, 0, 1, 96), rsk[:, 0:n], ALU.mult, ALU.mult,
                            [rk("kf"), rk("rsk"), RC], [rdest])
                    else:
                        STT("dve", kg[:, 0:n], kf[:, 0:n], V(l, gname, 0, 1, 96), rsk[:, 0:n], ALU.mult, ALU.mult,
                            [rk("kf"), rk("rsk"), RC], [rk("kg")])
                        TT("dve", t1[:, 0:n], kg[:, 0:n], ropec[:, pos0:pos0 + n], ALU.mult, [rk("kg"), R("rope")], [rk("t1")])
                st.append(s2)
                if pos0 is not None:
                    def s3():
                        b3, rb3 = bank()
                        MM(b3[0:96, 0:n], rmat[:], kg[:, 0:n], True, True, [rk("kg"), RC], [rb3])
                        TT("dve", t2[:, 0:n], b3[0:96, 0:n], ropes[:, pos0:pos0 + n], ALU.mult, [rb3, R("rope")], [rk("t2")])
                        TT("dve", dest, t1[:, 0:n], t2[:, 0:n], ALU.add, [rk("t1"), rk("t2")], [rdest])
                    st.append(s3)
                return st

            def k_steps(s, h, bi):
                KT, VT = KTs[bi], VTs[bi]
                rKT, rVT = R("KT", bi), R("VT", bi)
                kc0 = s * Lk
                st = []
                for k0 in range(0, Lk, 512):
                    kn = min(512, Lk - k0)

                    def pre(kf, rkf, k0=k0, kn=kn):
                        bk, rb = bank()
                        for c in range(2):
                            MM(bk[0:64, 0:kn], wkn[:, c, h * 64:(h + 1) * 64], CKV[:, c, kc0 + k0:kc0 + k0 + kn],
                               c == 0, c == 1, [rAW, R("CKV")], [rb])
                        CP("dve", kf[0:64, 0:kn], bk[0:64, 0:kn], [rb], [rkf])
                        CP("pool", kf[64:96, 0:kn], KR96[64:96, kc0 + k0:kc0 + k0 + kn], [R("KR96")], [rkf])
                    pos0 = (k0 - nkctx) if (sample and k0 >= nkctx) else None
                    st += norm_steps("K", kn, "qkk", pos0, KT[:, k0:k0 + kn], rKT, pre)
                for kb0 in range(0, nkb, 8):
                    nb = min(8, nkb - kb0)

                    def vstep(kb0=kb0, nb=nb):
                        bk, rb = bank()
                        for i in range(nb):
                            kb = kb0 + i
                            for c in range(2):
                                MM(bk[:, i * 64:(i + 1) * 64], CKV[:, c, kc0 + kb * 128:kc0 + (kb + 1) * 128],
                                   wkv[:, c, h * 64:(h + 1) * 64], c == 0, c == 1, [rAW, R("CKV")], [rb])
                        CP("dve", VT[:, kb0:kb0 + nb, 0:64], bk[:, 0:nb * 64].rearrange("p (a b) -> p a b", b=64),
                           [rb], [rVT])
                    st.append(vstep)
                return st

            def q_steps(s, h, q0, qi_):
                qn = min(512, L - q0)
                tq = s * L + q0
                QT = QTs[qi_ % 2]
                rQT = R("QT", qi_ % 2)

                def pre(kf, rkf):
                    bk, rb = bank()
                    for c in range(3):
                        MM(bk[0:96, 0:qn], wq[:, c, h * 96:(h + 1) * 96], QN[:, c, tq:tq + qn], c == 0, c == 2,
                           [rAW, R("QN")], [rb])
                    CP("dve", kf[:, 0:qn], bk[0:96, 0:qn], [rb], [rkf])
                return norm_steps("Q", qn, "qkq", q0 if sample else None, QT[:, 0:qn], rQT, pre)

            def merge(a, b):
                out = []
                ia = ib = 0
                while ia < len(a) or ib < len(b):
                    if ia < len(a):
                        out.append(a[ia]); ia += 1
                    if ib < len(b):
                        out.append(b[ib]); ib += 1
                return out

            def attend(s, h, q0, qi_, bi, pending):
                qn = min(512, L - q0)
                tq = s * L + q0
                QT = QTs[qi_ % 2]
                rQT = R("QT", qi_ % 2)
                KT, VT = KTs[bi], VTs[bi]
                rKT, rVT = R("KT", bi), R("VT", bi)
                bo, rbo = obank()
                pendq = []
                LA = 3
                per = -(-len(pending) // nkb) if pending else 0
                stride = max(1, nkb // max(1, len(pending)))
                for kb in range(nkb):
                    bs_, rbs = bank()
                    MM(bs_[:, 0:qn], KT[:, kb * 128:(kb + 1) * 128], QT[:, 0:qn], True, True, [rKT, rQT], [rbs])
                    PT = PTs[pti[0] % 4]
                    rPT = R("PT", pti[0] % 4)
                    pti[0] += 1
                    ACT(PT[:, 0:qn], bs_[:, 0:qn], AF.Exp, [rbs], [rPT], scale=1.0 / math.sqrt(96.0))
                    pendq.append((kb, PT, rPT))
                    if len(pendq) > LA:
                        pkb, pPT, prPT = pendq.pop(0)
                        MM(bo[0:65, 0:qn], VT[:, pkb, :], pPT[:, 0:qn], pkb == 0, pkb == nkb - 1, [rVT, prPT], [rbo])
                    if kb == min(2, nkb - 1) and deferred[0] is not None:
                        deferred[0]()
                        deferred[0] = None
                    if kb % stride == 0:
                        for _ in range(per):
                            if pending:
                                pending.pop(0)()
                while pendq:
                    pkb, pPT, prPT = pendq.pop(0)
                    MM(bo[0:65, 0:qn], VT[:, pkb, :], pPT[:, 0:qn], pkb == 0, pkb == nkb - 1, [rVT, prPT], [rbo])
                while pending:
                    pending.pop(0)()
                CP("dve", osb[:, 0:qn], bo[0:65, 0:qn], [rbo], [R("osb")])

                def fin():
                    bd, rbd = bank()
                    MM(bd[0:64, 0:qn], sel[:], osb[:, 0:qn], True, True, [R("osb"), RC], [rbd])
                    ACT(rd_[:, 0:qn], bd[0:64, 0:qn], AF.Ln, [rbd], [R("rd")])
                    ACT(rd_[:, 0:qn], rd_[:, 0:qn], AF.Exp, [R("rd")], [R("rd")], scale=-1.0)
                    if h % 2 == 0:
                        TT("dve", OT2[0:64, h // 2, tq:tq + qn], osb[0:64, 0:qn], rd_[:, 0:qn], ALU.mult,
                           [R("osb"), R("rd")], [R("OT2")])
                    else:
                        TT("dve", obf[:, 0:qn], osb[0:64, 0:qn], rd_[:, 0:qn], ALU.mult, [R("osb"), R("rd")], [R("obf")])
                        DMA("sp", OT2[64:128, h // 2, tq:tq + qn], obf[:, 0:qn], [R("obf")], [R("OT2")])
                deferred[0] = fin

            if sample:
                heads = [(s_, h_) for s_ in range(nseq) for h_ in range(8)]
                qtl = list(range(0, L, 512))
                for f_ in merge(k_steps(heads[0][0], heads[0][1], 0), q_steps(heads[0][0], heads[0][1], qtl[0], 0)):
                    f_()
                qcnt = 0
                for hi, (s_, h_) in enumerate(heads):
                    knext = k_steps(heads[hi + 1][0], heads[hi + 1][1], (hi + 1) % 2) if hi + 1 < len(heads) else []
                    ksh = -(-len(knext) // len(qtl)) if knext else 0
                    for qi, q0 in enumerate(qtl):
                        if qi + 1 < len(qtl):
                            nxt = q_steps(s_, h_, qtl[qi + 1], qcnt + 1)
                        elif hi + 1 < len(heads):
                            nxt = q_steps(heads[hi + 1][0], heads[hi + 1][1], qtl[0], qcnt + 1)
                        else:
                            nxt = []
                        kpart, knext = knext[:ksh], knext[ksh:]
                        attend(s_, h_, q0, qcnt, hi % 2, merge(kpart, nxt))
                        qcnt += 1
                if deferred[0] is not None:
                    deferred[0]()
                    deferred[0] = None
            else:
                KTa = [alloc(8 * 256 * 2, BF16, [8, 256], parts=96) for _ in range(2)]
                QTa = [alloc(8 * 256 * 2, BF16, [8, 256], parts=96) for _ in range(2)]
                VTa = [alloc(2 * 8 * 65 * 2, BF16, [2, 8, 65]) for _ in range(2)]
                for i_ in range(2):
                    MEMSET("pool", VTa[i_][:, :, :, 64:65], 1.0, [R("VTa", i_)])
                bset = {}
                for nm in ("K", "Q"):
                    bset[nm] = dict(kf=alloc(8 * 256 * 4, F32, [8, 256], parts=96),
                                    sq=alloc(8 * 256 * 2, BF16, [8, 256], parts=96),
                                    rs=alloc(8 * 256 * 4, F32, [8, 256], parts=96))
                osb2 = alloc(512 * 4, F32, [512], parts=65)
                rd2 = alloc(512 * 4, F32, [512], parts=64)
                obf2 = alloc(256 * 2, BF16, [256], parts=64)

                def prep_steps(s, bi):
                    c0 = s * L
                    st = []
                    for nm in ("K", "Q"):
                        t = bset[nm]
                        kf, sq, rs = t["kf"], t["sq"], t["rs"]
                        rk = lambda x, nm=nm: R(x + "a", nm)
                        dest = (KTa if nm == "K" else QTa)[bi]
                        rdest = R("KTa" if nm == "K" else "QTa", bi)

                        def s1(nm=nm, kf=kf, sq=sq, rk=rk):
                            for hp in range(4):
                                bk, rb = bank()
                                for i in range(2):
                                    h = 2 * hp + i
                                    if nm == "K":
                                        for c in range(2):
                                            MM(bk[0:64, i * 256:(i + 1) * 256], wkn[:, c, h * 64:(h + 1) * 64],
                                               CKV[:, c, c0:c0 + L], c == 0, c == 1, [rAW, R("CKV")], [rb])
                                    else:
                                        for c in range(3):
                                            MM(bk[0:96, i * 256:(i + 1) * 256], wq[:, c, h * 96:(h + 1) * 96],
                                               QN[:, c, c0:c0 + L], c == 0, c == 2, [rAW, R("QN")], [rb])
                                rows = 64 if nm == "K" else 96
                                CP("dve" if hp % 2 == 0 else "act", kf[0:rows, 2 * hp:2 * hp + 2, :],
                                   bk[0:rows, 0:512].rearrange("p (a b) -> p a b", b=256), [rb], [rk("kf")])
                            if nm == "K":
                                CP("pool", kf[64:96, :, :], KR96[64:96, c0:c0 + L].unsqueeze(1).broadcast_to([32, 8, 256]),
                                   [R("KR96")], [rk("kf")])
                            TT("dve", sq[:, :, :], kf[:, :, :], kf[:, :, :], ALU.mult, [rk("kf")], [rk("sq")])
                        st.append(s1)

                        def s2(nm=nm, kf=kf, sq=sq, rs=rs, rk=rk, dest=dest, rdest=rdest):
                            for hp in range(4):
                                b2, rb2 = bank()
                                MM(b2[0:96, 0:512], onesb[0:96, 0:96], sq[:, 2 * hp:2 * hp + 2, :], True, True, [rk("sq"), RC], [rb2])
                                ACT(rs[:, 2 * hp:2 * hp + 2, :], b2[0:96, 0:512].rearrange("p (a b) -> p a b", b=256), AF.Ln,
                                    [rb2, RC], [rk("rs")], bias=epsc[0:96, 0:1], scale=1.0 / 96)
                            ACT(rs[:, :, :], rs[:, :, :], AF.Exp, [rk("rs")], [rk("rs")], scale=-0.5)
                            STT("dve", dest[:, :, :], kf[:, :, :], V(l, "qkk" if nm == "K" else "qkq", 0, 1, 96), rs[:, :, :],
                                ALU.mult, ALU.mult, [rk("kf"), rk("rs"), RC], [rdest])
                        st.append(s2)

                    def sv():
                        for kb in range(2):
                            bk, rb = bank()
                            for c in range(2):
                                MM(bk[:, 0:512], CKV[:, c, c0 + kb * 128:c0 + (kb + 1) * 128], wkv[:, c, :], c == 0, c == 1,
                                   [rAW, R("CKV")], [rb])
                            CP("act", VTa[bi][:, kb, :, 0:64], bk[:, 0:512].rearrange("p (a b) -> p a b", b=64), [rb], [R("VTa", bi)])
                    st.insert(2, sv)
                    return st

                def attend_seq(s, bi, pending):
                    c0 = s * L
                    KT_, QT_, VT_ = KTa[bi], QTa[bi], VTa[bi]
                    rKT_, rQT_, rVT_ = R("KTa", bi), R("QTa", bi), R("VTa", bi)
                    pend = None
                    bo = rbo = None
                    for h in range(9):
                        if h < 8:
                            bs_, rbs = bank()
                            for kb in range(2):
                                MM(bs_[:, kb * 256:(kb + 1) * 256], KT_[:, h, kb * 128:(kb + 1) * 128], QT_[:, h, :], True, True,
                                   [rKT_, rQT_], [rbs])
                            PT = PTs[pti[0] % 4]
                            rPT = R("PT", pti[0] % 4)
                            pti[0] += 1
                            ACT(PT[:, 0:512], bs_[:, 0:512], AF.Exp, [rbs], [rPT], scale=1.0 / math.sqrt(96.0))
                        if pend is not None:
                            ph, pPT, prPT = pend
                            if ph % 2 == 0:
                                bo, rbo = obank()
                            for kb in range(2):
                                MM(bo[0:65, (ph % 2) * 256:(ph % 2) * 256 + 256], VT_[:, kb, ph, :], pPT[:, kb * 256:(kb + 1) * 256],
                                   kb == 0, kb == 1, [rVT_, prPT], [rbo])
                            if ph % 2 == 1:
                                if deferred[0] is not None:
                                    deferred[0]()
                                    deferred[0] = None
                                CP("dve", osb2[:, :], bo[0:65, 0:512], [rbo], [R("osb2")])

                                def fin(hp=ph // 2):
                                    bd, rbd = bank()
                                    MM(bd[0:64, 0:512], sel[:], osb2[:, :], True, True, [R("osb2"), RC], [rbd])
                                    ACT(rd2[:, :], bd[0:64, 0:512], AF.Ln, [rbd], [R("rd2")])
                                    ACT(rd2[:, :], rd2[:, :], AF.Exp, [R("rd2")], [R("rd2")], scale=-1.0)
                                    TT("dve", OT2[0:64, hp, c0:c0 + L], osb2[0:64, 0:256], rd2[:, 0:256], ALU.mult,
                                       [R("osb2"), R("rd2")], [R("OT2")])
                                    TT("dve", obf2[:, :], osb2[0:64, 256:512], rd2[:, 256:512], ALU.mult,
                                       [R("osb2"), R("rd2")], [R("obf2")])
                                    DMA("sp", OT2[64:128, hp, c0:c0 + L], obf2[:, :], [R("obf2")], [R("OT2")])
                                deferred[0] = fin
                        pend = (h, PT, rPT) if h < 8 else None
                        if pending and h % 2 == 1:
                            pending.pop(0)()
                    while pending:
                        pending.pop(0)()

                for f_ in prep_steps(0, 0):
                    f_()
                for s_ in range(nseq):
                    nxt = prep_steps(s_ + 1, (s_ + 1) % 2) if s_ + 1 < nseq else []
                    attend_seq(s_, s_ % 2, nxt)
                if deferred[0] is not None:
                    deferred[0]()
                    deferred[0] = None
            P.barrier()
            release(m_long)
            ntb = L // 128
            MW_SLOT = ARENA_BYTES - 9216
            MW0 = alloc_at(MW_SLOT, 4608 * 2, BF16, [4608])
            DMA("pool", MW0, wmgd[l, 0], [], [R("MW", 0)])
            if sample:
                H = L // 2
                Zp = alloc(4 * (H + 1) * 2, BF16, [4, H + 1])
                Zm = alloc(4 * (H + 1) * 2, BF16, [4, H + 1])
                AB = alloc(9 * 4 * 256 * 2, BF16, [9, 4, 256])
                DBs = [alloc(9 * 2 * 512 * 2, BF16, [9, 2, 512]) for _ in range(2)]
                assert mark() <= MW_SLOT
                rZ = R("ZY", ps)
                TT("dve", Zp[:, :, 1:H], ZY[:, :, 1:H], ZY[:, :, L - 1:H:-1], ALU.add, [rZ], [R("Zp")])
                TT("dve", Zm[:, :, 1:H], ZY[:, :, 1:H], ZY[:, :, L - 1:H:-1], ALU.subtract, [rZ], [R("Zm")])
                CP("act", Zp[:, :, 0:1], ZY[:, :, 0:1], [rZ], [R("Zp")])
                CP("act", Zp[:, :, H:H + 1], ZY[:, :, H:H + 1], [rZ], [R("Zp")])
                MEMSET("pool", Zm[:, :, 0:1], 0.0, [R("Zm")])
                for tb in range(8):
                    for gp in range(2):
                        bk, rb = bank()
                        for i in range(2):
                            g = gp * 2 + i
                            MM(bk[:, i * 256:i * 256 + 128], Zp[:, g, tb * 128:(tb + 1) * 128], cscb[:, 0:128], True, True,
                               [R("Zp"), RC], [rb])
                            MM(bk[:, i * 256 + 128:(i + 1) * 256], Zm[:, g, tb * 128:(tb + 1) * 128], cscb[:, 128:256], True, True,
                               [R("Zm"), RC], [rb])
                        CP("act" if (tb + gp) % 2 else "dve", AB[:, tb, gp * 2:gp * 2 + 2, :],
                           bk[:, 0:512].rearrange("p (a b) -> p a b", b=256), [rb], [R("AB")])
                bk, rb = bank()
                for g in range(4):
                    MM(bk[0:1, g * 128:(g + 1) * 128], Zp[:, g, H:H + 1], cscb[:, 0:128], True, True, [R("Zp"), RC], [rb])
                CP("dve", AB[0:1, 8, :, 0:128], bk[0:1, 0:512].rearrange("p (a b) -> p a b", b=128), [rb], [R("AB")])
                for lb in range(L // 512):
                    DB = DBs[lb % 2]
                    rDB = R("DB", lb % 2)
                    DMA("sp", DB, dftsd[lb], [], [rDB])
                    for g in range(4):
                        bk, rb = bank()
                        for tb in range(8):
                            MM(bk[:, 0:512], AB[:, tb, g, 0:128], DB[:, tb, 0, :], tb == 0, False, [R("AB"), rDB], [rb])
                            MM(bk[:, 0:512], AB[:, tb, g, 128:256], DB[:, tb, 1, :], False, False, [R("AB"), rDB], [rb])
                        MM(bk[:, 0:512], AB[0:1, 8, g, 0:128], DB[0:1, 8, 0, :], False, True, [R("AB"), rDB], [rb])
                        CP("act" if g % 2 else "dve", ZY[:, g, lb * 512:(lb + 1) * 512], bk[:, 0:512], [rb], [rZ])
            else:
                AB = alloc(ntb * 4 * 256 * 2, BF16, [ntb, 4, 256])
                if sample:
                    DBs = [alloc(16 * 2 * 512 * 2, BF16, [16, 2, 512]) for _ in range(2)]
                assert mark() <= MW_SLOT
                for s in range(nseq):
                    tb0 = s * L
                    for tb in range(ntb):
                        for gp in range(2):
                            bk, rb = bank()
                            for i in range(2):
                                g = gp * 2 + i
                                MM(bk[:, i * 256:(i + 1) * 256], ZY[:, g, tb0 + tb * 128:tb0 + (tb + 1) * 128], cscb[:], True, True,
                                   [R("ZY", ps), RC], [rb])
                            CP("act" if (tb + gp) % 2 else "dve", AB[:, tb, gp * 2:gp * 2 + 2, :],
                               bk[:, 0:512].rearrange("p (a b) -> p a b", b=256), [rb], [R("AB")])
                    LB = 512 if sample else 256
                    for lb in range(L // LB):
                        if sample:
                            DB = DBs[lb % 2]
                            rDB = R("DB", lb % 2)
                            DMA("sp", DB, dftsd[lb], [], [rDB])
                            dsl = lambda tb, cs: DB[:, tb, cs, :]
                        else:
                            rDB = RC
                            dsl = lambda tb, cs: dftp[:, tb, cs, :]
                        bk, rb = bank()
                        ng = 512 // LB
                        for g in range(4):
                            if g % ng == 0 and g > 0:
                                bk, rb = bank()
                            o0 = (g % ng) * LB
                            for tb in range(ntb):
                                MM(bk[:, o0:o0 + LB], AB[:, tb, g, 0:128], dsl(tb, 0), tb == 0, False, [R("AB"), rDB], [rb])
                                MM(bk[:, o0:o0 + LB], AB[:, tb, g, 128:256], dsl(tb, 1), False, tb == ntb - 1, [R("AB"), rDB], [rb])
                            if g % ng == ng - 1:
                                g0 = g - ng + 1
                                CP("act" if lb % 2 else "dve", ZY[:, g0:g0 + ng, tb0 + lb * LB:tb0 + (lb + 1) * LB],
                                   bk[:, 0:512].rearrange("p (a b) -> p a b", b=LB), [rb], [R("ZY", ps)])
            P.barrier()
            release(m_long)
            MIX = alloc(8 * T * 2, BF16, [8, T])
            WO = alloc(8 * 1024 * 2, BF16, [8, 1024])
            DMA("pool", WO, wod[l], [], [R("WO")])
            m_mix = mark()
            assert mark() + 9216 + 6144 + 4096 <= MW_SLOT
            MWs = [MW0, alloc(4608 * 2, BF16, [4608])]
            G = [alloc(512 * 4, F32, [512]) for _ in range(3)]
            m1 = alloc(512 * 4, F32, [512])
            m2 = alloc(512 * 4, F32, [512])
            for j in range(8):
                MW = MWs[j % 2]
                rMW = R("MW", j % 2)
                if j + 1 < 8:
                    DMA("pool", MWs[(j + 1) % 2], wmgd[l, j + 1], [], [R("MW", (j + 1) % 2)])
                for ti, (tg, n, pcs) in enumerate(tiles):
                    for br in range(3):
                        bk, rb = bank()
                        for kc in range(8):
                            o = (kc * 3 + br) * 128
                            MM(bk[:, 0:n], MW[:, o:o + 128], HT[:, kc, tg:tg + n], kc == 0, kc == 7, [rMW, rHT(ti)], [rb])
                        ACT(G[br][:, 0:n], bk[:, 0:n], AF.Sigmoid, [rb, RC], [R("G", br)], bias=V(l, "bg", br * 8 + j), scale=1.0)
                    ybk = []
                    for bi, src, rsrc in ((0, ZY, R("ZY", ps)), (1, YC, R("YC")), (2, OT2, R("OT2"))):
                        bk, rb = bank()
                        for c in range(4):
                            o = 3072 + bi * 512 + c * 128
                            MM(bk[:, 0:n], MW[:, o:o + 128], src[:, c, tg:tg + n], c == 0, c == 3, [rMW, rsrc], [rb])
                        ybk.append((bk, rb))
                    TT("dve", m1[:, 0:n], ybk[0][0][:, 0:n], G[0][:, 0:n], ALU.mult, [ybk[0][1], R("G", 0)], [R("m1")])
                    TT("dve", m2[:, 0:n], ybk[1][0][:, 0:n], G[1][:, 0:n], ALU.mult, [ybk[1][1], R("G", 1)], [R("m2")])
                    TT("pool", m1[:, 0:n], m1[:, 0:n], m2[:, 0:n], ALU.add, [R("m1"), R("m2")], [R("m1")])
                    TT("dve", m2[:, 0:n], ybk[2][0][:, 0:n], G[2][:, 0:n], ALU.mult, [ybk[2][1], R("G", 2), R("m1")], [R("m2")])
                    TT("pool", MIX[:, j, tg:tg + n], m1[:, 0:n], m2[:, 0:n], ALU.add, [R("m1"), R("m2")], [R("MIX")])
            P.barrier()
            release(m_mix)
            xts = [alloc(8 * TN * 4, F32, [8, TN]) for _ in range(2)]
            sq = alloc(8 * TN * 2, BF16, [8, TN])
            rs = alloc(TN * 4, F32, [TN])
            tmp = alloc(8 * TN * 4, F32, [8, TN])
            for s_ in range(nseq):
                MEMSET("pool", HT[:, :, s_ * (L + 2):s_ * (L + 2) + 1], 0.0, [R("HTall")])
                MEMSET("pool", HT[:, :, s_ * (L + 2) + L + 1:s_ * (L + 2) + L + 2], 0.0, [R("HTall")])
            for ti, (tg, n, pcs) in enumerate(tiles):
                xt = xts[ti % 2]
                rxt = R("xt", ti % 2)
                DMA("sp", xt[:, :, 0:n], xsrc[:, :, tg:tg + n], [R("Y", ps, ti)], [rxt])
                for j in range(8):
                    bk, rb = bank()
                    for kc in range(8):
                        MM(bk[:, 0:n], WO[:, kc, j * 128:(j + 1) * 128], MIX[:, kc, tg:tg + n], kc == 0, kc == 7,
                           [R("WO"), R("MIX")], [rb])
                    STT("dve", xt[:, j, 0:n], bk[:, 0:n], MOD[:, l, ps, 16 + j:17 + j], xt[:, j, 0:n], ALU.mult, ALU.add,
                        [rb, rxt, RMOD], [rxt])
                DMA("sp", yout[:, :, tg:tg + n], xt[:, :, 0:n], [rxt], [R("Y", ps, ti)])
                norm_to_HT(l, xt, rxt, ti, n, [(off, ln_, s_ * (L + 2) + 1 + p0) for (s_, p0, ln_, off) in pcs], 24, (sq, rs, tmp))
            P.barrier()
            ast["off"] = (8 * HTW * 2 + 31) // 32 * 32
            ACTT = alloc(22 * T * 2, BF16, [22, T])
            m_f = mark()
            FUs = [alloc(8 * 256 * 2, BF16, [8, 256]) for _ in range(3)]
            upas = [alloc(HTW * 4, F32, [HTW]) for _ in range(2)]
            upbs = [alloc(HTW * 4, F32, [HTW]) for _ in range(2)]
            Wd = HTW - 2
            ta = alloc(Wd * 4, F32, [Wd])
            tb_ = alloc(Wd * 4, F32, [Wd])
            sa = alloc(Wd * 4, F32, [Wd])
            coltiles = [(c0, min(512, HTW - c0)) for c0 in range(0, HTW, 512)]

            def make_chain(c):
                upa, upb = upas[c % 2], upbs[c % 2]
                rua, rub = R("upa", c % 2), R("upb", c % 2)
                cb = 22 + c
                st = []
                st.append(lambda: ACT(ta[:], upa[:, 1:1 + Wd], AF.Identity, [rua, RC], [R("ta")], bias=V(l, "fb", c), scale=V(l, "fw", 44 + c)))
                st.append(lambda: ACT(tb_[:], upb[:, 1:1 + Wd], AF.Identity, [rub, RC], [R("tb")], bias=V(l, "fb", cb), scale=V(l, "fw", 44 + cb)))
                st.append(lambda: STT("dve", ta[:], upa[:, 0:Wd], V(l, "fw", c), ta[:], ALU.mult, ALU.add, [rua, R("ta"), RC], [R("ta")]))
                st.append(lambda: STT("dve", ta[:], upa[:, 2:2 + Wd], V(l, "fw", 88 + c), ta[:], ALU.mult, ALU.add, [rua, R("ta"), RC], [R("ta")]))
                st.append(lambda: ACT(sa[:], ta[:], AF.Silu, [R("ta")], [R("sa")]))
                st.append(lambda: STT("dve", tb_[:], upb[:, 0:Wd], V(l, "fw", cb), tb_[:], ALU.mult, ALU.add, [rub, R("tb"), RC], [R("tb")]))
                st.append(lambda: STT("dve", tb_[:], upb[:, 2:2 + Wd], V(l, "fw", 88 + cb), tb_[:], ALU.mult, ALU.add, [rub, R("tb"), RC], [R("tb")]))

                def mults():
                    for s_ in range(nseq):
                        j0 = s_ * (L + 2)
                        TT("pool", ACTT[:, c, s_ * L:(s_ + 1) * L], sa[:, j0:j0 + L], tb_[:, j0:j0 + L], ALU.mult,
                           [R("sa"), R("tb")], [R("ACTT")])
                st.append(mults)
                return st

            steps = []
            for c in range(22):
                FU = FUs[c % 3]
                rFU = R("FU", c % 3)
                upa, upb = upas[c % 2], upbs[c % 2]
                rua, rub = R("upa", c % 2), R("upb", c % 2)
                if c == 0:
                    DMA("pool", FU, fud[l, 0], [], [rFU])
                    DMA("pool", FUs[1], fud[l, 1], [], [R("FU", 1)])
                if c + 2 < 22:
                    DMA("pool", FUs[(c + 2) % 3], fud[l, c + 2], [], [R("FU", (c + 2) % 3)])
                per = -(-len(steps) // len(coltiles)) if steps else 0
                for (c0, cn) in coltiles:
                    ba, rba = bank()
                    bb, rbb = bank()
                    for kc in range(8):
                        MM(ba[:, 0:cn], FU[:, kc, 0:128], HT[:, kc, c0:c0 + cn], kc == 0, kc == 7, [rFU, R("HTall")], [rba])
                    for kc in range(8):
                        MM(bb[:, 0:cn], FU[:, kc, 128:256], HT[:, kc, c0:c0 + cn], kc == 0, kc == 7, [rFU, R("HTall")], [rbb])
                    CP("act", upa[:, c0:c0 + cn], ba[:, 0:cn], [rba], [rua])
                    CP("dve", upb[:, c0:c0 + cn], bb[:, 0:cn], [rbb], [rub])
                    for _ in range(per):
                        if steps:
                            steps.pop(0)()
                while steps:
                    steps.pop(0)()
                steps = make_chain(c)
            while steps:
                steps.pop(0)()
            P.barrier()
            release(m_f)
            FDs = [alloc(22 * 128 * 2, BF16, [22, 128]) for _ in range(3)]
            xts = [alloc(8 * TN * 4, F32, [8, TN]) for _ in range(2)]
            sq = alloc(8 * TN * 2, BF16, [8, TN])
            rs = alloc(TN * 4, F32, [TN])
            tmp = alloc(2 * TN * 4, F32, [2, TN])
            fcnt = 0
            til = list(enumerate(tiles))
            nfd = 8 * ((len(til) + 1) // 2)
            for p0_ in range(0, len(til), 2):
                pr = til[p0_:p0_ + 2]
                for k_, (ti, (tg, n, pcs)) in enumerate(pr):
                    DMA("sp", xts[k_][:, :, 0:n], yout[:, :, tg:tg + n], [R("Y", ps, ti)], [R("xt", k_)])
                for j in range(8):
                    FD = FDs[fcnt % 3]
                    rFD = R("FD", fcnt % 3)
                    if fcnt == 0:
                        DMA("pool", FD, fdd[l, 0], [], [rFD])
                        DMA("pool", FDs[1], fdd[l, 1], [], [R("FD", 1)])
                    if fcnt + 2 < nfd:
                        DMA("pool", FDs[(fcnt + 2) % 3], fdd[l, (fcnt + 2) % 8], [], [R("FD", (fcnt + 2) % 3)])
                    fcnt += 1
                    for k_, (ti, (tg, n, pcs)) in enumerate(pr):
                        bk, rb = bank()
                        for c in range(22):
                            MM(bk[:, 0:n], FD[:, c, :], ACTT[:, c, tg:tg + n], c == 0, c == 21, [rFD, R("ACTT")], [rb])
                        STT("dve", xts[k_][:, j, 0:n], bk[:, 0:n], MOD[:, l, ps, 40 + j:41 + j], xts[k_][:, j, 0:n],
                            ALU.mult, ALU.add, [rb, R("xt", k_), RMOD], [R("xt", k_)])
                for k_, (ti, (tg, n, pcs)) in enumerate(pr):
                    DMA("sp", yout[:, :, tg:tg + n], xts[k_][:, :, 0:n], [R("xt", k_)], [R("Y", ps, ti)])
                    if l + 1 < depth:
                        norm_to_HT(l + 1, xts[k_], R("xt", k_), ti, n, [(0, n, tg)], 0, (sq, rs, tmp))
            P.barrier()

    if 0 in passes:
        run_pass(0, 1024, 4, 256, xp, yp, False)
    if 1 in passes:
        run_pass(1, 2048, 1, 2048, xs, ys, True)
    P.counts = {e: len(P.ops[e]) for e in ENGS}
    P.nwaits = {e: sum(len(o.waits) for o in P.ops[e]) for e in ENGS}
    build_program.stats = (P.counts, P.nwaits)
    build_program.P = P
    P.emit()
    P.close()
    return nc, ast["max"]


def _km(W, kc):
    K, N = W.shape
    return np.ascontiguousarray(W.reshape(kc, 128, N).transpose(1, 0, 2))


def _cols(v, n):
    return np.ascontiguousarray(v.reshape(n, 128).T)


def _host_consts():
    c = {}
    c["onesf"] = np.ones((128, 128), np.float32)
    c["ident"] = np.eye(128, dtype=np.float32)
    k = np.arange(128)
    ang = 2 * np.pi * np.outer(k, k) / 128.0
    c["csc"] = np.concatenate([np.cos(ang), -np.sin(ang)], axis=1).astype(np.float32) / np.sqrt(128.0)

    def dft(L):
        m = np.arange(L, dtype=np.float64)
        a = 2 * np.pi * (np.outer(m, m) % L) / L
        return np.cos(a) / np.sqrt(L), np.sin(a) / np.sqrt(L)

    C, S = dft(256)
    dp = np.stack([C.reshape(2, 128, 256), S.reshape(2, 128, 256)], axis=2)
    c["dftp"] = np.ascontiguousarray(dp.transpose(1, 0, 2, 3)).astype(ml_dtypes.bfloat16)
    C, S = dft(2048)
    ds = np.zeros((4, 128, 9, 2, 512), np.float64)
    for tb in range(8):
        ds[:, :, tb, 0, :] = C[tb * 128:(tb + 1) * 128, :].reshape(128, 4, 512).transpose(1, 0, 2)
        ds[:, :, tb, 1, :] = S[tb * 128:(tb + 1) * 128, :].reshape(128, 4, 512).transpose(1, 0, 2)
    ds[:, 0, 8, 0, :] = C[1024, :].reshape(4, 512)
    c["dfts"] = ds.astype(ml_dtypes.bfloat16)
    Ls = 2048
    pos = np.arange(Ls)
    row = (pos // 64).astype(np.float32)
    col = (pos % 64).astype(np.float32)
    half = 16
    inv = (10000.0 ** (-np.arange(0, half, 2, dtype=np.float32) / half)).astype(np.float32)
    rc = np.ones((96, Ls), np.float32)
    rsn = np.zeros((96, Ls), np.float32)
    for axis, pv in enumerate((row, col)):
        a = (pv[None, :] * inv[:, None]).astype(np.float32)
        for hf in range(2):
            r0 = 64 + axis * 16 + hf * 8
            rc[r0:r0 + 8] = np.cos(a)
            rsn[r0:r0 + 8] = np.sin(a)
    c["ropec"] = rc
    c["ropes"] = rsn
    rm = np.zeros((96, 96), np.float32)
    for axis in range(2):
        for f in range(8):
            r1 = 64 + axis * 16 + f
            r2 = r1 + 8
            rm[r2, r1] = -1.0
            rm[r1, r2] = 1.0
    c["rmat"] = rm
    sl = np.zeros((65, 64), np.float32)
    sl[64, :] = 1.0
    c["sel"] = sl
    return c


_CACHE = {}


def kernel(x_prompt, x_sample, cache_ckv, cache_krope, c, c_ctx, ada_w, ada_b, norm1_g, norm2_g,
           w_in, w_gate, b_gate, w_fourier, conv_dw, conv_dw_b, conv_ln_g, conv_ln_b, w_conv_out,
           q_norm_g, w_q_up, kv_norm_g, w_kv_up, qk_q_g, qk_k_g, w_mla_out, w_out,
           ffn_up, ffn_dw, ffn_dw_b, ffn_down):
    f = lambda a: np.asarray(a, dtype=np.float32)
    x_prompt, x_sample, cache_ckv, cache_krope, c, c_ctx = map(f, (x_prompt, x_sample, cache_ckv, cache_krope, c, c_ctx))
    ada_w, ada_b, norm1_g, norm2_g, w_in, w_gate, b_gate = map(f, (ada_w, ada_b, norm1_g, norm2_g, w_in, w_gate, b_gate))
    w_fourier, conv_dw, conv_dw_b, conv_ln_g, conv_ln_b, w_conv_out = map(f, (w_fourier, conv_dw, conv_dw_b, conv_ln_g, conv_ln_b, w_conv_out))
    q_norm_g, w_q_up, kv_norm_g, w_kv_up, qk_q_g, qk_k_g, w_mla_out, w_out = map(f, (q_norm_g, w_q_up, kv_norm_g, w_kv_up, qk_q_g, qk_k_g, w_mla_out, w_out))
    ffn_up, ffn_dw, ffn_dw_b, ffn_down = map(f, (ffn_up, ffn_dw, ffn_dw_b, ffn_down))
    Ld = DEPTH
    if "nc" not in _CACHE:
        _CACHE["nc"] = build_program()[0]
        _CACHE["consts"] = _host_consts()
    nc = _CACHE["nc"]
    consts = _CACHE["consts"]

    sh = dict(consts)
    sh["adaw"] = np.stack([_km(ada_w[l], 8) for l in range(Ld)])
    sh["adab"] = np.stack([_cols(ada_b[l], 48) for l in range(Ld)])
    vecs = np.zeros((Ld, 128, NV), np.float32)
    for l in range(Ld):
        def put(name, arr):
            vecs[l, :arr.shape[0], VOFF[name]:VOFF[name] + arr.shape[1]] = arr
        put("n1g", _cols(norm1_g[l], 8))
        put("n2g", _cols(norm2_g[l], 8))
        put("bg", _cols(b_gate[l], 24))
        put("cw", _cols(conv_dw[l].reshape(-1), 124))
        put("cb", _cols(conv_dw_b[l], 4))
        put("lng", _cols(conv_ln_g[l], 4))
        put("lnb", _cols(conv_ln_b[l], 4))
        put("qng", _cols(q_norm_g[l], 3))
        put("kvg", _cols(kv_norm_g[l], 2))
        put("fw", _cols(ffn_dw[l].reshape(-1), 132))
        put("fb", _cols(ffn_dw_b[l], 44))
        put("qkq", qk_q_g[l].reshape(96, 1))
        put("qkk", qk_k_g[l].reshape(96, 1))
    sh["vec"] = vecs
    win = np.zeros((Ld, 128, 8, WIN_COLS), np.float32)
    for l in range(Ld):
        wk = _km(w_in[l], 8)
        win[l, :, :, 0:2176] = wk[:, :, 0:2176]
        win[l, :, :, 2176 + 64:2176 + 96] = wk[:, :, 2176:2208]
    sh["win"] = win
    wmg = np.zeros((Ld, 8, 128, 4608), np.float32)
    for l in range(Ld):
        g = _km(w_gate[l], 8).reshape(128, 8, 3, 8, 128)
        wf = _km(w_fourier[l], 4).reshape(128, 4, 8, 128)
        wc = _km(w_conv_out[l], 4).reshape(128, 4, 8, 128)
        wm = _km(w_mla_out[l], 4).reshape(128, 4, 8, 128)
        for j in range(8):
            wmg[l, j, :, 0:3072] = g[:, :, :, j, :].reshape(128, 3072)
            wmg[l, j, :, 3072:3584] = wf[:, :, j, :].reshape(128, 512)
            wmg[l, j, :, 3584:4096] = wc[:, :, j, :].reshape(128, 512)
            wmg[l, j, :, 4096:4608] = wm[:, :, j, :].reshape(128, 512)
    sh["wmg"] = wmg
    sh["wq"] = np.stack([_km(w_q_up[l], 3) for l in range(Ld)])
    wkv4 = np.stack([_km(w_kv_up[l], 2) for l in range(Ld)]).reshape(Ld, 128, 2, 8, 128)
    sh["wkn"] = np.ascontiguousarray(wkv4[..., 0:64]).reshape(Ld, 128, 2, 512)
    sh["wkv"] = np.ascontiguousarray(wkv4[..., 64:128]).reshape(Ld, 128, 2, 512)
    sh["wo"] = np.stack([_km(w_out[l], 8) for l in range(Ld)])
    fu = np.zeros((Ld, 22, 128, 8, 256), np.float32)
    for l in range(Ld):
        u = _km(ffn_up[l], 8)
        for cc in range(22):
            fu[l, cc, :, :, 0:128] = u[:, :, cc * 128:(cc + 1) * 128]
            fu[l, cc, :, :, 128:256] = u[:, :, 2816 + cc * 128:2816 + (cc + 1) * 128]
    sh["fu"] = fu
    fd = np.zeros((Ld, 8, 128, 22, 128), np.float32)
    for l in range(Ld):
        d = _km(ffn_down[l], 22).reshape(128, 22, 8, 128)
        fd[l] = d.transpose(2, 0, 1, 3)
    sh["fd"] = fd

    in_maps = []
    for i in range(8):
        b = i // 2
        m = dict(sh)
        xpi = x_prompt[4 * i:4 * i + 4].reshape(1024, 8, 128)
        m["xp"] = np.ascontiguousarray(xpi.transpose(2, 1, 0))
        m["xs"] = np.ascontiguousarray(x_sample[b].reshape(2048, 8, 128).transpose(2, 1, 0))
        cvv = np.stack([c_ctx, c[b]], axis=-1)
        m["cv"] = np.ascontiguousarray(cvv.reshape(8, 128, 2).transpose(1, 0, 2))
        m["cck"] = np.ascontiguousarray(cache_ckv[b].reshape(Ld, 512, 2, 128).transpose(0, 3, 2, 1))
        m["ckr"] = np.ascontiguousarray(cache_krope[b].transpose(0, 2, 1))
        in_maps.append(m)

    if _CACHE.get('prep_only'):
        return in_maps
    res = run_bass_kernel_spmd(nc, in_maps, core_ids=list(range(8)))
    rs = res.results
    y_prompt = np.zeros((32, 256, 1024), np.float32)
    y_sample = np.zeros((4, 2048, 1024), np.float32)
    new_ckv = np.zeros((32, Ld, 256, 256), np.float32)
    new_kr = np.zeros((32, Ld, 256, 32), np.float32)
    for i in range(8):
        r = rs[i]
        y_prompt[4 * i:4 * i + 4] = np.asarray(r["yp"]).transpose(2, 1, 0).reshape(4, 256, 1024)
        if i % 2 == 0:
            y_sample[i // 2] = np.asarray(r["ys"]).transpose(2, 1, 0).reshape(2048, 1024)
        ck = np.asarray(r["ockv"])
        new_ckv[4 * i:4 * i + 4] = ck.transpose(3, 0, 2, 1).reshape(4, 256, Ld, 256).transpose(0, 2, 1, 3)
        kr = np.asarray(r["okr"])
        new_kr[4 * i:4 * i + 4] = kr.transpose(2, 0, 1).reshape(4, 256, Ld, 32).transpose(0, 2, 1, 3)
    return (y_prompt, y_sample, new_ckv, new_kr)
```

```python
import math
import numpy as np
import ml_dtypes
from contextlib import ExitStack
import concourse.bass as bass
import concourse.mybir as mybir
from concourse.bass_utils import run_bass_kernel_spmd

F32 = mybir.dt.float32
BF16 = mybir.dt.bfloat16
ALU = mybir.AluOpType
AF = mybir.ActivationFunctionType

ENGS = ("pe", "act", "dve", "pool", "sp")
EPOCH = 20000
NDMASEM = 6

DEPTH = 4
EPS = 1e-6


class Res:
    __slots__ = ("name", "writers", "readers")

    def __init__(self, name=""):
        self.name = name
        self.writers = []
        self.readers = []


class Op:
    __slots__ = ("eng", "idx", "fn", "waits", "needed", "is_dma", "ev", "dma_slot")

    def __init__(self, eng, idx, fn, is_dma):
        self.eng = eng
        self.idx = idx
        self.fn = fn
        self.waits = []
        self.needed = False
        self.is_dma = is_dma
        self.ev = None
        self.dma_slot = None


class Prog:
    def __init__(self, nc):
        self.nc = nc
        self.ops = {e: [] for e in ENGS}
        self.seen = {e: {} for e in ENGS}
        self.seen_dma = {e: set() for e in ENGS}
        self.dma_count = {e: 0 for e in ENGS}
        self.dma_last = {e: {} for e in ENGS}
        self.last_compute = {e: None for e in ENGS}
        self.stack = ExitStack()

    def _dep(self, op, prod, same_ok):
        if prod is None or prod is op:
            return
        F = op.eng
        if prod.is_dma:
            if id(prod) in self.seen_dma[F]:
                return
            self.seen_dma[F].add(id(prod))
            op.waits.append(prod)
            prod.needed = True
            return
        if prod.eng == F and same_ok:
            return
        if self.seen[F].get(prod.eng, -1) >= prod.idx:
            return
        self.seen[F][prod.eng] = prod.idx
        op.waits.append(prod)
        prod.needed = True

    def op(self, eng, fn, reads=(), writes=(), is_dma=False):
        lst = self.ops[eng]
        o = Op(eng, len(lst), fn, is_dma)
        if is_dma:
            k = self.dma_count[eng]
            self.dma_count[eng] = k + 1
            o.dma_slot = k % NDMASEM
            prev = self.dma_last[eng].get(o.dma_slot)
            self.dma_last[eng][o.dma_slot] = o
            if prev is not None:
                self._dep(o, prev, same_ok=False)
        cand = {}

        def add(p, same_ok):
            if p is None or p is o:
                return
            if p.is_dma:
                self._dep(o, p, same_ok)
                return
            if p.eng == eng and same_ok:
                return
            c = cand.get(p.eng)
            if c is None or c.idx < p.idx:
                cand[p.eng] = p

        for r in reads:
            for w in r.writers:
                add(w, eng == "pe")
        for w in writes:
            for ww in w.writers:
                add(ww, True)
            for rd in w.readers:
                add(rd, True)
        for p in cand.values():
            self._dep(o, p, same_ok=False)
        for r in reads:
            r.readers.append(o)
        for w in writes:
            if w.readers:
                w.writers = [o]
                w.readers = []
            else:
                w.writers.append(o)
                if len(w.writers) > 64:
                    w.writers = w.writers[-64:] if all(x.eng == o.eng and not x.is_dma for x in w.writers) else w.writers
        lst.append(o)
        if not is_dma:
            self.last_compute[eng] = o
        return o

    def barrier(self):
        lasts = [self.last_compute[e] for e in ENGS if self.last_compute[e] is not None]
        dmas = [o for e in ENGS for o in self.dma_last[e].values()]
        o = Op("sp", len(self.ops["sp"]), (lambda e: e.nop()), False)
        for p in lasts:
            self._dep(o, p, same_ok=True)
        for p in dmas:
            self._dep(o, p, same_ok=False)
        self.ops["sp"].append(o)
        self.last_compute["sp"] = o
        for F in ENGS:
            if F == "sp":
                continue
            b = Op(F, len(self.ops[F]), None, False)
            self._dep(b, o, same_ok=True)
            self.ops[F].append(b)
            for p in lasts:
                if self.seen[F].get(p.eng, -1) < p.idx:
                    self.seen[F][p.eng] = p.idx
            for p in dmas:
                self.seen_dma[F].add(id(p))

    def emit(self):
        nc = self.nc
        st = self.stack
        sems = {}
        for e in ENGS:
            cnt = 0
            ep = 0
            for o in self.ops[e]:
                if o.is_dma or o.fn is None:
                    continue
                if o.needed:
                    if cnt >= EPOCH:
                        ep += 1
                        cnt = 0
                    cnt += 1
                    key = (e, ep)
                    if key not in sems:
                        sems[key] = st.enter_context(nc.semaphore(f"s_{e}_{ep}"))
                    o.ev = (sems[key], cnt)
        dsem = {}
        for e in ENGS:
            per_slot = {}
            for o in self.ops[e]:
                if not o.is_dma:
                    continue
                key = (e, o.dma_slot)
                if key not in dsem:
                    dsem[key] = st.enter_context(nc.semaphore(f"d_{e}_{o.dma_slot}"))
                per_slot[o.dma_slot] = per_slot.get(o.dma_slot, 0) + 16
                o.ev = (dsem[key], per_slot[o.dma_slot])

        def run(ename):
            def body(eng):
                for o in self.ops[ename]:
                    for p in o.waits:
                        eng.wait_ge(p.ev[0], p.ev[1])
                    if o.fn is None:
                        continue
                    ins = o.fn(eng)
                    if o.is_dma:
                        ins.then_inc(o.ev[0], 16)
                    elif o.needed:
                        ins.then_inc(o.ev[0], 1)
                for slot, last in self.dma_last[ename].items():
                    eng.wait_ge(last.ev[0], last.ev[1])
            return body

        with nc.Block() as block:
            block.tensor(run("pe"))
            block.scalar(run("act"))
            block.vector(run("dve"))
            block.gpsimd(run("pool"))
            block.sync(run("sp"))

    def close(self):
        self.stack.close()


VOFF = {}
_o = 0
for _n, _w in (("n1g", 8), ("n2g", 8), ("bg", 24), ("cw", 124), ("cb", 4), ("lng", 4), ("lnb", 4),
               ("qng", 3), ("kvg", 2), ("fw", 132), ("fb", 44), ("qkq", 1), ("qkk", 1)):
    VOFF[_n] = _o
    _o += _w
NV = _o

WIN_COLS = 2176 + 96
ARENA_BYTES = 190 * 1024


def build_program(depth=DEPTH, passes=(0, 1)):
    nc = bass.Bass("TRN2", target_bir_lowering=False)
    P = Prog(nc)

    def din(name, shape, dt=F32):
        return nc.dram_tensor(name, list(shape), dt, kind="ExternalInput").ap()

    def dout(name, shape, dt=F32):
        return nc.dram_tensor(name, list(shape), dt, kind="ExternalOutput").ap()

    xp = din("xp", [128, 8, 1024])
    xs = din("xs", [128, 8, 2048])
    cvd = din("cv", [128, 8, 2])
    adaw = din("adaw", [DEPTH, 128, 8, 6144])
    adab = din("adab", [DEPTH, 128, 48])
    vecd = din("vec", [DEPTH, 128, NV])
    wind = din("win", [DEPTH, 128, 8, WIN_COLS])
    wmgd = din("wmg", [DEPTH, 8, 128, 4608])
    wqd = din("wq", [DEPTH, 128, 3, 768])
    wknd = din("wkn", [DEPTH, 128, 2, 512])
    wkvd = din("wkv", [DEPTH, 128, 2, 512])
    wod = din("wo", [DEPTH, 128, 8, 1024])
    fud = din("fu", [DEPTH, 22, 128, 8, 256])
    fdd = din("fd", [DEPTH, 8, 128, 22, 128])
    cckd = din("cck", [DEPTH, 128, 2, 512])
    ckrd = din("ckr", [DEPTH, 32, 512])
    onesfd = din("onesf", [128, 128])
    identd = din("ident", [128, 128])
    cscd = din("csc", [128, 256])
    dftpd = din("dftp", [128, 2, 2, 256], BF16)
    dftsd = din("dfts", [4, 128, 9, 2, 512], BF16)
    ropecd = din("ropec", [96, 2048])
    ropesd = din("ropes", [96, 2048])
    rmatd = din("rmat", [96, 96])
    seld = din("sel", [65, 64])
    yp = dout("yp", [128, 8, 1024])
    ys = dout("ys", [128, 8, 2048])
    ockv = dout("ockv", [DEPTH, 128, 2, 1024])
    okr = dout("okr", [DEPTH, 32, 1024])

    sb = lambda name, shape, dt: P.stack.enter_context(nc.sbuf_tensor("sb_" + name, list(shape), dt))
    resmap = {}

    def R(*key):
        r = resmap.get(key)
        if r is None:
            r = Res(str(key))
            resmap[key] = r
        return r

    def MM(out, lhsT, rhs, st, sp, rd, wr):
        P.op("pe", lambda e: e.matmul(out, lhsT, rhs, start=st, stop=sp), reads=rd, writes=wr)

    def ACT(out, in_, func, rd, wr, bias=None, scale=None):
        kw = {}
        if bias is not None:
            kw["bias"] = bias
        if scale is not None:
            kw["scale"] = scale
        P.op("act", lambda e: e.activation(out, in_, func, **kw), reads=rd, writes=wr)

    def TS(eng, out, in0, s1, s2, op0, op1, rd, wr):
        if s2 is None:
            P.op(eng, lambda e: e.tensor_scalar(out, in0, s1, None, op0), reads=rd, writes=wr)
        else:
            P.op(eng, lambda e: e.tensor_scalar(out, in0, s1, s2, op0, op1), reads=rd, writes=wr)

    def TT(eng, out, in0, in1, op, rd, wr):
        P.op(eng, lambda e: e.tensor_tensor(out, in0, in1, op), reads=rd, writes=wr)

    def STT(eng, out, in0, scalar, in1, op0, op1, rd, wr):
        P.op(eng, lambda e: e.scalar_tensor_tensor(out, in0, scalar, in1, op0, op1), reads=rd, writes=wr)

    def CP(eng, out, in_, rd, wr):
        if eng == "act":
            P.op("act", lambda e: e.activation(out, in_, AF.Copy), reads=rd, writes=wr)
        else:
            P.op(eng, lambda e: e.tensor_copy(out, in_), reads=rd, writes=wr)

    def RECIP(out, in_, rd, wr):
        P.op("dve", lambda e: e.reciprocal(out, in_), reads=rd, writes=wr)

    def MEMSET(eng, ap, val, wr):
        P.op(eng, lambda e: e.memset(ap, val), writes=wr)

    def DMA(q, out, in_, rd, wr):
        P.op(q, lambda e: e.dma_start(out=out, in_=in_), reads=rd, writes=wr, is_dma=True)

    banks = []
    for i in range(8):
        t = P.stack.enter_context(nc.psum_tensor(f"bank{i}", [128, 512], F32))
        banks.append((t, Res(f"bank{i}")))
    bstate = {"i": 0}

    def bank():
        b = banks[bstate["i"] % 6]
        bstate["i"] += 1
        return b

    ostate = {"i": 0}

    def obank():
        b = banks[6 + ostate["i"] % 2]
        ostate["i"] += 1
        return b

    onesf = sb("onesf", [128, 128], F32)
    onesb = sb("onesb", [128, 128], BF16)
    identb = sb("identb", [128, 128], BF16)
    cscb = sb("cscb", [128, 256], BF16)
    dftp = sb("dftp", [128, 2, 2, 256], BF16)
    rmat = sb("rmat", [96, 96], BF16)
    sel = sb("sel", [65, 64], F32)
    epsc = sb("epsc", [128, 1], F32)
    vec = sb("vecs", [128, DEPTH, NV], F32)
    modv = sb("modv", [128, DEPTH, 2, 48], F32)
    cvt = sb("cvt", [128, 8, 2], F32)
    scv = sb("scv", [128, 8, 2], BF16)
    adabt = sb("adabt", [128, DEPTH, 48], F32)
    RC = R("consts")
    DMA("sp", onesf[:], onesfd, [], [RC])
    DMA("pool", onesb[:], onesfd, [], [RC])
    DMA("pool", identb[:], identd, [], [RC])
    DMA("pool", cscb[:], cscd, [], [RC])
    DMA("sp", dftp[:], dftpd, [], [RC])
    DMA("pool", rmat[:], rmatd, [], [RC])
    DMA("sp", sel[:], seld, [], [RC])
    DMA("sp", cvt[:], cvd, [], [RC])
    for l in range(DEPTH):
        DMA("sp", vec[:, l, :], vecd[l], [], [RC])
        DMA("sp", adabt[:, l, :], adab[l], [], [RC])
    MEMSET("dve", epsc[:], EPS, [RC])

    def V(l, name, j=0, n=1, rows=128):
        o = VOFF[name] + j
        return vec[0:rows, l, o:o + n]

    arena = sb("arena", [128, ARENA_BYTES // 4], F32)
    ast = {"off": 0, "max": 0}

    def alloc(nbytes_pp, dt, shape_free, parts=128):
        off = ast["off"]
        nb = (nbytes_pp + 31) // 32 * 32
        assert off + nb <= ARENA_BYTES, f"arena overflow {off + nb}"
        ast["off"] = off + nb
        ast["max"] = max(ast["max"], ast["off"])
        ap = arena[0:parts, off // 4:(off + nb) // 4]
        esz = 4 if dt == F32 else 2
        nel = 1
        for s in shape_free:
            nel *= s
        assert nel * esz <= nb
        if dt != F32:
            ap = ap.bitcast(dt)
        ap = ap[:, 0:nel]
        if len(shape_free) == 2:
            ap = ap.rearrange("p (a b) -> p a b", a=shape_free[0])
        elif len(shape_free) == 3:
            ap = ap.rearrange("p (a b c) -> p a b c", a=shape_free[0], b=shape_free[1])
        elif len(shape_free) == 4:
            ap = ap.rearrange("p (a b c d) -> p a b c d", a=shape_free[0], b=shape_free[1], c=shape_free[2])
        return ap

    def alloc_at(off, nbytes_pp, dt, shape_free, parts=128):
        save = ast["off"]
        ast["off"] = off
        ap = alloc(nbytes_pp, dt, shape_free, parts)
        ast["off"] = save
        return ap

    def mark():
        return ast["off"]

    def release(m):
        ast["off"] = m

    ACT(scv[:], cvt[:], AF.Silu, [RC], [R("scv")])
    m0 = mark()
    wb = [alloc(8 * 1024 * 2, BF16, [8, 1024]) for _ in range(2)]
    for l in range(DEPTH):
        bk, br_ = bank()
        for v in range(6):
            w = wb[(l * 6 + v) % 2]
            rw = R("adawbuf", (l * 6 + v) % 2)
            DMA("pool", w, adaw[l, :, :, v * 1024:(v + 1) * 1024], [], [rw])
            for j in range(8):
                col = v * 8 + j
                for kc in range(8):
                    MM(bk[:, 2 * col:2 * col + 2], w[:, kc, j * 128:(j + 1) * 128], scv[:, kc, :],
                       kc == 0, kc == 7, [rw, R("scv")], [br_])
        bk3 = bk[:, 0:96].rearrange("p (a b) -> p a b", b=2)
        for ps in range(2):
            TT("dve", modv[:, l, ps, :], bk3[:, :, ps], adabt[:, l, :], ALU.add, [br_, RC], [R("modraw", l, ps)])
    MOD = sb("MOD", [128, DEPTH, 2, 48], F32)
    for l in range(DEPTH):
        for ps in range(2):
            rr = [R("modraw", l, ps), RC]
            wr = [R("MOD")]
            mv = modv[:, l, ps, :]
            STT("dve", MOD[:, l, ps, 0:8], mv[:, 8:16], 1.0, V(l, "n1g", 0, 8), ALU.add, ALU.mult, rr, wr)
            CP("dve", MOD[:, l, ps, 8:16], mv[:, 0:8], rr, wr)
            CP("dve", MOD[:, l, ps, 16:24], mv[:, 16:24], rr, wr)
            STT("dve", MOD[:, l, ps, 24:32], mv[:, 32:40], 1.0, V(l, "n2g", 0, 8), ALU.add, ALU.mult, rr, wr)
            CP("dve", MOD[:, l, ps, 32:40], mv[:, 24:32], rr, wr)
            CP("dve", MOD[:, l, ps, 40:48], mv[:, 40:48], rr, wr)
    P.barrier()
    release(m0)
    RMOD = R("MOD")

    def run_pass(ps, T, nseq, L, xin, yout, sample):
        nkctx = 512 if sample else 0
        Lk = L + nkctx
        TN = 512
        tiles = []
        for tg in range(0, T, TN):
            pcs = []
            off = 0
            while off < TN:
                s_, p0 = divmod(tg + off, L)
                ln_ = min(L - p0, TN - off)
                pcs.append((s_, p0, ln_, off))
                off += ln_
            tiles.append((tg, TN, pcs))
        HTW = nseq * (L + 2)
        UW = nseq * (L + 30)
        KW = nseq * Lk
        release(0)
        HT = alloc(8 * HTW * 2, BF16, [8, HTW])
        ZY = alloc(4 * T * 2, BF16, [4, T])
        YC = alloc(4 * T * 2, BF16, [4, T])
        OT2 = alloc(4 * T * 2, BF16, [4, T])
        m_long = mark()
        MEMSET("pool", HT, 0.0, [R("HTall")])
        P.barrier()

        def rHT(ti):
            return R("HT", ps, ti)

        def norm_to_HT(l, xt, rxt, ti, n, dst, aoff, tmpn):
            sq, rs, tmp = tmpn
            rsq, rrs, rtmp = R("sq"), R("rs"), R("ntmp")
            ACT(sq[:, :, 0:n], xt[:, :, 0:n], AF.Square, [rxt], [rsq])
            bk, rb = bank()
            for kc in range(8):
                MM(bk[:, 0:n], onesb[:], sq[:, kc, 0:n], kc == 0, kc == 7, [rsq, RC], [rb])
            ACT(rs[:, 0:n], bk[:, 0:n], AF.Ln, [rb, RC], [rrs], bias=epsc[:, 0:1], scale=1.0 / 1024)
            ACT(rs[:, 0:n], rs[:, 0:n], AF.Exp, [rrs], [rrs], scale=-0.5)
            nsl = tmp.shape[1]
            for kc in range(8):
                sl = kc % nsl
                rtmp = R("ntmp", sl)
                TT("dve", tmp[:, sl, 0:n], xt[:, kc, 0:n], rs[:, 0:n], ALU.mult, [rxt, rrs], [rtmp])
                for (off, ln_, hc) in dst:
                    ACT(HT[:, kc, hc:hc + ln_], tmp[:, sl, off:off + ln_], AF.Identity, [rtmp, RMOD], [rHT(ti), R("HTall")],
                        bias=MOD[:, l, ps, aoff + 8 + kc:aoff + 9 + kc], scale=MOD[:, l, ps, aoff + kc:aoff + kc + 1])

        for l in range(depth):
            xsrc = xin if l == 0 else yout
            release(m_long)
            QN = alloc(3 * T * 2, BF16, [3, T])
            CKV = alloc(2 * KW * 2, BF16, [2, KW])
            KR96 = alloc(KW * 4, F32, [KW], parts=96)
            m_u = mark()
            U = alloc(4 * UW * 2, BF16, [4, UW])
            m_s1 = mark()
            MEMSET("pool", U, 0.0, [R("U")])
            xts = [alloc(8 * TN * 4, F32, [8, TN]) for _ in range(2)]
            sq = alloc(8 * TN * 2, BF16, [8, TN])
            rs = alloc(TN * 4, F32, [TN])
            tmp = alloc(8 * TN * 4, F32, [8, TN])
            for ti, (tg, n, pcs) in enumerate(tiles):
                if l > 0:
                    break
                xt = xts[ti % 2]
                rxt = R("xt", ti % 2)
                DMA("sp", xt[:, :, 0:n], xsrc[:, :, tg:tg + n], [R("Y", ps, ti)], [rxt])
                norm_to_HT(l, xt, rxt, ti, n, [(0, n, tg)], 0, (sq, rs, tmp))
            if l == 0:
                P.barrier()
            release(m_s1)
            WIN = alloc(8 * WIN_COLS * 2, BF16, [8, WIN_COLS])
            wpieces = [(0, 512), (512, 1024), (1024, 1536), (1536, WIN_COLS)]
            for pi_, (a_, b_) in enumerate(wpieces):
                DMA("pool", WIN[:, :, a_:b_], wind[l, :, :, a_:b_], [], [R("WIN", pi_)])
            qf = alloc(3 * TN * 4, F32, [3, TN])
            kvf = alloc(2 * TN * 4, F32, [2, TN])
            sqq = alloc(3 * TN * 2, BF16, [3, TN])
            rs = alloc(TN * 4, F32, [TN])
            sg = alloc(TN * 4, F32, [TN])
            if sample:
                DMA("pool", CKV[:, :, 0:512], cckd[l], [], [R("CKV")])
                DMA("sp", KR96[64:96, 0:512], ckrd[l], [], [R("KR96")])
            for ti, (tg, n, pcs) in enumerate(tiles):
                rhp = [[rHT(ti), R("WIN", pi_)] for pi_ in range(4)]
                hsl = lambda kc: HT[:, kc, tg:tg + n]
                kcol = nkctx + tg
                for c in range(4):
                    bk, rb = bank()
                    for kc in range(8):
                        MM(bk[:, 0:n], WIN[:, kc, c * 128:(c + 1) * 128], hsl(kc), kc == 0, kc == 7, rhp[0], [rb])
                    CP("act", ZY[:, c, tg:tg + n], bk[:, 0:n], [rb], [R("ZY", ps)])
                for c in range(4):
                    ba, rba = bank()
                    bg, rbg = bank()
                    for kc in range(8):
                        MM(ba[:, 0:n], WIN[:, kc, 512 + c * 128:512 + (c + 1) * 128], hsl(kc), kc == 0, kc == 7, rhp[1], [rba])
                    for kc in range(8):
                        MM(bg[:, 0:n], WIN[:, kc, 1024 + c * 128:1024 + (c + 1) * 128], hsl(kc), kc == 0, kc == 7, rhp[2], [rbg])
                    ACT(sg[:, 0:n], bg[:, 0:n], AF.Sigmoid, [rbg], [R("sg")])
                    for (s_, p0, ln_, off) in pcs:
                        ucol = s_ * (L + 30) + 15 + p0
                        TT("dve", U[:, c, ucol:ucol + ln_], ba[:, off:off + ln_], sg[:, off:off + ln_], ALU.mult,
                           [rba, R("sg")], [R("U")])
                for c in range(3):
                    bk, rb = bank()
                    for kc in range(8):
                        MM(bk[:, 0:n], WIN[:, kc, 1536 + c * 128:1536 + (c + 1) * 128], hsl(kc), kc == 0, kc == 7, rhp[3], [rb])
                    CP("dve", qf[:, c, 0:n], bk[:, 0:n], [rb], [R("qf")])
                ACT(sqq[:, :, 0:n], qf[:, :, 0:n], AF.Square, [R("qf")], [R("sqq")])
                bk, rb = bank()
                for c in range(3):
                    MM(bk[:, 0:n], onesb[:], sqq[:, c, 0:n], c == 0, c == 2, [R("sqq"), RC], [rb])
                ACT(rs[:, 0:n], bk[:, 0:n], AF.Ln, [rb, RC], [R("rs1")], bias=epsc[:, 0:1], scale=1.0 / 384)
                ACT(rs[:, 0:n], rs[:, 0:n], AF.Exp, [R("rs1")], [R("rs1")], scale=-0.5)
                for c in range(3):
                    STT("dve", QN[:, c, tg:tg + n], qf[:, c, 0:n], V(l, "qng", c), rs[:, 0:n], ALU.mult, ALU.mult,
                        [R("qf"), R("rs1"), RC], [R("QN")])
                for c in range(2):
                    bk, rb = bank()
                    for kc in range(8):
                        MM(bk[:, 0:n], WIN[:, kc, 1920 + c * 128:1920 + (c + 1) * 128], hsl(kc), kc == 0, kc == 7, rhp[3], [rb])
                    CP("dve", kvf[:, c, 0:n], bk[:, 0:n], [rb], [R("kvf")])
                ACT(sqq[:, 0:2, 0:n], kvf[:, :, 0:n], AF.Square, [R("kvf")], [R("sqq")])
                bk, rb = bank()
                for c in range(2):
                    MM(bk[:, 0:n], onesb[:], sqq[:, c, 0:n], c == 0, c == 1, [R("sqq"), RC], [rb])
                ACT(rs[:, 0:n], bk[:, 0:n], AF.Ln, [rb, RC], [R("rs1")], bias=epsc[:, 0:1], scale=1.0 / 256)
                ACT(rs[:, 0:n], rs[:, 0:n], AF.Exp, [R("rs1")], [R("rs1")], scale=-0.5)
                for c in range(2):
                    STT("dve", kvf[:, c, 0:n], kvf[:, c, 0:n], V(l, "kvg", c), rs[:, 0:n], ALU.mult, ALU.mult,
                        [R("kvf"), R("rs1"), RC], [R("kvf")])
                CP("act", CKV[:, :, kcol:kcol + n], kvf[:, :, 0:n], [R("kvf")], [R("CKV")])
                if not sample:
                    DMA("sp", ockv[l, :, :, tg:tg + n], kvf[:, :, 0:n], [R("kvf")], [R("ockv")])
                bk, rb = bank()
                for kc in range(8):
                    MM(bk[0:96, 0:n], WIN[:, kc, 2176:2272], hsl(kc), kc == 0, kc == 7, rhp[3], [rb])
                CP("dve", KR96[64:96, kcol:kcol + n], bk[64:96, 0:n], [rb], [R("KR96")])
                if not sample:
                    DMA("sp", okr[l, :, tg:tg + n], KR96[64:96, kcol:kcol + n], [R("KR96")], [R("okr")])
            P.barrier()
            release(m_s1)
            diag = alloc(124 * 128 * 2, BF16, [124, 128])
            cv = alloc(4 * 512 * 4, F32, [4, 512])
            sqf = alloc(4 * 512 * 4, F32, [4, 512])
            mu = alloc(512 * 4, F32, [512])
            var = alloc(512 * 4, F32, [512])
            tcv = alloc(512 * 4, F32, [512])
            diag4 = diag.rearrange("p (k c) m -> p k c m", c=4)
            cw0 = VOFF["cw"]
            cw4 = vec[:, l, cw0:cw0 + 124].rearrange("p (k c) -> p k c", c=4)
            for cc in range(4):
                TT("dve", diag4[:, :, cc, :], identb[:].unsqueeze(1).broadcast_to([128, 31, 128]),
                   cw4[:, :, cc].unsqueeze(2).broadcast_to([128, 31, 128]), ALU.mult, [RC], [R("diag", cc)])
            for s in range(nseq):
                ub = s * (L + 30)
                for l0 in range(0, L, 512):
                    ln = min(512, L - l0)
                    tq = s * L + l0
                    for cc in range(4):
                        bk, rb = bank()
                        for k in range(31):
                            MM(bk[:, 0:ln], diag[:, k * 4 + cc, :], U[:, cc, ub + l0 + k:ub + l0 + k + ln], k == 0, k == 30,
                               [R("diag", cc), R("U")], [rb])
                        ACT(cv[:, cc, 0:ln], bk[:, 0:ln], AF.Identity, [rb, RC], [R("cv")], bias=V(l, "cb", cc), scale=1.0)
                    ACT(sqf[:, :, 0:ln], cv[:, :, 0:ln], AF.Square, [R("cv")], [R("sqf")])
                    bm, rbm = bank()
                    bq, rbq = bank()
                    for cc in range(4):
                        MM(bm[:, 0:ln], onesf[:], cv[:, cc, 0:ln], cc == 0, cc == 3, [R("cv"), RC], [rbm])
                    for cc in range(4):
                        MM(bq[:, 0:ln], onesf[:], sqf[:, cc, 0:ln], cc == 0, cc == 3, [R("sqf"), RC], [rbq])
                    TS("dve", mu[:, 0:ln], bm[:, 0:ln], 1.0 / 512, None, ALU.mult, None, [rbm], [R("mu")])
                    TT("dve", var[:, 0:ln], mu[:, 0:ln], mu[:, 0:ln], ALU.mult, [R("mu")], [R("var")])
                    STT("dve", var[:, 0:ln], bq[:, 0:ln], 1.0 / 512, var[:, 0:ln], ALU.mult, ALU.subtract,
                        [rbq, R("var")], [R("var")])
                    ACT(var[:, 0:ln], var[:, 0:ln], AF.Ln, [R("var"), RC], [R("var")], bias=epsc[:, 0:1], scale=1.0)
                    ACT(var[:, 0:ln], var[:, 0:ln], AF.Exp, [R("var")], [R("var")], scale=-0.5)
                    for cc in range(4):
                        TT("dve", tcv[:, 0:ln], cv[:, cc, 0:ln], mu[:, 0:ln], ALU.subtract, [R("cv"), R("mu")], [R("tcv")])
                        TT("dve", tcv[:, 0:ln], tcv[:, 0:ln], var[:, 0:ln], ALU.mult, [R("tcv"), R("var")], [R("tcv")])
                        ACT(YC[:, cc, tq:tq + ln], tcv[:, 0:ln], AF.Silu, [R("tcv"), RC], [R("YC")],
                            bias=V(l, "lnb", cc), scale=V(l, "lng", cc))
            P.barrier()
            release(m_u)
            wq = alloc(3 * 768 * 2, BF16, [3, 768])
            wkn = alloc(2 * 512 * 2, BF16, [2, 512])
            wkv = alloc(2 * 512 * 2, BF16, [2, 512])
            rAW = R("attw")
            DMA("pool", wq, wqd[l], [], [rAW])
            DMA("pool", wkn, wknd[l], [], [rAW])
            DMA("pool", wkv, wkvd[l], [], [rAW])
            nkb = Lk // 128
            KTs = [alloc(Lk * 2, BF16, [Lk], parts=96) for _ in range(2)]
            QTs = [alloc(512 * 2, BF16, [512], parts=96) for _ in range(2)]
            VTs = [alloc(nkb * 65 * 2, BF16, [nkb, 65]) for _ in range(2)]
            PTs = [alloc(512 * 2, BF16, [512]) for _ in range(4)]
            tset = {}
            for nm in ("K", "Q"):
                tset[nm] = dict(
                    kf=alloc(512 * 4, F32, [512], parts=96), kg=alloc(512 * 2, BF16, [512], parts=96),
                    sqk=alloc(512 * 2, BF16, [512], parts=96), rsk=alloc(512 * 4, F32, [512], parts=96),
                    t1=alloc(512 * 4, F32, [512], parts=96), t2=alloc(512 * 4, F32, [512], parts=96))
            osb = alloc(512 * 4, F32, [512], parts=65)
            rd_ = alloc(512 * 4, F32, [512], parts=64)
            obf = alloc(512 * 2, BF16, [512], parts=64)
            if sample:
                ropec = alloc(2048 * 4, F32, [2048], parts=96)
                ropes = alloc(2048 * 4, F32, [2048], parts=96)
                DMA("sp", ropec, ropecd, [], [R("rope")])
                DMA("sp", ropes, ropesd, [], [R("rope")])
            for i_ in range(2):
                MEMSET("pool", VTs[i_][:, :, 64:65], 1.0, [R("VT", i_)])
            pti = [0]
            deferred = [None]

            def norm_steps(nm, n, gname, pos0, dest, rdest, pre):
                t = tset[nm]
                kf, kg, sqk, rsk, t1, t2 = t["kf"], t["kg"], t["sqk"], t["rsk"], t["t1"], t["t2"]
                rk = lambda x: R(x, nm)
                st = []
                hold = {}

                def s1():
                    pre(kf, rk("kf"))
                    TT("dve", sqk[:, 0:n], kf[:, 0:n], kf[:, 0:n], ALU.mult, [rk("kf")], [rk("sqk")])
                st.append(s1)

                def s2():
                    b2, rb2 = bank()
                    hold["b2"] = (b2, rb2)
                    MM(b2[0:96, 0:n], onesb[0:96, 0:96], sqk[:, 0:n], True, True, [rk("sqk"), RC], [rb2])
                    ACT(rsk[:, 0:n], b2[0:96, 0:n], AF.Ln, [rb2, RC], [rk("rsk")], bias=epsc[0:96, 0:1], scale=1.0 / 96)
                    ACT(rsk[:, 0:n], rsk[:, 0:n], AF.Exp, [rk("rsk")], [rk("rsk")], scale=-0.5)
                    if pos0 is None:
                        STT("dve", dest, kf[:, 0:n], V(l, gname, 0, 1, 96), rsk[:, 0:n], ALU.mult, ALU.mult,
                            [rk("kf"), rk("rsk"), RC], [rdest])
                    else:
                        STT("dve", kg[:, 0:n], kf[:, 0:n], V(l, gname, 0, 1, 96), rsk[:, 0:n], ALU.mult, ALU.mult,
                            [rk("kf"), rk("rsk"), RC], [rk("kg")])
                        TT("dve", t1[:, 0:n], kg[:, 0:n], ropec[:, pos0:pos0 + n], ALU.mult, [rk("kg"), R("rope")], [rk("t1")])
                st.append(s2)
                if pos0 is not None:
                    def s3():
                        b3, rb3 = bank()
                        MM(b3[0:96, 0:n], rmat[:], kg[:, 0:n], True, True, [rk("kg"), RC], [rb3])
                        TT("dve", t2[:, 0:n], b3[0:96, 0:n], ropes[:, pos0:pos0 + n], ALU.mult, [rb3, R("rope")], [rk("t2")])
                        TT("dve", dest, t1[:, 0:n], t2[:, 0:n], ALU.add, [rk("t1"), rk("t2")], [rdest])
                    st.append(s3)
                return st

            def k_steps(s, h, bi):
                KT, VT = KTs[bi], VTs[bi]
                rKT, rVT = R("KT", bi), R("VT", bi)
                kc0 = s * Lk
                st = []
                for k0 in range(0, Lk, 512):
                    kn = min(512, Lk - k0)

                    def pre(kf, rkf, k0=k0, kn=kn):
                        bk, rb = bank()
                        for c in range(2):
                            MM(bk[0:64, 0:kn], wkn[:, c, h * 64:(h + 1) * 64], CKV[:, c, kc0 + k0:kc0 + k0 + kn],
                               c == 0, c == 1, [rAW, R("CKV")], [rb])
                        CP("dve", kf[0:64, 0:kn], bk[0:64, 0:kn], [rb], [rkf])
                        CP("pool", kf[64:96, 0:kn], KR96[64:96, kc0 + k0:kc0 + k0 + kn], [R("KR96")], [rkf])
                    pos0 = (k0 - nkctx) if (sample and k0 >= nkctx) else None
                    st += norm_steps("K", kn, "qkk", pos0, KT[:, k0:k0 + kn], rKT, pre)
                for kb0 in range(0, nkb, 8):
                    nb = min(8, nkb - kb0)

                    def vstep(kb0=kb0, nb=nb):
                        bk, rb = bank()
                        for i in range(nb):
                            kb = kb0 + i
                            for c in range(2):
                                MM(bk[:, i * 64:(i + 1) * 64], CKV[:, c, kc0 + kb * 128:kc0 + (kb + 1) * 128],
                                   wkv[:, c, h * 64:(h + 1) * 64], c == 0, c == 1, [rAW, R("CKV")], [rb])
                        CP("dve", VT[:, kb0:kb0 + nb, 0:64], bk[:, 0:nb * 64].rearrange("p (a b) -> p a b", b=64),
                           [rb], [rVT])
                    st.append(vstep)
                return st

            def q_steps(s, h, q0, qi_):
                qn = min(512, L - q0)
                tq = s * L + q0
                QT = QTs[qi_ % 2]
                rQT = R("QT", qi_ % 2)

                def pre(kf, rkf):
                    bk, rb = bank()
                    for c in range(3):
                        MM(bk[0:96, 0:qn], wq[:, c, h * 96:(h + 1) * 96], QN[:, c, tq:tq + qn], c == 0, c == 2,
                           [rAW, R("QN")], [rb])
                    CP("dve", kf[:, 0:qn], bk[0:96, 0:qn], [rb], [rkf])
                return norm_steps("Q", qn, "qkq", q0 if sample else None, QT[:, 0:qn], rQT, pre)

            def merge(a, b):
                out = []
                ia = ib = 0
                while ia < len(a) or ib < len(b):
                    if ia < len(a):
                        out.append(a[ia]); ia += 1
                    if ib < len(b):
                        out.append(b[ib]); ib += 1
                return out

            def attend(s, h, q0, qi_, bi, pending):
                qn = min(512, L - q0)
                tq = s * L + q0
                QT = QTs[qi_ % 2]
                rQT = R("QT", qi_ % 2)
                KT, VT = KTs[bi], VTs[bi]
                rKT, rVT = R("KT", bi), R("VT", bi)
                bo, rbo = obank()
                pendq = []
                LA = 3
                per = -(-len(pending) // nkb) if pending else 0
                stride = max(1, nkb // max(1, len(pending)))
                for kb in range(nkb):
                    bs_, rbs = bank()
                    MM(bs_[:, 0:qn], KT[:, kb * 128:(kb + 1) * 128], QT[:, 0:qn], True, True, [rKT, rQT], [rbs])
                    PT = PTs[pti[0] % 4]
                    rPT = R("PT", pti[0] % 4)
                    pti[0] += 1
                    ACT(PT[:, 0:qn], bs_[:, 0:qn], AF.Exp, [rbs], [rPT], scale=1.0 / math.sqrt(96.0))
                    pendq.append((kb, PT, rPT))
                    if len(pendq) > LA:
                        pkb, pPT, prPT = pendq.pop(0)
                        MM(bo[0:65, 0:qn], VT[:, pkb, :], pPT[:, 0:qn], pkb == 0, pkb == nkb - 1, [rVT, prPT], [rbo])
                    if kb == min(2, nkb - 1) and deferred[0] is not None:
                        deferred[0]()
                        deferred[0] = None
                    if kb % stride == 0:
                        for _ in range(per):
                            if pending:
                                pending.pop(0)()
                while pendq:
                    pkb, pPT, prPT = pendq.pop(0)
                    MM(bo[0:65, 0:qn], VT[:, pkb, :], pPT[:, 0:qn], pkb == 0, pkb == nkb - 1, [rVT, prPT], [rbo])
                while pending:
                    pending.pop(0)()
                CP("dve", osb[:, 0:qn], bo[0:65, 0:qn], [rbo], [R("osb")])

                def fin():
                    bd, rbd = bank()
                    MM(bd[0:64, 0:qn], sel[:], osb[:, 0:qn], True, True, [R("osb"), RC], [rbd])
                    ACT(rd_[:, 0:qn], bd[0:64, 0:qn], AF.Ln, [rbd], [R("rd")])
                    ACT(rd_[:, 0:qn], rd_[:, 0:qn], AF.Exp, [R("rd")], [R("rd")], scale=-1.0)
                    if h % 2 == 0:
                        TT("dve", OT2[0:64, h // 2, tq:tq + qn], osb[0:64, 0:qn], rd_[:, 0:qn], ALU.mult,
                           [R("osb"), R("rd")], [R("OT2")])
                    else:
                        TT("dve", obf[:, 0:qn], osb[0:64, 0:qn], rd_[:, 0:qn], ALU.mult, [R("osb"), R("rd")], [R("obf")])
                        DMA("sp", OT2[64:128, h // 2, tq:tq + qn], obf[:, 0:qn], [R("obf")], [R("OT2")])
                deferred[0] = fin

            if sample:
                heads = [(s_, h_) for s_ in range(nseq) for h_ in range(8)]
                qtl = list(range(0, L, 512))
                for f_ in merge(k_steps(heads[0][0], heads[0][1], 0), q_steps(heads[0][0], heads[0][1], qtl[0], 0)):
                    f_()
                qcnt = 0
                for hi, (s_, h_) in enumerate(heads):
                    knext = k_steps(heads[hi + 1][0], heads[hi + 1][1], (hi + 1) % 2) if hi + 1 < len(heads) else []
                    ksh = -(-len(knext) // len(qtl)) if knext else 0
                    for qi, q0 in enumerate(qtl):
                        if qi + 1 < len(qtl):
                            nxt = q_steps(s_, h_, qtl[qi + 1], qcnt + 1)
                        elif hi + 1 < len(heads):
                            nxt = q_steps(heads[hi + 1][0], heads[hi + 1][1], qtl[0], qcnt + 1)
                        else:
                            nxt = []
                        kpart, knext = knext[:ksh], knext[ksh:]
                        attend(s_, h_, q0, qcnt, hi % 2, merge(kpart, nxt))
                        qcnt += 1
                if deferred[0] is not None:
                    deferred[0]()
                    deferred[0] = None
            else:
                KTa = [alloc(8 * 256 * 2, BF16, [8, 256], parts=96) for _ in range(2)]
                QTa = [alloc(8 * 256 * 2, BF16, [8, 256], parts=96) for _ in range(2)]
                VTa = [alloc(2 * 8 * 65 * 2, BF16, [2, 8, 65]) for _ in range(2)]
                for i_ in range(2):
                    MEMSET("pool", VTa[i_][:, :, :, 64:65], 1.0, [R("VTa", i_)])
                bset = {}
                for nm in ("K", "Q"):
                    bset[nm] = dict(kf=alloc(8 * 256 * 4, F32, [8, 256], parts=96),
                                    sq=alloc(8 * 256 * 2, BF16, [8, 256], parts=96),
                                    rs=alloc(8 * 256 * 4, F32, [8, 256], parts=96))
                osb2 = alloc(512 * 4, F32, [512], parts=65)
                rd2 = alloc(512 * 4, F32, [512], parts=64)
                obf2 = alloc(256 * 2, BF16, [256], parts=64)

                def prep_steps(s, bi):
                    c0 = s * L
                    st = []
                    for nm in ("K", "Q"):
                        t = bset[nm]
                        kf, sq, rs = t["kf"], t["sq"], t["rs"]
                        rk = lambda x, nm=nm: R(x + "a", nm)
                        dest = (KTa if nm == "K" else QTa)[bi]
                        rdest = R("KTa" if nm == "K" else "QTa", bi)

                        def s1(nm=nm, kf=kf, sq=sq, rk=rk):
                            for hp in range(4):
                                bk, rb = bank()
                                for i in range(2):
                                    h = 2 * hp + i
                                    if nm == "K":
                                        for c in range(2):
                                            MM(bk[0:64, i * 256:(i + 1) * 256], wkn[:, c, h * 64:(h + 1) * 64],
                                               CKV[:, c, c0:c0 + L], c == 0, c == 1, [rAW, R("CKV")], [rb])
                                    else:
                                        for c in range(3):
                                            MM(bk[0:96, i * 256:(i + 1) * 256], wq[:, c, h * 96:(h + 1) * 96],
                                               QN[:, c, c0:c0 + L], c == 0, c == 2, [rAW, R("QN")], [rb])
                                rows = 64 if nm == "K" else 96
                                CP("dve" if hp % 2 == 0 else "act", kf[0:rows, 2 * hp:2 * hp + 2, :],
                                   bk[0:rows, 0:512].rearrange("p (a b) -> p a b", b=256), [rb], [rk("kf")])
                            if nm == "K":
                                CP("pool", kf[64:96, :, :], KR96[64:96, c0:c0 + L].unsqueeze(1).broadcast_to([32, 8, 256]),
                                   [R("KR96")], [rk("kf")])
                            TT("dve", sq[:, :, :], kf[:, :, :], kf[:, :, :], ALU.mult, [rk("kf")], [rk("sq")])
                        st.append(s1)

                        def s2(nm=nm, kf=kf, sq=sq, rs=rs, rk=rk, dest=dest, rdest=rdest):
                            for hp in range(4):
                                b2, rb2 = bank()
                                MM(b2[0:96, 0:512], onesb[0:96, 0:96], sq[:, 2 * hp:2 * hp + 2, :], True, True, [rk("sq"), RC], [rb2])
                                ACT(rs[:, 2 * hp:2 * hp + 2, :], b2[0:96, 0:512].rearrange("p (a b) -> p a b", b=256), AF.Ln,
                                    [rb2, RC], [rk("rs")], bias=epsc[0:96, 0:1], scale=1.0 / 96)
                            ACT(rs[:, :, :], rs[:, :, :], AF.Exp, [rk("rs")], [rk("rs")], scale=-0.5)
                            STT("dve", dest[:, :, :], kf[:, :, :], V(l, "qkk" if nm == "K" else "qkq", 0, 1, 96), rs[:, :, :],
                                ALU.mult, ALU.mult, [rk("kf"), rk("rs"), RC], [rdest])
                        st.append(s2)

                    def sv():
                        for kb in range(2):
                            bk, rb = bank()
                            for c in range(2):
                                MM(bk[:, 0:512], CKV[:, c, c0 + kb * 128:c0 + (kb + 1) * 128], wkv[:, c, :], c == 0, c == 1,
                                   [rAW, R("CKV")], [rb])
                            CP("act", VTa[bi][:, kb, :, 0:64], bk[:, 0:512].rearrange("p (a b) -> p a b", b=64), [rb], [R("VTa", bi)])
                    return [st[0], st[2], sv, st[1], st[3]]

                def attend_seq(s, bi, pending):
                    c0 = s * L
                    KT_, QT_, VT_ = KTa[bi], QTa[bi], VTa[bi]
                    rKT_, rQT_, rVT_ = R("KTa", bi), R("QTa", bi), R("VTa", bi)
                    pend = None
                    bo = rbo = None
                    for h in range(9):
                        if h < 8:
                            bs_, rbs = bank()
                            for kb in range(2):
                                MM(bs_[:, kb * 256:(kb + 1) * 256], KT_[:, h, kb * 128:(kb + 1) * 128], QT_[:, h, :], True, True,
                                   [rKT_, rQT_], [rbs])
                            PT = PTs[pti[0] % 4]
                            rPT = R("PT", pti[0] % 4)
                            pti[0] += 1
                            ACT(PT[:, 0:512], bs_[:, 0:512], AF.Exp, [rbs], [rPT], scale=1.0 / math.sqrt(96.0))
                        if pend is not None:
                            ph, pPT, prPT = pend
                            if ph % 2 == 0:
                                bo, rbo = obank()
                            for kb in range(2):
                                MM(bo[0:65, (ph % 2) * 256:(ph % 2) * 256 + 256], VT_[:, kb, ph, :], pPT[:, kb * 256:(kb + 1) * 256],
                                   kb == 0, kb == 1, [rVT_, prPT], [rbo])
                            if ph % 2 == 1:
                                if deferred[0] is not None:
                                    deferred[0]()
                                    deferred[0] = None
                                CP("dve", osb2[:, :], bo[0:65, 0:512], [rbo], [R("osb2")])

                                def fin(hp=ph // 2):
                                    bd, rbd = bank()
                                    MM(bd[0:64, 0:512], sel[:], osb2[:, :], True, True, [R("osb2"), RC], [rbd])
                                    ACT(rd2[:, :], bd[0:64, 0:512], AF.Ln, [rbd], [R("rd2")])
                                    ACT(rd2[:, :], rd2[:, :], AF.Exp, [R("rd2")], [R("rd2")], scale=-1.0)
                                    TT("dve", OT2[0:64, hp, c0:c0 + L], osb2[0:64, 0:256], rd2[:, 0:256], ALU.mult,
                                       [R("osb2"), R("rd2")], [R("OT2")])
                                    TT("dve", obf2[:, :], osb2[0:64, 256:512], rd2[:, 256:512], ALU.mult,
                                       [R("osb2"), R("rd2")], [R("obf2")])
                                    DMA("sp", OT2[64:128, hp, c0:c0 + L], obf2[:, :], [R("obf2")], [R("OT2")])
                                deferred[0] = fin
                        pend = (h, PT, rPT) if h < 8 else None
                        if pending and h % 2 == 1:
                            pending.pop(0)()
                    while pending:
                        pending.pop(0)()

                for f_ in prep_steps(0, 0):
                    f_()
                for s_ in range(nseq):
                    nxt = prep_steps(s_ + 1, (s_ + 1) % 2) if s_ + 1 < nseq else []
                    attend_seq(s_, s_ % 2, nxt)
                if deferred[0] is not None:
                    deferred[0]()
                    deferred[0] = None
            P.barrier()
            release(m_long)
            ntb = L // 128
            MW_SLOT = ARENA_BYTES - 9216
            MW0 = alloc_at(MW_SLOT, 4608 * 2, BF16, [4608])
            DMA("pool", MW0, wmgd[l, 0], [], [R("MW", 0)])
            if sample:
                H = L // 2
                Zp = alloc(4 * (H + 1) * 2, BF16, [4, H + 1])
                Zm = alloc(4 * (H + 1) * 2, BF16, [4, H + 1])
                AB = alloc(9 * 4 * 256 * 2, BF16, [9, 4, 256])
                DBs = [alloc(9 * 2 * 512 * 2, BF16, [9, 2, 512]) for _ in range(2)]
                assert mark() <= MW_SLOT
                rZ = R("ZY", ps)
                TT("dve", Zp[:, :, 1:H], ZY[:, :, 1:H], ZY[:, :, L - 1:H:-1], ALU.add, [rZ], [R("Zp")])
                TT("dve", Zm[:, :, 1:H], ZY[:, :, 1:H], ZY[:, :, L - 1:H:-1], ALU.subtract, [rZ], [R("Zm")])
                CP("act", Zp[:, :, 0:1], ZY[:, :, 0:1], [rZ], [R("Zp")])
                CP("act", Zp[:, :, H:H + 1], ZY[:, :, H:H + 1], [rZ], [R("Zp")])
                MEMSET("pool", Zm[:, :, 0:1], 0.0, [R("Zm")])
                for tb in range(8):
                    for gp in range(2):
                        bk, rb = bank()
                        for i in range(2):
                            g = gp * 2 + i
                            MM(bk[:, i * 256:i * 256 + 128], Zp[:, g, tb * 128:(tb + 1) * 128], cscb[:, 0:128], True, True,
                               [R("Zp"), RC], [rb])
                            MM(bk[:, i * 256 + 128:(i + 1) * 256], Zm[:, g, tb * 128:(tb + 1) * 128], cscb[:, 128:256], True, True,
                               [R("Zm"), RC], [rb])
                        CP("act" if (tb + gp) % 2 else "dve", AB[:, tb, gp * 2:gp * 2 + 2, :],
                           bk[:, 0:512].rearrange("p (a b) -> p a b", b=256), [rb], [R("AB")])
                bk, rb = bank()
                for g in range(4):
                    MM(bk[0:1, g * 128:(g + 1) * 128], Zp[:, g, H:H + 1], cscb[:, 0:128], True, True, [R("Zp"), RC], [rb])
                CP("dve", AB[0:1, 8, :, 0:128], bk[0:1, 0:512].rearrange("p (a b) -> p a b", b=128), [rb], [R("AB")])
                for lb in range(L // 512):
                    DB = DBs[lb % 2]
                    rDB = R("DB", lb % 2)
                    DMA("sp", DB, dftsd[lb], [], [rDB])
                    for g in range(4):
                        bk, rb = bank()
                        for tb in range(8):
                            MM(bk[:, 0:512], AB[:, tb, g, 0:128], DB[:, tb, 0, :], tb == 0, False, [R("AB"), rDB], [rb])
                            MM(bk[:, 0:512], AB[:, tb, g, 128:256], DB[:, tb, 1, :], False, False, [R("AB"), rDB], [rb])
                        MM(bk[:, 0:512], AB[0:1, 8, g, 0:128], DB[0:1, 8, 0, :], False, True, [R("AB"), rDB], [rb])
                        CP("act" if g % 2 else "dve", ZY[:, g, lb * 512:(lb + 1) * 512], bk[:, 0:512], [rb], [rZ])
            else:
                AB = alloc(ntb * 4 * 256 * 2, BF16, [ntb, 4, 256])
                if sample:
                    DBs = [alloc(16 * 2 * 512 * 2, BF16, [16, 2, 512]) for _ in range(2)]
                assert mark() <= MW_SLOT
                for s in range(nseq):
                    tb0 = s * L
                    for tb in range(ntb):
                        for gp in range(2):
                            bk, rb = bank()
                            for i in range(2):
                                g = gp * 2 + i
                                MM(bk[:, i * 256:(i + 1) * 256], ZY[:, g, tb0 + tb * 128:tb0 + (tb + 1) * 128], cscb[:], True, True,
                                   [R("ZY", ps), RC], [rb])
                            CP("act" if (tb + gp) % 2 else "dve", AB[:, tb, gp * 2:gp * 2 + 2, :],
                               bk[:, 0:512].rearrange("p (a b) -> p a b", b=256), [rb], [R("AB")])
                    LB = 512 if sample else 256
                    for lb in range(L // LB):
                        if sample:
                            DB = DBs[lb % 2]
                            rDB = R("DB", lb % 2)
                            DMA("sp", DB, dftsd[lb], [], [rDB])
                            dsl = lambda tb, cs: DB[:, tb, cs, :]
                        else:
                            rDB = RC
                            dsl = lambda tb, cs: dftp[:, tb, cs, :]
                        bk, rb = bank()
                        ng = 512 // LB
                        for g in range(4):
                            if g % ng == 0 and g > 0:
                                bk, rb = bank()
                            o0 = (g % ng) * LB
                            for tb in range(ntb):
                                MM(bk[:, o0:o0 + LB], AB[:, tb, g, 0:128], dsl(tb, 0), tb == 0, False, [R("AB"), rDB], [rb])
                                MM(bk[:, o0:o0 + LB], AB[:, tb, g, 128:256], dsl(tb, 1), False, tb == ntb - 1, [R("AB"), rDB], [rb])
                            if g % ng == ng - 1:
                                g0 = g - ng + 1
                                CP("act" if lb % 2 else "dve", ZY[:, g0:g0 + ng, tb0 + lb * LB:tb0 + (lb + 1) * LB],
                                   bk[:, 0:512].rearrange("p (a b) -> p a b", b=LB), [rb], [R("ZY", ps)])
            P.barrier()
            release(m_long)
            MIX = alloc(8 * T * 2, BF16, [8, T])
            WO = alloc(8 * 1024 * 2, BF16, [8, 1024])
            DMA("pool", WO, wod[l], [], [R("WO")])
            m_mix = mark()
            assert mark() + 9216 + 6144 + 4096 <= MW_SLOT
            MWs = [MW0, alloc(4608 * 2, BF16, [4608])]
            G = [alloc(512 * 4, F32, [512]) for _ in range(3)]
            m1 = alloc(512 * 4, F32, [512])
            m2 = alloc(512 * 4, F32, [512])
            for j in range(8):
                MW = MWs[j % 2]
                rMW = R("MW", j % 2)
                if j + 1 < 8:
                    DMA("pool", MWs[(j + 1) % 2], wmgd[l, j + 1], [], [R("MW", (j + 1) % 2)])
                for ti, (tg, n, pcs) in enumerate(tiles):
                    for br in range(3):
                        bk, rb = bank()
                        for kc in range(8):
                            o = (kc * 3 + br) * 128
                            MM(bk[:, 0:n], MW[:, o:o + 128], HT[:, kc, tg:tg + n], kc == 0, kc == 7, [rMW, rHT(ti)], [rb])
                        ACT(G[br][:, 0:n], bk[:, 0:n], AF.Sigmoid, [rb, RC], [R("G", br)], bias=V(l, "bg", br * 8 + j), scale=1.0)
                    ybk = []
                    for bi, src, rsrc in ((0, ZY, R("ZY", ps)), (1, YC, R("YC")), (2, OT2, R("OT2"))):
                        bk, rb = bank()
                        for c in range(4):
                            o = 3072 + bi * 512 + c * 128
                            MM(bk[:, 0:n], MW[:, o:o + 128], src[:, c, tg:tg + n], c == 0, c == 3, [rMW, rsrc], [rb])
                        ybk.append((bk, rb))
                    TT("dve", m1[:, 0:n], ybk[0][0][:, 0:n], G[0][:, 0:n], ALU.mult, [ybk[0][1], R("G", 0)], [R("m1")])
                    TT("dve", m2[:, 0:n], ybk[1][0][:, 0:n], G[1][:, 0:n], ALU.mult, [ybk[1][1], R("G", 1)], [R("m2")])
                    TT("pool", m1[:, 0:n], m1[:, 0:n], m2[:, 0:n], ALU.add, [R("m1"), R("m2")], [R("m1")])
                    TT("dve", m2[:, 0:n], ybk[2][0][:, 0:n], G[2][:, 0:n], ALU.mult, [ybk[2][1], R("G", 2), R("m1")], [R("m2")])
                    TT("pool", MIX[:, j, tg:tg + n], m1[:, 0:n], m2[:, 0:n], ALU.add, [R("m1"), R("m2")], [R("MIX")])
            P.barrier()
            release(m_mix)
            xts = [alloc(8 * TN * 4, F32, [8, TN]) for _ in range(2)]
            sq = alloc(8 * TN * 2, BF16, [8, TN])
            rs = alloc(TN * 4, F32, [TN])
            tmp = alloc(8 * TN * 4, F32, [8, TN])
            for s_ in range(nseq):
                MEMSET("pool", HT[:, :, s_ * (L + 2):s_ * (L + 2) + 1], 0.0, [R("HTall")])
                MEMSET("pool", HT[:, :, s_ * (L + 2) + L + 1:s_ * (L + 2) + L + 2], 0.0, [R("HTall")])
            for ti, (tg, n, pcs) in enumerate(tiles):
                xt = xts[ti % 2]
                rxt = R("xt", ti % 2)
                DMA("sp", xt[:, :, 0:n], xsrc[:, :, tg:tg + n], [R("Y", ps, ti)], [rxt])
                for j in range(8):
                    bk, rb = bank()
                    for kc in range(8):
                        MM(bk[:, 0:n], WO[:, kc, j * 128:(j + 1) * 128], MIX[:, kc, tg:tg + n], kc == 0, kc == 7,
                           [R("WO"), R("MIX")], [rb])
                    STT("dve", xt[:, j, 0:n], bk[:, 0:n], MOD[:, l, ps, 16 + j:17 + j], xt[:, j, 0:n], ALU.mult, ALU.add,
                        [rb, rxt, RMOD], [rxt])
                DMA("sp", yout[:, :, tg:tg + n], xt[:, :, 0:n], [rxt], [R("Y", ps, ti)])
                norm_to_HT(l, xt, rxt, ti, n, [(off, ln_, s_ * (L + 2) + 1 + p0) for (s_, p0, ln_, off) in pcs], 24, (sq, rs, tmp))
            P.barrier()
            ast["off"] = (8 * HTW * 2 + 31) // 32 * 32
            ACTT = alloc(22 * T * 2, BF16, [22, T])
            m_f = mark()
            FUs = [alloc(8 * 256 * 2, BF16, [8, 256]) for _ in range(3)]
            upas = [alloc(HTW * 4, F32, [HTW]) for _ in range(2)]
            upbs = [alloc(HTW * 4, F32, [HTW]) for _ in range(2)]
            Wd = HTW - 2
            ta = alloc(Wd * 4, F32, [Wd])
            tb_ = alloc(Wd * 4, F32, [Wd])
            sa = alloc(Wd * 4, F32, [Wd])
            coltiles = [(c0, min(512, HTW - c0)) for c0 in range(0, HTW, 512)]

            def make_chain(c):
                upa, upb = upas[c % 2], upbs[c % 2]
                rua, rub = R("upa", c % 2), R("upb", c % 2)
                cb = 22 + c
                st = []
                st.append(lambda: ACT(ta[:], upa[:, 1:1 + Wd], AF.Identity, [rua, RC], [R("ta")], bias=V(l, "fb", c), scale=V(l, "fw", 44 + c)))
                st.append(lambda: ACT(tb_[:], upb[:, 1:1 + Wd], AF.Identity, [rub, RC], [R("tb")], bias=V(l, "fb", cb), scale=V(l, "fw", 44 + cb)))
                st.append(lambda: STT("dve", ta[:], upa[:, 0:Wd], V(l, "fw", c), ta[:], ALU.mult, ALU.add, [rua, R("ta"), RC], [R("ta")]))
                st.append(lambda: STT("dve", ta[:], upa[:, 2:2 + Wd], V(l, "fw", 88 + c), ta[:], ALU.mult, ALU.add, [rua, R("ta"), RC], [R("ta")]))
                st.append(lambda: ACT(sa[:], ta[:], AF.Silu, [R("ta")], [R("sa")]))
                st.append(lambda: STT("dve", tb_[:], upb[:, 0:Wd], V(l, "fw", cb), tb_[:], ALU.mult, ALU.add, [rub, R("tb"), RC], [R("tb")]))
                st.append(lambda: STT("dve", tb_[:], upb[:, 2:2 + Wd], V(l, "fw", 88 + cb), tb_[:], ALU.mult, ALU.add, [rub, R("tb"), RC], [R("tb")]))

                def mults():
                    for s_ in range(nseq):
                        j0 = s_ * (L + 2)
                        TT("pool", ACTT[:, c, s_ * L:(s_ + 1) * L], sa[:, j0:j0 + L], tb_[:, j0:j0 + L], ALU.mult,
                           [R("sa"), R("tb")], [R("ACTT")])
                st.append(mults)
                return st

            steps = []
            for c in range(22):
                FU = FUs[c % 3]
                rFU = R("FU", c % 3)
                upa, upb = upas[c % 2], upbs[c % 2]
                rua, rub = R("upa", c % 2), R("upb", c % 2)
                if c == 0:
                    DMA("pool", FU, fud[l, 0], [], [rFU])
                    DMA("pool", FUs[1], fud[l, 1], [], [R("FU", 1)])
                if c + 2 < 22:
                    DMA("pool", FUs[(c + 2) % 3], fud[l, c + 2], [], [R("FU", (c + 2) % 3)])
                per = -(-len(steps) // len(coltiles)) if steps else 0
                for (c0, cn) in coltiles:
                    ba, rba = bank()
                    bb, rbb = bank()
                    for kc in range(8):
                        MM(ba[:, 0:cn], FU[:, kc, 0:128], HT[:, kc, c0:c0 + cn], kc == 0, kc == 7, [rFU, R("HTall")], [rba])
                    for kc in range(8):
                        MM(bb[:, 0:cn], FU[:, kc, 128:256], HT[:, kc, c0:c0 + cn], kc == 0, kc == 7, [rFU, R("HTall")], [rbb])
                    CP("act", upa[:, c0:c0 + cn], ba[:, 0:cn], [rba], [rua])
                    CP("dve", upb[:, c0:c0 + cn], bb[:, 0:cn], [rbb], [rub])
                    for _ in range(per):
                        if steps:
                            steps.pop(0)()
                while steps:
                    steps.pop(0)()
                steps = make_chain(c)
            while steps:
                steps.pop(0)()
            P.barrier()
            release(m_f)
            FDs = [alloc(22 * 128 * 2, BF16, [22, 128]) for _ in range(3)]
            xts = [alloc(8 * TN * 4, F32, [8, TN]) for _ in range(2)]
            sq = alloc(8 * TN * 2, BF16, [8, TN])
            rs = alloc(TN * 4, F32, [TN])
            tmp = alloc(2 * TN * 4, F32, [2, TN])
            fcnt = 0
            til = list(enumerate(tiles))
            nfd = 8 * ((len(til) + 1) // 2)
            for p0_ in range(0, len(til), 2):
                pr = til[p0_:p0_ + 2]
                for k_, (ti, (tg, n, pcs)) in enumerate(pr):
                    DMA("sp", xts[k_][:, :, 0:n], yout[:, :, tg:tg + n], [R("Y", ps, ti)], [R("xt", k_)])
                for j in range(8):
                    FD = FDs[fcnt % 3]
                    rFD = R("FD", fcnt % 3)
                    if fcnt == 0:
                        DMA("pool", FD, fdd[l, 0], [], [rFD])
                        DMA("pool", FDs[1], fdd[l, 1], [], [R("FD", 1)])
                    if fcnt + 2 < nfd:
                        DMA("pool", FDs[(fcnt + 2) % 3], fdd[l, (fcnt + 2) % 8], [], [R("FD", (fcnt + 2) % 3)])
                    fcnt += 1
                    for k_, (ti, (tg, n, pcs)) in enumerate(pr):
                        bk, rb = bank()
                        for c in range(22):
                            MM(bk[:, 0:n], FD[:, c, :], ACTT[:, c, tg:tg + n], c == 0, c == 21, [rFD, R("ACTT")], [rb])
                        STT("dve", xts[k_][:, j, 0:n], bk[:, 0:n], MOD[:, l, ps, 40 + j:41 + j], xts[k_][:, j, 0:n],
                            ALU.mult, ALU.add, [rb, R("xt", k_), RMOD], [R("xt", k_)])
                for k_, (ti, (tg, n, pcs)) in enumerate(pr):
                    DMA("sp", yout[:, :, tg:tg + n], xts[k_][:, :, 0:n], [R("xt", k_)], [R("Y", ps, ti)])
                    if l + 1 < depth:
                        norm_to_HT(l + 1, xts[k_], R("xt", k_), ti, n, [(0, n, tg)], 0, (sq, rs, tmp))
            P.barrier()

    if 0 in passes:
        run_pass(0, 1024, 4, 256, xp, yp, False)
    if 1 in passes:
        run_pass(1, 2048, 1, 2048, xs, ys, True)
    P.counts = {e: len(P.ops[e]) for e in ENGS}
    P.nwaits = {e: sum(len(o.waits) for o in P.ops[e]) for e in ENGS}
    build_program.stats = (P.counts, P.nwaits)
    build_program.P = P
    P.emit()
    P.close()
    return nc, ast["max"]


def _km(W, kc):
    K, N = W.shape
    return np.ascontiguousarray(W.reshape(kc, 128, N).transpose(1, 0, 2))


def _cols(v, n):
    return np.ascontiguousarray(v.reshape(n, 128).T)


def _host_consts():
    c = {}
    c["onesf"] = np.ones((128, 128), np.float32)
    c["ident"] = np.eye(128, dtype=np.float32)
    k = np.arange(128)
    ang = 2 * np.pi * np.outer(k, k) / 128.0
    c["csc"] = np.concatenate([np.cos(ang), -np.sin(ang)], axis=1).astype(np.float32) / np.sqrt(128.0)

    def dft(L):
        m = np.arange(L, dtype=np.float64)
        a = 2 * np.pi * (np.outer(m, m) % L) / L
        return np.cos(a) / np.sqrt(L), np.sin(a) / np.sqrt(L)

    C, S = dft(256)
    dp = np.stack([C.reshape(2, 128, 256), S.reshape(2, 128, 256)], axis=2)
    c["dftp"] = np.ascontiguousarray(dp.transpose(1, 0, 2, 3)).astype(ml_dtypes.bfloat16)
    C, S = dft(2048)
    ds = np.zeros((4, 128, 9, 2, 512), np.float64)
    for tb in range(8):
        ds[:, :, tb, 0, :] = C[tb * 128:(tb + 1) * 128, :].reshape(128, 4, 512).transpose(1, 0, 2)
        ds[:, :, tb, 1, :] = S[tb * 128:(tb + 1) * 128, :].reshape(128, 4, 512).transpose(1, 0, 2)
    ds[:, 0, 8, 0, :] = C[1024, :].reshape(4, 512)
    c["dfts"] = ds.astype(ml_dtypes.bfloat16)
    Ls = 2048
    pos = np.arange(Ls)
    row = (pos // 64).astype(np.float32)
    col = (pos % 64).astype(np.float32)
    half = 16
    inv = (10000.0 ** (-np.arange(0, half, 2, dtype=np.float32) / half)).astype(np.float32)
    rc = np.ones((96, Ls), np.float32)
    rsn = np.zeros((96, Ls), np.float32)
    for axis, pv in enumerate((row, col)):
        a = (pv[None, :] * inv[:, None]).astype(np.float32)
        for hf in range(2):
            r0 = 64 + axis * 16 + hf * 8
            rc[r0:r0 + 8] = np.cos(a)
            rsn[r0:r0 + 8] = np.sin(a)
    c["ropec"] = rc
    c["ropes"] = rsn
    rm = np.zeros((96, 96), np.float32)
    for axis in range(2):
        for f in range(8):
            r1 = 64 + axis * 16 + f
            r2 = r1 + 8
            rm[r2, r1] = -1.0
            rm[r1, r2] = 1.0
    c["rmat"] = rm
    sl = np.zeros((65, 64), np.float32)
    sl[64, :] = 1.0
    c["sel"] = sl
    return c


_CACHE = {}


def kernel(x_prompt, x_sample, cache_ckv, cache_krope, c, c_ctx, ada_w, ada_b, norm1_g, norm2_g,
           w_in, w_gate, b_gate, w_fourier, conv_dw, conv_dw_b, conv_ln_g, conv_ln_b, w_conv_out,
           q_norm_g, w_q_up, kv_norm_g, w_kv_up, qk_q_g, qk_k_g, w_mla_out, w_out,
           ffn_up, ffn_dw, ffn_dw_b, ffn_down):
    f = lambda a: np.asarray(a, dtype=np.float32)
    x_prompt, x_sample, cache_ckv, cache_krope, c, c_ctx = map(f, (x_prompt, x_sample, cache_ckv, cache_krope, c, c_ctx))
    ada_w, ada_b, norm1_g, norm2_g, w_in, w_gate, b_gate = map(f, (ada_w, ada_b, norm1_g, norm2_g, w_in, w_gate, b_gate))
    w_fourier, conv_dw, conv_dw_b, conv_ln_g, conv_ln_b, w_conv_out = map(f, (w_fourier, conv_dw, conv_dw_b, conv_ln_g, conv_ln_b, w_conv_out))
    q_norm_g, w_q_up, kv_norm_g, w_kv_up, qk_q_g, qk_k_g, w_mla_out, w_out = map(f, (q_norm_g, w_q_up, kv_norm_g, w_kv_up, qk_q_g, qk_k_g, w_mla_out, w_out))
    ffn_up, ffn_dw, ffn_dw_b, ffn_down = map(f, (ffn_up, ffn_dw, ffn_dw_b, ffn_down))
    Ld = DEPTH
    if "nc" not in _CACHE:
        _CACHE["nc"] = build_program()[0]
        _CACHE["consts"] = _host_consts()
    nc = _CACHE["nc"]
    consts = _CACHE["consts"]

    sh = dict(consts)
    sh["adaw"] = np.stack([_km(ada_w[l], 8) for l in range(Ld)])
    sh["adab"] = np.stack([_cols(ada_b[l], 48) for l in range(Ld)])
    vecs = np.zeros((Ld, 128, NV), np.float32)
    for l in range(Ld):
        def put(name, arr):
            vecs[l, :arr.shape[0], VOFF[name]:VOFF[name] + arr.shape[1]] = arr
        put("n1g", _cols(norm1_g[l], 8))
        put("n2g", _cols(norm2_g[l], 8))
        put("bg", _cols(b_gate[l], 24))
        put("cw", _cols(conv_dw[l].reshape(-1), 124))
        put("cb", _cols(conv_dw_b[l], 4))
        put("lng", _cols(conv_ln_g[l], 4))
        put("lnb", _cols(conv_ln_b[l], 4))
        put("qng", _cols(q_norm_g[l], 3))
        put("kvg", _cols(kv_norm_g[l], 2))
        put("fw", _cols(ffn_dw[l].reshape(-1), 132))
        put("fb", _cols(ffn_dw_b[l], 44))
        put("qkq", qk_q_g[l].reshape(96, 1))
        put("qkk", qk_k_g[l].reshape(96, 1))
    sh["vec"] = vecs
    win = np.zeros((Ld, 128, 8, WIN_COLS), np.float32)
    for l in range(Ld):
        wk = _km(w_in[l], 8)
        win[l, :, :, 0:2176] = wk[:, :, 0:2176]
        win[l, :, :, 2176 + 64:2176 + 96] = wk[:, :, 2176:2208]
    sh["win"] = win
    wmg = np.zeros((Ld, 8, 128, 4608), np.float32)
    for l in range(Ld):
        g = _km(w_gate[l], 8).reshape(128, 8, 3, 8, 128)
        wf = _km(w_fourier[l], 4).reshape(128, 4, 8, 128)
        wc = _km(w_conv_out[l], 4).reshape(128, 4, 8, 128)
        wm = _km(w_mla_out[l], 4).reshape(128, 4, 8, 128)
        for j in range(8):
            wmg[l, j, :, 0:3072] = g[:, :, :, j, :].reshape(128, 3072)
            wmg[l, j, :, 3072:3584] = wf[:, :, j, :].reshape(128, 512)
            wmg[l, j, :, 3584:4096] = wc[:, :, j, :].reshape(128, 512)
            wmg[l, j, :, 4096:4608] = wm[:, :, j, :].reshape(128, 512)
    sh["wmg"] = wmg
    sh["wq"] = np.stack([_km(w_q_up[l], 3) for l in range(Ld)])
    wkv4 = np.stack([_km(w_kv_up[l], 2) for l in range(Ld)]).reshape(Ld, 128, 2, 8, 128)
    sh["wkn"] = np.ascontiguousarray(wkv4[..., 0:64]).reshape(Ld, 128, 2, 512)
    sh["wkv"] = np.ascontiguousarray(wkv4[..., 64:128]).reshape(Ld, 128, 2, 512)
    sh["wo"] = np.stack([_km(w_out[l], 8) for l in range(Ld)])
    fu = np.zeros((Ld, 22, 128, 8, 256), np.float32)
    for l in range(Ld):
        u = _km(ffn_up[l], 8)
        for cc in range(22):
            fu[l, cc, :, :, 0:128] = u[:, :, cc * 128:(cc + 1) * 128]
            fu[l, cc, :, :, 128:256] = u[:, :, 2816 + cc * 128:2816 + (cc + 1) * 128]
    sh["fu"] = fu
    fd = np.zeros((Ld, 8, 128, 22, 128), np.float32)
    for l in range(Ld):
        d = _km(ffn_down[l], 22).reshape(128, 22, 8, 128)
        fd[l] = d.transpose(2, 0, 1, 3)
    sh["fd"] = fd

    in_maps = []
    for i in range(8):
        b = i // 2
        m = dict(sh)
        xpi = x_prompt[4 * i:4 * i + 4].reshape(1024, 8, 128)
        m["xp"] = np.ascontiguousarray(xpi.transpose(2, 1, 0))
        m["xs"] = np.ascontiguousarray(x_sample[b].reshape(2048, 8, 128).transpose(2, 1, 0))
        cvv = np.stack([c_ctx, c[b]], axis=-1)
        m["cv"] = np.ascontiguousarray(cvv.reshape(8, 128, 2).transpose(1, 0, 2))
        m["cck"] = np.ascontiguousarray(cache_ckv[b].reshape(Ld, 512, 2, 128).transpose(0, 3, 2, 1))
        m["ckr"] = np.ascontiguousarray(cache_krope[b].transpose(0, 2, 1))
        in_maps.append(m)

    if _CACHE.get('prep_only'):
        return in_maps
    res = run_bass_kernel_spmd(nc, in_maps, core_ids=list(range(8)))
    rs = res.results
    y_prompt = np.zeros((32, 256, 1024), np.float32)
    y_sample = np.zeros((4, 2048, 1024), np.float32)
    new_ckv = np.zeros((32, Ld, 256, 256), np.float32)
    new_kr = np.zeros((32, Ld, 256, 32), np.float32)
    for i in range(8):
        r = rs[i]
        y_prompt[4 * i:4 * i + 4] = np.asarray(r["yp"]).transpose(2, 1, 0).reshape(4, 256, 1024)
        if i % 2 == 0:
            y_sample[i // 2] = np.asarray(r["ys"]).transpose(2, 1, 0).reshape(2048, 1024)
        ck = np.asarray(r["ockv"])
        new_ckv[4 * i:4 * i + 4] = ck.transpose(3, 0, 2, 1).reshape(4, 256, Ld, 256).transpose(0, 2, 1, 3)
        kr = np.asarray(r["okr"])
        new_kr[4 * i:4 * i + 4] = kr.transpose(2, 0, 1).reshape(4, 256, Ld, 32).transpose(0, 2, 1, 3)
    return (y_prompt, y_sample, new_ckv, new_kr)
```

```python
import math
import numpy as np
import ml_dtypes
from contextlib import ExitStack
import concourse.bass as bass
import concourse.mybir as mybir
from concourse.bass_utils import run_bass_kernel_spmd

F32 = mybir.dt.float32
BF16 = mybir.dt.bfloat16
ALU = mybir.AluOpType
AF = mybir.ActivationFunctionType

ENGS = ("pe", "act", "dve", "pool", "sp")
EPOCH = 20000
NDMASEM = 6

DEPTH = 4
EPS = 1e-6


class Res:
    __slots__ = ("name", "writers", "readers")

    def __init__(self, name=""):
        self.name = name
        self.writers = []
        self.readers = []


class Op:
    __slots__ = ("eng", "idx", "fn", "waits", "needed", "is_dma", "ev", "dma_slot")

    def __init__(self, eng, idx, fn, is_dma):
        self.eng = eng
        self.idx = idx
        self.fn = fn
        self.waits = []
        self.needed = False
        self.is_dma = is_dma
        self.ev = None
        self.dma_slot = None


class Prog:
    def __init__(self, nc):
        self.nc = nc
        self.ops = {e: [] for e in ENGS}
        self.seen = {e: {} for e in ENGS}
        self.seen_dma = {e: set() for e in ENGS}
        self.dma_count = {e: 0 for e in ENGS}
        self.dma_last = {e: {} for e in ENGS}
        self.last_compute = {e: None for e in ENGS}
        self.stack = ExitStack()

    def _dep(self, op, prod, same_ok):
        if prod is None or prod is op:
            return
        F = op.eng
        if prod.is_dma:
            if id(prod) in self.seen_dma[F]:
                return
            self.seen_dma[F].add(id(prod))
            op.waits.append(prod)
            prod.needed = True
            return
        if prod.eng == F and same_ok:
            return
        if self.seen[F].get(prod.eng, -1) >= prod.idx:
            return
        self.seen[F][prod.eng] = prod.idx
        op.waits.append(prod)
        prod.needed = True

    def op(self, eng, fn, reads=(), writes=(), is_dma=False):
        lst = self.ops[eng]
        o = Op(eng, len(lst), fn, is_dma)
        if is_dma:
            k = self.dma_count[eng]
            self.dma_count[eng] = k + 1
            o.dma_slot = k % NDMASEM
            prev = self.dma_last[eng].get(o.dma_slot)
            self.dma_last[eng][o.dma_slot] = o
            if prev is not None:
                self._dep(o, prev, same_ok=False)
        cand = {}

        def add(p, same_ok):
            if p is None or p is o:
                return
            if p.is_dma:
                self._dep(o, p, same_ok)
                return
            if p.eng == eng and same_ok:
                return
            c = cand.get(p.eng)
            if c is None or c.idx < p.idx:
                cand[p.eng] = p

        for r in reads:
            for w in r.writers:
                add(w, eng == "pe")
        for w in writes:
            for ww in w.writers:
                add(ww, True)
            for rd in w.readers:
                add(rd, True)
        for p in cand.values():
            self._dep(o, p, same_ok=False)
        for r in reads:
            r.readers.append(o)
        for w in writes:
            if w.readers:
                w.writers = [o]
                w.readers = []
            else:
                w.writers.append(o)
                if len(w.writers) > 64:
                    w.writers = w.writers[-64:] if all(x.eng == o.eng and not x.is_dma for x in w.writers) else w.writers
        lst.append(o)
        if not is_dma:
            self.last_compute[eng] = o
        return o

    def barrier(self):
        lasts = [self.last_compute[e] for e in ENGS if self.last_compute[e] is not None]
        dmas = [o for e in ENGS for o in self.dma_last[e].values()]
        o = Op("sp", len(self.ops["sp"]), (lambda e: e.nop()), False)
        for p in lasts:
            self._dep(o, p, same_ok=True)
        for p in dmas:
            self._dep(o, p, same_ok=False)
        self.ops["sp"].append(o)
        self.last_compute["sp"] = o
        for F in ENGS:
            if F == "sp":
                continue
            b = Op(F, len(self.ops[F]), None, False)
            self._dep(b, o, same_ok=True)
            self.ops[F].append(b)
            for p in lasts:
                if self.seen[F].get(p.eng, -1) < p.idx:
                    self.seen[F][p.eng] = p.idx
            for p in dmas:
                self.seen_dma[F].add(id(p))

    def emit(self):
        nc = self.nc
        st = self.stack
        sems = {}
        for e in ENGS:
            cnt = 0
            ep = 0
            for o in self.ops[e]:
                if o.is_dma or o.fn is None:
                    continue
                if o.needed:
                    if cnt >= EPOCH:
                        ep += 1
                        cnt = 0
                    cnt += 1
                    key = (e, ep)
                    if key not in sems:
                        sems[key] = st.enter_context(nc.semaphore(f"s_{e}_{ep}"))
                    o.ev = (sems[key], cnt)
        dsem = {}
        for e in ENGS:
            per_slot = {}
            for o in self.ops[e]:
                if not o.is_dma:
                    continue
                key = (e, o.dma_slot)
                if key not in dsem:
                    dsem[key] = st.enter_context(nc.semaphore(f"d_{e}_{o.dma_slot}"))
                per_slot[o.dma_slot] = per_slot.get(o.dma_slot, 0) + 16
                o.ev = (dsem[key], per_slot[o.dma_slot])

        def run(ename):
            def body(eng):
                for o in self.ops[ename]:
                    for p in o.waits:
                        eng.wait_ge(p.ev[0], p.ev[1])
                    if o.fn is None:
                        continue
                    ins = o.fn(eng)
                    if o.is_dma:
                        ins.then_inc(o.ev[0], 16)
                    elif o.needed:
                        ins.then_inc(o.ev[0], 1)
                for slot, last in self.dma_last[ename].items():
                    eng.wait_ge(last.ev[0], last.ev[1])
            return body

        with nc.Block() as block:
            block.tensor(run("pe"))
            block.scalar(run("act"))
            block.vector(run("dve"))
            block.gpsimd(run("pool"))
            block.sync(run("sp"))

    def close(self):
        self.stack.close()


VOFF = {}
_o = 0
for _n, _w in (("n1g", 8), ("n2g", 8), ("bg", 24), ("cw", 124), ("cb", 4), ("lng", 4), ("lnb", 4),
               ("qng", 3), ("kvg", 2), ("fw", 132), ("fb", 44), ("qkq", 1), ("qkk", 1)):
    VOFF[_n] = _o
    _o += _w
NV = _o

WIN_COLS = 2176 + 96
ARENA_BYTES = 190 * 1024


def build_program(depth=DEPTH, passes=(0, 1)):
    nc = bass.Bass("TRN2", target_bir_lowering=False)
    P = Prog(nc)

    def din(name, shape, dt=F32):
        return nc.dram_tensor(name, list(shape), dt, kind="ExternalInput").ap()

    def dout(name, shape, dt=F32):
        return nc.dram_tensor(name, list(shape), dt, kind="ExternalOutput").ap()

    xp = din("xp", [128, 8, 1024])
    xs = din("xs", [128, 8, 2048])
    cvd = din("cv", [128, 8, 2])
    adaw = din("adaw", [DEPTH, 128, 8, 6144])
    adab = din("adab", [DEPTH, 128, 48])
    vecd = din("vec", [DEPTH, 128, NV])
    wind = din("win", [DEPTH, 128, 8, WIN_COLS])
    wmgd = din("wmg", [DEPTH, 8, 128, 4608])
    wqd = din("wq", [DEPTH, 128, 3, 768])
    wknd = din("wkn", [DEPTH, 128, 2, 512])
    wkvd = din("wkv", [DEPTH, 128, 2, 512])
    wod = din("wo", [DEPTH, 128, 8, 1024])
    fud = din("fu", [DEPTH, 22, 128, 8, 256])
    fdd = din("fd", [DEPTH, 8, 128, 22, 128])
    cckd = din("cck", [DEPTH, 128, 2, 512])
    ckrd = din("ckr", [DEPTH, 32, 512])
    onesfd = din("onesf", [128, 128])
    identd = din("ident", [128, 128])
    cscd = din("csc", [128, 256])
    dftpd = din("dftp", [128, 2, 2, 256], BF16)
    dftsd = din("dfts", [4, 128, 9, 2, 512], BF16)
    ropecd = din("ropec", [96, 2048])
    ropesd = din("ropes", [96, 2048])
    rmatd = din("rmat", [96, 96])
    seld = din("sel", [65, 64])
    yp = dout("yp", [128, 8, 1024])
    ys = dout("ys", [128, 8, 2048])
    ockv = dout("ockv", [DEPTH, 128, 2, 1024])
    okr = dout("okr", [DEPTH, 32, 1024])

    sb = lambda name, shape, dt: P.stack.enter_context(nc.sbuf_tensor("sb_" + name, list(shape), dt))
    resmap = {}

    def R(*key):
        r = resmap.get(key)
        if r is None:
            r = Res(str(key))
            resmap[key] = r
        return r

    def MM(out, lhsT, rhs, st, sp, rd, wr):
        P.op("pe", lambda e: e.matmul(out, lhsT, rhs, start=st, stop=sp), reads=rd, writes=wr)

    def ACT(out, in_, func, rd, wr, bias=None, scale=None):
        kw = {}
        if bias is not None:
            kw["bias"] = bias
        if scale is not None:
            kw["scale"] = scale
        P.op("act", lambda e: e.activation(out, in_, func, **kw), reads=rd, writes=wr)

    def TS(eng, out, in0, s1, s2, op0, op1, rd, wr):
        if s2 is None:
            P.op(eng, lambda e: e.tensor_scalar(out, in0, s1, None, op0), reads=rd, writes=wr)
        else:
            P.op(eng, lambda e: e.tensor_scalar(out, in0, s1, s2, op0, op1), reads=rd, writes=wr)

    def TT(eng, out, in0, in1, op, rd, wr):
        P.op(eng, lambda e: e.tensor_tensor(out, in0, in1, op), reads=rd, writes=wr)

    def STT(eng, out, in0, scalar, in1, op0, op1, rd, wr):
        P.op(eng, lambda e: e.scalar_tensor_tensor(out, in0, scalar, in1, op0, op1), reads=rd, writes=wr)

    def CP(eng, out, in_, rd, wr):
        if eng == "act":
            P.op("act", lambda e: e.activation(out, in_, AF.Copy), reads=rd, writes=wr)
        else:
            P.op(eng, lambda e: e.tensor_copy(out, in_), reads=rd, writes=wr)

    def RECIP(out, in_, rd, wr):
        P.op("dve", lambda e: e.reciprocal(out, in_), reads=rd, writes=wr)

    def MEMSET(eng, ap, val, wr):
        P.op(eng, lambda e: e.memset(ap, val), writes=wr)

    def DMA(q, out, in_, rd, wr):
        P.op(q, lambda e: e.dma_start(out=out, in_=in_), reads=rd, writes=wr, is_dma=True)

    banks = []
    for i in range(8):
        t = P.stack.enter_context(nc.psum_tensor(f"bank{i}", [128, 512], F32))
        banks.append((t, Res(f"bank{i}")))
    bstate = {"i": 0}

    def bank():
        b = banks[bstate["i"] % 6]
        bstate["i"] += 1
        return b

    ostate = {"i": 0}

    def obank():
        b = banks[6 + ostate["i"] % 2]
        ostate["i"] += 1
        return b

    onesf = sb("onesf", [128, 128], F32)
    onesb = sb("onesb", [128, 128], BF16)
    identb = sb("identb", [128, 128], BF16)
    cscb = sb("cscb", [128, 256], BF16)
    dftp = sb("dftp", [128, 2, 2, 256], BF16)
    rmat = sb("rmat", [96, 96], BF16)
    sel = sb("sel", [65, 64], F32)
    epsc = sb("epsc", [128, 1], F32)
    vec = sb("vecs", [128, DEPTH, NV], F32)
    modv = sb("modv", [128, DEPTH, 2, 48], F32)
    cvt = sb("cvt", [128, 8, 2], F32)
    scv = sb("scv", [128, 8, 2], BF16)
    adabt = sb("adabt", [128, DEPTH, 48], F32)
    RC = R("consts")
    DMA("sp", onesf[:], onesfd, [], [RC])
    DMA("pool", onesb[:], onesfd, [], [RC])
    DMA("pool", identb[:], identd, [], [RC])
    DMA("pool", cscb[:], cscd, [], [RC])
    DMA("sp", dftp[:], dftpd, [], [RC])
    DMA("pool", rmat[:], rmatd, [], [RC])
    DMA("sp", sel[:], seld, [], [RC])
    DMA("sp", cvt[:], cvd, [], [RC])
    for l in range(DEPTH):
        DMA("sp", vec[:, l, :], vecd[l], [], [RC])
        DMA("sp", adabt[:, l, :], adab[l], [], [RC])
    MEMSET("dve", epsc[:], EPS, [RC])

    def V(l, name, j=0, n=1, rows=128):
        o = VOFF[name] + j
        return vec[0:rows, l, o:o + n]

    arena = sb("arena", [128, ARENA_BYTES // 4], F32)
    ast = {"off": 0, "max": 0}

    def alloc(nbytes_pp, dt, shape_free, parts=128):
        off = ast["off"]
        nb = (nbytes_pp + 31) // 32 * 32
        assert off + nb <= ARENA_BYTES, f"arena overflow {off + nb}"
        ast["off"] = off + nb
        ast["max"] = max(ast["max"], ast["off"])
        ap = arena[0:parts, off // 4:(off + nb) // 4]
        esz = 4 if dt == F32 else 2
        nel = 1
        for s in shape_free:
            nel *= s
        assert nel * esz <= nb
        if dt != F32:
            ap = ap.bitcast(dt)
        ap = ap[:, 0:nel]
        if len(shape_free) == 2:
            ap = ap.rearrange("p (a b) -> p a b", a=shape_free[0])
        elif len(shape_free) == 3:
            ap = ap.rearrange("p (a b c) -> p a b c", a=shape_free[0], b=shape_free[1])
        elif len(shape_free) == 4:
            ap = ap.rearrange("p (a b c d) -> p a b c d", a=shape_free[0], b=shape_free[1], c=shape_free[2])
        return ap

    def alloc_at(off, nbytes_pp, dt, shape_free, parts=128):
        save = ast["off"]
        ast["off"] = off
        ap = alloc(nbytes_pp, dt, shape_free, parts)
        ast["off"] = save
        return ap

    def mark():
        return ast["off"]

    def release(m):
        ast["off"] = m

    ACT(scv[:], cvt[:], AF.Silu, [RC], [R("scv")])
    m0 = mark()
    wb = [alloc(8 * 1024 * 2, BF16, [8, 1024]) for _ in range(2)]
    for l in range(DEPTH):
        bk, br_ = bank()
        for v in range(6):
            w = wb[(l * 6 + v) % 2]
            rw = R("adawbuf", (l * 6 + v) % 2)
            DMA("pool", w, adaw[l, :, :, v * 1024:(v + 1) * 1024], [], [rw])
            for j in range(8):
                col = v * 8 + j
                for kc in range(8):
                    MM(bk[:, 2 * col:2 * col + 2], w[:, kc, j * 128:(j + 1) * 128], scv[:, kc, :],
                       kc == 0, kc == 7, [rw, R("scv")], [br_])
        bk3 = bk[:, 0:96].rearrange("p (a b) -> p a b", b=2)
        for ps in range(2):
            TT("dve", modv[:, l, ps, :], bk3[:, :, ps], adabt[:, l, :], ALU.add, [br_, RC], [R("modraw", l, ps)])
    MOD = sb("MOD", [128, DEPTH, 2, 48], F32)
    for l in range(DEPTH):
        for ps in range(2):
            rr = [R("modraw", l, ps), RC]
            wr = [R("MOD")]
            mv = modv[:, l, ps, :]
            STT("dve", MOD[:, l, ps, 0:8], mv[:, 8:16], 1.0, V(l, "n1g", 0, 8), ALU.add, ALU.mult, rr, wr)
            CP("dve", MOD[:, l, ps, 8:16], mv[:, 0:8], rr, wr)
            CP("dve", MOD[:, l, ps, 16:24], mv[:, 16:24], rr, wr)
            STT("dve", MOD[:, l, ps, 24:32], mv[:, 32:40], 1.0, V(l, "n2g", 0, 8), ALU.add, ALU.mult, rr, wr)
            CP("dve", MOD[:, l, ps, 32:40], mv[:, 24:32], rr, wr)
            CP("dve", MOD[:, l, ps, 40:48], mv[:, 40:48], rr, wr)
    P.barrier()
    release(m0)
    RMOD = R("MOD")

    def run_pass(ps, T, nseq, L, xin, yout, sample):
        nkctx = 512 if sample else 0
        Lk = L + nkctx
        TN = 512
        tiles = []
        for tg in range(0, T, TN):
            pcs = []
            off = 0
            while off < TN:
                s_, p0 = divmod(tg + off, L)
                ln_ = min(L - p0, TN - off)
                pcs.append((s_, p0, ln_, off))
                off += ln_
            tiles.append((tg, TN, pcs))
        HTW = nseq * (L + 2)
        UW = nseq * (L + 30)
        KW = nseq * Lk
        release(0)
        HT = alloc(8 * HTW * 2, BF16, [8, HTW])
        ZY = alloc(4 * T * 2, BF16, [4, T])
        YC = alloc(4 * T * 2, BF16, [4, T])
        OT2 = alloc(4 * T * 2, BF16, [4, T])
        m_long = mark()
        MEMSET("pool", HT, 0.0, [R("HTall")])
        P.barrier()

        def rHT(ti):
            return R("HT", ps, ti)

        def norm_to_HT(l, xt, rxt, ti, n, dst, aoff, tmpn):
            sq, rs, tmp = tmpn
            rsq, rrs, rtmp = R("sq"), R("rs"), R("ntmp")
            ACT(sq[:, :, 0:n], xt[:, :, 0:n], AF.Square, [rxt], [rsq])
            bk, rb = bank()
            for kc in range(8):
                MM(bk[:, 0:n], onesb[:], sq[:, kc, 0:n], kc == 0, kc == 7, [rsq, RC], [rb])
            ACT(rs[:, 0:n], bk[:, 0:n], AF.Ln, [rb, RC], [rrs], bias=epsc[:, 0:1], scale=1.0 / 1024)
            ACT(rs[:, 0:n], rs[:, 0:n], AF.Exp, [rrs], [rrs], scale=-0.5)
            nsl = tmp.shape[1]
            for kc in range(8):
                sl = kc % nsl
                rtmp = R("ntmp", sl)
                TT("dve", tmp[:, sl, 0:n], xt[:, kc, 0:n], rs[:, 0:n], ALU.mult, [rxt, rrs], [rtmp])
                for (off, ln_, hc) in dst:
                    ACT(HT[:, kc, hc:hc + ln_], tmp[:, sl, off:off + ln_], AF.Identity, [rtmp, RMOD], [rHT(ti), R("HTall")],
                        bias=MOD[:, l, ps, aoff + 8 + kc:aoff + 9 + kc], scale=MOD[:, l, ps, aoff + kc:aoff + kc + 1])

        for l in range(depth):
            xsrc = xin if l == 0 else yout
            release(m_long)
            QN = alloc(3 * T * 2, BF16, [3, T])
            CKV = alloc(2 * KW * 2, BF16, [2, KW])
            KR96 = alloc(KW * 4, F32, [KW], parts=96)
            m_u = mark()
            U = alloc(4 * UW * 2, BF16, [4, UW])
            m_s1 = mark()
            MEMSET("pool", U, 0.0, [R("U")])
            xts = [alloc(8 * TN * 4, F32, [8, TN]) for _ in range(2)]
            sq = alloc(8 * TN * 2, BF16, [8, TN])
            rs = alloc(TN * 4, F32, [TN])
            tmp = alloc(8 * TN * 4, F32, [8, TN])
            for ti, (tg, n, pcs) in enumerate(tiles):
                if l > 0:
                    break
                xt = xts[ti % 2]
                rxt = R("xt", ti % 2)
                DMA("sp", xt[:, :, 0:n], xsrc[:, :, tg:tg + n], [R("Y", ps, ti)], [rxt])
                norm_to_HT(l, xt, rxt, ti, n, [(0, n, tg)], 0, (sq, rs, tmp))
            if l == 0:
                P.barrier()
            release(m_s1)
            WIN = alloc(8 * WIN_COLS * 2, BF16, [8, WIN_COLS])
            wpieces = [(0, 512), (512, 1024), (1024, 1536), (1536, WIN_COLS)]
            for pi_, (a_, b_) in enumerate(wpieces):
                DMA("pool", WIN[:, :, a_:b_], wind[l, :, :, a_:b_], [], [R("WIN", pi_)])
            qf = alloc(3 * TN * 4, F32, [3, TN])
            kvf = alloc(2 * TN * 4, F32, [2, TN])
            sqq = alloc(3 * TN * 2, BF16, [3, TN])
            sqk2 = alloc(2 * TN * 2, BF16, [2, TN])
            rs = alloc(TN * 4, F32, [TN])
            rs2 = alloc(TN * 4, F32, [TN])
            sg = alloc(TN * 4, F32, [TN])
            if sample:
                DMA("pool", CKV[:, :, 0:512], cckd[l], [], [R("CKV")])
                DMA("sp", KR96[64:96, 0:512], ckrd[l], [], [R("KR96")])
            for ti, (tg, n, pcs) in enumerate(tiles):
                rhp = [[rHT(ti), R("WIN", pi_)] for pi_ in range(4)]
                hsl = lambda kc: HT[:, kc, tg:tg + n]
                kcol = nkctx + tg
                for c in range(4):
                    bk, rb = bank()
                    for kc in range(8):
                        MM(bk[:, 0:n], WIN[:, kc, c * 128:(c + 1) * 128], hsl(kc), kc == 0, kc == 7, rhp[0], [rb])
                    CP("act", ZY[:, c, tg:tg + n], bk[:, 0:n], [rb], [R("ZY", ps)])
                for c in range(4):
                    ba, rba = bank()
                    bg, rbg = bank()
                    for kc in range(8):
                        MM(ba[:, 0:n], WIN[:, kc, 512 + c * 128:512 + (c + 1) * 128], hsl(kc), kc == 0, kc == 7, rhp[1], [rba])
                    for kc in range(8):
                        MM(bg[:, 0:n], WIN[:, kc, 1024 + c * 128:1024 + (c + 1) * 128], hsl(kc), kc == 0, kc == 7, rhp[2], [rbg])
                    ACT(sg[:, 0:n], bg[:, 0:n], AF.Sigmoid, [rbg], [R("sg")])
                    for (s_, p0, ln_, off) in pcs:
                        ucol = s_ * (L + 30) + 15 + p0
                        TT("dve", U[:, c, ucol:ucol + ln_], ba[:, off:off + ln_], sg[:, off:off + ln_], ALU.mult,
                           [rba, R("sg")], [R("U")])
                for c in range(3):
                    bk, rb = bank()
                    for kc in range(8):
                        MM(bk[:, 0:n], WIN[:, kc, 1536 + c * 128:1536 + (c + 1) * 128], hsl(kc), kc == 0, kc == 7, rhp[3], [rb])
                    CP("dve", qf[:, c, 0:n], bk[:, 0:n], [rb], [R("qf")])
                ACT(sqq[:, :, 0:n], qf[:, :, 0:n], AF.Square, [R("qf")], [R("sqq")])
                for c in range(2):
                    bk, rb = bank()
                    for kc in range(8):
                        MM(bk[:, 0:n], WIN[:, kc, 1920 + c * 128:1920 + (c + 1) * 128], hsl(kc), kc == 0, kc == 7, rhp[3], [rb])
                    CP("dve", kvf[:, c, 0:n], bk[:, 0:n], [rb], [R("kvf")])
                ACT(sqk2[:, :, 0:n], kvf[:, :, 0:n], AF.Square, [R("kvf")], [R("sqk2")])
                bk, rb = bank()
                for kc in range(8):
                    MM(bk[0:96, 0:n], WIN[:, kc, 2176:2272], hsl(kc), kc == 0, kc == 7, rhp[3], [rb])
                CP("dve", KR96[64:96, kcol:kcol + n], bk[64:96, 0:n], [rb], [R("KR96")])
                if not sample:
                    DMA("sp", okr[l, :, tg:tg + n], KR96[64:96, kcol:kcol + n], [R("KR96")], [R("okr")])
                bk, rb = bank()
                for c in range(3):
                    MM(bk[:, 0:n], onesb[:], sqq[:, c, 0:n], c == 0, c == 2, [R("sqq"), RC], [rb])
                ACT(rs[:, 0:n], bk[:, 0:n], AF.Ln, [rb, RC], [R("rs1")], bias=epsc[:, 0:1], scale=1.0 / 384)
                ACT(rs[:, 0:n], rs[:, 0:n], AF.Exp, [R("rs1")], [R("rs1")], scale=-0.5)
                bk, rb = bank()
                for c in range(2):
                    MM(bk[:, 0:n], onesb[:], sqk2[:, c, 0:n], c == 0, c == 1, [R("sqk2"), RC], [rb])
                ACT(rs2[:, 0:n], bk[:, 0:n], AF.Ln, [rb, RC], [R("rs2")], bias=epsc[:, 0:1], scale=1.0 / 256)
                ACT(rs2[:, 0:n], rs2[:, 0:n], AF.Exp, [R("rs2")], [R("rs2")], scale=-0.5)
                for c in range(3):
                    STT("dve", QN[:, c, tg:tg + n], qf[:, c, 0:n], V(l, "qng", c), rs[:, 0:n], ALU.mult, ALU.mult,
                        [R("qf"), R("rs1"), RC], [R("QN")])
                for c in range(2):
                    STT("dve", kvf[:, c, 0:n], kvf[:, c, 0:n], V(l, "kvg", c), rs2[:, 0:n], ALU.mult, ALU.mult,
                        [R("kvf"), R("rs2"), RC], [R("kvf")])
                CP("act", CKV[:, :, kcol:kcol + n], kvf[:, :, 0:n], [R("kvf")], [R("CKV")])
                if not sample:
                    DMA("sp", ockv[l, :, :, tg:tg + n], kvf[:, :, 0:n], [R("kvf")], [R("ockv")])
            P.barrier()
            release(m_s1)
            diag = alloc(124 * 128 * 2, BF16, [124, 128])
            cv = alloc(4 * 512 * 4, F32, [4, 512])
            sqf = alloc(4 * 512 * 4, F32, [4, 512])
            mu = alloc(512 * 4, F32, [512])
            var = alloc(512 * 4, F32, [512])
            tcv = alloc(512 * 4, F32, [512])
            diag4 = diag.rearrange("p (k c) m -> p k c m", c=4)
            cw0 = VOFF["cw"]
            cw4 = vec[:, l, cw0:cw0 + 124].rearrange("p (k c) -> p k c", c=4)
            for cc in range(4):
                TT("dve", diag4[:, :, cc, :], identb[:].unsqueeze(1).broadcast_to([128, 31, 128]),
                   cw4[:, :, cc].unsqueeze(2).broadcast_to([128, 31, 128]), ALU.mult, [RC], [R("diag", cc)])
            for s in range(nseq):
                ub = s * (L + 30)
                for l0 in range(0, L, 512):
                    ln = min(512, L - l0)
                    tq = s * L + l0
                    for cc in range(4):
                        bk, rb = bank()
                        for k in range(31):
                            MM(bk[:, 0:ln], diag[:, k * 4 + cc, :], U[:, cc, ub + l0 + k:ub + l0 + k + ln], k == 0, k == 30,
                               [R("diag", cc), R("U")], [rb])
                        ACT(cv[:, cc, 0:ln], bk[:, 0:ln], AF.Identity, [rb, RC], [R("cv")], bias=V(l, "cb", cc), scale=1.0)
                    ACT(sqf[:, :, 0:ln], cv[:, :, 0:ln], AF.Square, [R("cv")], [R("sqf")])
                    bm, rbm = bank()
                    bq, rbq = bank()
                    for cc in range(4):
                        MM(bm[:, 0:ln], onesf[:], cv[:, cc, 0:ln], cc == 0, cc == 3, [R("cv"), RC], [rbm])
                    for cc in range(4):
                        MM(bq[:, 0:ln], onesf[:], sqf[:, cc, 0:ln], cc == 0, cc == 3, [R("sqf"), RC], [rbq])
                    TS("dve", mu[:, 0:ln], bm[:, 0:ln], 1.0 / 512, None, ALU.mult, None, [rbm], [R("mu")])
                    TT("dve", var[:, 0:ln], mu[:, 0:ln], mu[:, 0:ln], ALU.mult, [R("mu")], [R("var")])
                    STT("dve", var[:, 0:ln], bq[:, 0:ln], 1.0 / 512, var[:, 0:ln], ALU.mult, ALU.subtract,
                        [rbq, R("var")], [R("var")])
                    ACT(var[:, 0:ln], var[:, 0:ln], AF.Ln, [R("var"), RC], [R("var")], bias=epsc[:, 0:1], scale=1.0)
                    ACT(var[:, 0:ln], var[:, 0:ln], AF.Exp, [R("var")], [R("var")], scale=-0.5)
                    for cc in range(4):
                        TT("dve", tcv[:, 0:ln], cv[:, cc, 0:ln], mu[:, 0:ln], ALU.subtract, [R("cv"), R("mu")], [R("tcv")])
                        TT("dve", tcv[:, 0:ln], tcv[:, 0:ln], var[:, 0:ln], ALU.mult, [R("tcv"), R("var")], [R("tcv")])
                        ACT(YC[:, cc, tq:tq + ln], tcv[:, 0:ln], AF.Silu, [R("tcv"), RC], [R("YC")],
                            bias=V(l, "lnb", cc), scale=V(l, "lng", cc))
            P.barrier()
            release(m_u)
            wq = alloc(3 * 768 * 2, BF16, [3, 768])
            wkn = alloc(2 * 512 * 2, BF16, [2, 512])
            wkv = alloc(2 * 512 * 2, BF16, [2, 512])
            rAW = R("attw")
            DMA("pool", wq, wqd[l], [], [rAW])
            DMA("pool", wkn, wknd[l], [], [rAW])
            DMA("pool", wkv, wkvd[l], [], [rAW])
            nkb = Lk // 128
            KTs = [alloc(Lk * 2, BF16, [Lk], parts=96) for _ in range(2)]
            QTs = [alloc(512 * 2, BF16, [512], parts=96) for _ in range(2)]
            VTs = [alloc(nkb * 65 * 2, BF16, [nkb, 65]) for _ in range(2)]
            PTs = [alloc(512 * 2, BF16, [512]) for _ in range(4)]
            tset = {}
            for nm in ("K", "Q"):
                tset[nm] = dict(
                    kf=alloc(512 * 4, F32, [512], parts=96), kg=alloc(512 * 2, BF16, [512], parts=96),
                    sqk=alloc(512 * 2, BF16, [512], parts=96), rsk=alloc(512 * 4, F32, [512], parts=96),
                    t1=alloc(512 * 4, F32, [512], parts=96), t2=alloc(512 * 4, F32, [512], parts=96))
            osb = alloc(512 * 4, F32, [512], parts=65)
            rd_ = alloc(512 * 4, F32, [512], parts=64)
            obf = alloc(512 * 2, BF16, [512], parts=64)
            if sample:
                ropec = alloc(2048 * 4, F32, [2048], parts=96)
                ropes = alloc(2048 * 4, F32, [2048], parts=96)
                DMA("sp", ropec, ropecd, [], [R("rope")])
                DMA("sp", ropes, ropesd, [], [R("rope")])
            for i_ in range(2):
                MEMSET("pool", VTs[i_][:, :, 64:65], 1.0, [R("VT", i_)])
            pti = [0]
            deferred = [None]

            def norm_steps(nm, n, gname, pos0, dest, rdest, pre):
                t = tset[nm]
                kf, kg, sqk, rsk, t1, t2 = t["kf"], t["kg"], t["sqk"], t["rsk"], t["t1"], t["t2"]
                rk = lambda x: R(x, nm)
                st = []
                hold = {}

                def s1():
                    pre(kf, rk("kf"))
                    TT("dve", sqk[:, 0:n], kf[:, 0:n], kf[:, 0:n], ALU.mult, [rk("kf")], [rk("sqk")])
                st.append(s1)

                def s2():
                    b2, rb2 = bank()
                    hold["b2"] = (b2, rb2)
                    MM(b2[0:96, 0:n], onesb[0:96, 0:96], sqk[:, 0:n], True, True, [rk("sqk"), RC], [rb2])
                    ACT(rsk[:, 0:n], b2[0:96, 0:n], AF.Ln, [rb2, RC], [rk("rsk")], bias=epsc[0:96, 0:1], scale=1.0 / 96)
                    ACT(rsk[:, 0:n], rsk[:, 0:n], AF.Exp, [rk("rsk")], [rk("rsk")], scale=-0.5)
                    if pos0 is None:
                        STT("dve", dest, kf[:, 0:n], V(l, gname, 0, 1, 96), rsk[:, 0:n], ALU.mult, ALU.mult,
                            [rk("kf"), rk("rsk"), RC], [rdest])
                    else:
                        STT("dve", kg[:, 0:n], kf[:, 0:n], V(l, gname, 0, 1, 96), rsk[:, 0:n], ALU.mult, ALU.mult,
                            [rk("kf"), rk("rsk"), RC], [rk("kg")])
                        TT("dve", t1[:, 0:n], kg[:, 0:n], ropec[:, pos0:pos0 + n], ALU.mult, [rk("kg"), R("rope")], [rk("t1")])
                st.append(s2)
                if pos0 is not None:
                    def s3():
                        b3, rb3 = bank()
                        MM(b3[0:96, 0:n], rmat[:], kg[:, 0:n], True, True, [rk("kg"), RC], [rb3])
                        TT("dve", t2[:, 0:n], b3[0:96, 0:n], ropes[:, pos0:pos0 + n], ALU.mult, [rb3, R("rope")], [rk("t2")])
                        TT("dve", dest, t1[:, 0:n], t2[:, 0:n], ALU.add, [rk("t1"), rk("t2")], [rdest])
                    st.append(s3)
                return st

            def k_steps(s, h, bi):
                KT, VT = KTs[bi], VTs[bi]
                rKT, rVT = R("KT", bi), R("VT", bi)
                kc0 = s * Lk
                st = []
                for k0 in range(0, Lk, 512):
                    kn = min(512, Lk - k0)

                    def pre(kf, rkf, k0=k0, kn=kn):
                        bk, rb = bank()
                        for c in range(2):
                            MM(bk[0:64, 0:kn], wkn[:, c, h * 64:(h + 1) * 64], CKV[:, c, kc0 + k0:kc0 + k0 + kn],
                               c == 0, c == 1, [rAW, R("CKV")], [rb])
                        CP("dve", kf[0:64, 0:kn], bk[0:64, 0:kn], [rb], [rkf])
                        CP("pool", kf[64:96, 0:kn], KR96[64:96, kc0 + k0:kc0 + k0 + kn], [R("KR96")], [rkf])
                    pos0 = (k0 - nkctx) if (sample and k0 >= nkctx) else None
                    st += norm_steps("K", kn, "qkk", pos0, KT[:, k0:k0 + kn], rKT, pre)
                for kb0 in range(0, nkb, 8):
                    nb = min(8, nkb - kb0)

                    def vstep(kb0=kb0, nb=nb):
                        bk, rb = bank()
                        for i in range(nb):
                            kb = kb0 + i
                            for c in range(2):
                                MM(bk[:, i * 64:(i + 1) * 64], CKV[:, c, kc0 + kb * 128:kc0 + (kb + 1) * 128],
                                   wkv[:, c, h * 64:(h + 1) * 64], c == 0, c == 1, [rAW, R("CKV")], [rb])
                        CP("dve", VT[:, kb0:kb0 + nb, 0:64], bk[:, 0:nb * 64].rearrange("p (a b) -> p a b", b=64),
                           [rb], [rVT])
                    st.append(vstep)
                return st

            def q_steps(s, h, q0, qi_):
                qn = min(512, L - q0)
                tq = s * L + q0
                QT = QTs[qi_ % 2]
                rQT = R("QT", qi_ % 2)

                def pre(kf, rkf):
                    bk, rb = bank()
                    for c in range(3):
                        MM(bk[0:96, 0:qn], wq[:, c, h * 96:(h + 1) * 96], QN[:, c, tq:tq + qn], c == 0, c == 2,
                           [rAW, R("QN")], [rb])
                    CP("dve", kf[:, 0:qn], bk[0:96, 0:qn], [rb], [rkf])
                return norm_steps("Q", qn, "qkq", q0 if sample else None, QT[:, 0:qn], rQT, pre)

            def merge(a, b):
                out = []
                ia = ib = 0
                while ia < len(a) or ib < len(b):
                    if ia < len(a):
                        out.append(a[ia]); ia += 1
                    if ib < len(b):
                        out.append(b[ib]); ib += 1
                return out

            def attend(s, h, q0, qi_, bi, pending):
                qn = min(512, L - q0)
                tq = s * L + q0
                QT = QTs[qi_ % 2]
                rQT = R("QT", qi_ % 2)
                KT, VT = KTs[bi], VTs[bi]
                rKT, rVT = R("KT", bi), R("VT", bi)
                bo, rbo = obank()
                pendq = []
                LA = 3
                per = -(-len(pending) // nkb) if pending else 0
                stride = max(1, nkb // max(1, len(pending)))
                for kb in range(nkb):
                    bs_, rbs = bank()
                    MM(bs_[:, 0:qn], KT[:, kb * 128:(kb + 1) * 128], QT[:, 0:qn], True, True, [rKT, rQT], [rbs])
                    PT = PTs[pti[0] % 4]
                    rPT = R("PT", pti[0] % 4)
                    pti[0] += 1
                    ACT(PT[:, 0:qn], bs_[:, 0:qn], AF.Exp, [rbs], [rPT], scale=1.0 / math.sqrt(96.0))
                    pendq.append((kb, PT, rPT))
                    if len(pendq) > LA:
                        pkb, pPT, prPT = pendq.pop(0)
                        MM(bo[0:65, 0:qn], VT[:, pkb, :], pPT[:, 0:qn], pkb == 0, pkb == nkb - 1, [rVT, prPT], [rbo])
                    if kb == min(2, nkb - 1) and deferred[0] is not None:
                        deferred[0]()
                        deferred[0] = None
                    if kb % stride == 0:
                        for _ in range(per):
                            if pending:
                                pending.pop(0)()
                while pendq:
                    pkb, pPT, prPT = pendq.pop(0)
                    MM(bo[0:65, 0:qn], VT[:, pkb, :], pPT[:, 0:qn], pkb == 0, pkb == nkb - 1, [rVT, prPT], [rbo])
                while pending:
                    pending.pop(0)()
                CP("dve", osb[:, 0:qn], bo[0:65, 0:qn], [rbo], [R("osb")])

                def fin():
                    bd, rbd = bank()
                    MM(bd[0:64, 0:qn], sel[:], osb[:, 0:qn], True, True, [R("osb"), RC], [rbd])
                    ACT(rd_[:, 0:qn], bd[0:64, 0:qn], AF.Ln, [rbd], [R("rd")])
                    ACT(rd_[:, 0:qn], rd_[:, 0:qn], AF.Exp, [R("rd")], [R("rd")], scale=-1.0)
                    if h % 2 == 0:
                        TT("dve", OT2[0:64, h // 2, tq:tq + qn], osb[0:64, 0:qn], rd_[:, 0:qn], ALU.mult,
                           [R("osb"), R("rd")], [R("OT2")])
                    else:
                        TT("dve", obf[:, 0:qn], osb[0:64, 0:qn], rd_[:, 0:qn], ALU.mult, [R("osb"), R("rd")], [R("obf")])
                        DMA("sp", OT2[64:128, h // 2, tq:tq + qn], obf[:, 0:qn], [R("obf")], [R("OT2")])
                deferred[0] = fin

            if sample:
                heads = [(s_, h_) for s_ in range(nseq) for h_ in range(8)]
                qtl = list(range(0, L, 512))
                for f_ in merge(k_steps(heads[0][0], heads[0][1], 0), q_steps(heads[0][0], heads[0][1], qtl[0], 0)):
                    f_()
                qcnt = 0
                for hi, (s_, h_) in enumerate(heads):
                    knext = k_steps(heads[hi + 1][0], heads[hi + 1][1], (hi + 1) % 2) if hi + 1 < len(heads) else []
                    ksh = -(-len(knext) // len(qtl)) if knext else 0
                    for qi, q0 in enumerate(qtl):
                        if qi + 1 < len(qtl):
                            nxt = q_steps(s_, h_, qtl[qi + 1], qcnt + 1)
                        elif hi + 1 < len(heads):
                            nxt = q_steps(heads[hi + 1][0], heads[hi + 1][1], qtl[0], qcnt + 1)
                        else:
                            nxt = []
                        kpart, knext = knext[:ksh], knext[ksh:]
                        attend(s_, h_, q0, qcnt, hi % 2, merge(kpart, nxt))
                        qcnt += 1
                if deferred[0] is not None:
                    deferred[0]()
                    deferred[0] = None
            else:
                KTa = [alloc(8 * 256 * 2, BF16, [8, 256], parts=96) for _ in range(2)]
                QTa = [alloc(8 * 256 * 2, BF16, [8, 256], parts=96) for _ in range(2)]
                VTa = [alloc(2 * 8 * 65 * 2, BF16, [2, 8, 65]) for _ in range(2)]
                for i_ in range(2):
                    MEMSET("pool", VTa[i_][:, :, :, 64:65], 1.0, [R("VTa", i_)])
                bset = {}
                for nm in ("K", "Q"):
                    bset[nm] = dict(kf=alloc(8 * 256 * 4, F32, [8, 256], parts=96),
                                    sq=alloc(8 * 256 * 2, BF16, [8, 256], parts=96),
                                    rs=alloc(8 * 256 * 4, F32, [8, 256], parts=96))
                osb2 = alloc(512 * 4, F32, [512], parts=65)
                rd2 = alloc(512 * 4, F32, [512], parts=64)
                obf2 = alloc(256 * 2, BF16, [256], parts=64)

                def prep_steps(s, bi):
                    c0 = s * L
                    st = []
                    for nm in ("K", "Q"):
                        t = bset[nm]
                        kf, sq, rs = t["kf"], t["sq"], t["rs"]
                        rk = lambda x, nm=nm: R(x + "a", nm)
                        dest = (KTa if nm == "K" else QTa)[bi]
                        rdest = R("KTa" if nm == "K" else "QTa", bi)

                        def s1(nm=nm, kf=kf, sq=sq, rk=rk):
                            for hp in range(4):
                                bk, rb = bank()
                                for i in range(2):
                                    h = 2 * hp + i
                                    if nm == "K":
                                        for c in range(2):
                                            MM(bk[0:64, i * 256:(i + 1) * 256], wkn[:, c, h * 64:(h + 1) * 64],
                                               CKV[:, c, c0:c0 + L], c == 0, c == 1, [rAW, R("CKV")], [rb])
                                    else:
                                        for c in range(3):
                                            MM(bk[0:96, i * 256:(i + 1) * 256], wq[:, c, h * 96:(h + 1) * 96],
                                               QN[:, c, c0:c0 + L], c == 0, c == 2, [rAW, R("QN")], [rb])
                                rows = 64 if nm == "K" else 96
                                CP("dve" if hp % 2 == 0 else "act", kf[0:rows, 2 * hp:2 * hp + 2, :],
                                   bk[0:rows, 0:512].rearrange("p (a b) -> p a b", b=256), [rb], [rk("kf")])
                            if nm == "K":
                                CP("pool", kf[64:96, :, :], KR96[64:96, c0:c0 + L].unsqueeze(1).broadcast_to([32, 8, 256]),
                                   [R("KR96")], [rk("kf")])
                            TT("dve", sq[:, :, :], kf[:, :, :], kf[:, :, :], ALU.mult, [rk("kf")], [rk("sq")])
                        st.append(s1)

                        def s2(nm=nm, kf=kf, sq=sq, rs=rs, rk=rk, dest=dest, rdest=rdest):
                            for hp in range(4):
                                b2, rb2 = bank()
                                MM(b2[0:96, 0:512], onesb[0:96, 0:96], sq[:, 2 * hp:2 * hp + 2, :], True, True, [rk("sq"), RC], [rb2])
                                ACT(rs[:, 2 * hp:2 * hp + 2, :], b2[0:96, 0:512].rearrange("p (a b) -> p a b", b=256), AF.Ln,
                                    [rb2, RC], [rk("rs")], bias=epsc[0:96, 0:1], scale=1.0 / 96)
                            ACT(rs[:, :, :], rs[:, :, :], AF.Exp, [rk("rs")], [rk("rs")], scale=-0.5)
                            STT("dve", dest[:, :, :], kf[:, :, :], V(l, "qkk" if nm == "K" else "qkq", 0, 1, 96), rs[:, :, :],
                                ALU.mult, ALU.mult, [rk("kf"), rk("rs"), RC], [rdest])
                        st.append(s2)

                    def sv():
                        for kb in range(2):
                            bk, rb = bank()
                            for c in range(2):
                                MM(bk[:, 0:512], CKV[:, c, c0 + kb * 128:c0 + (kb + 1) * 128], wkv[:, c, :], c == 0, c == 1,
                                   [rAW, R("CKV")], [rb])
                            CP("act", VTa[bi][:, kb, :, 0:64], bk[:, 0:512].rearrange("p (a b) -> p a b", b=64), [rb], [R("VTa", bi)])
                    return [st[0], st[2], sv, st[1], st[3]]

                def attend_seq(s, bi, pending):
                    c0 = s * L
                    KT_, QT_, VT_ = KTa[bi], QTa[bi], VTa[bi]
                    rKT_, rQT_, rVT_ = R("KTa", bi), R("QTa", bi), R("VTa", bi)
                    pend = None
                    bo = rbo = None
                    for h in range(9):
                        if h < 8:
                            bs_, rbs = bank()
                            for kb in range(2):
                                MM(bs_[:, kb * 256:(kb + 1) * 256], KT_[:, h, kb * 128:(kb + 1) * 128], QT_[:, h, :], True, True,
                                   [rKT_, rQT_], [rbs])
                            PT = PTs[pti[0] % 4]
                            rPT = R("PT", pti[0] % 4)
                            pti[0] += 1
                            ACT(PT[:, 0:512], bs_[:, 0:512], AF.Exp, [rbs], [rPT], scale=1.0 / math.sqrt(96.0))
                        if pend is not None:
                            ph, pPT, prPT = pend
                            if ph % 2 == 0:
                                bo, rbo = obank()
                            for kb in range(2):
                                MM(bo[0:65, (ph % 2) * 256:(ph % 2) * 256 + 256], VT_[:, kb, ph, :], pPT[:, kb * 256:(kb + 1) * 256],
                                   kb == 0, kb == 1, [rVT_, prPT], [rbo])
                            if ph % 2 == 1:
                                if deferred[0] is not None:
                                    deferred[0]()
                                    deferred[0] = None
                                CP("dve", osb2[:, :], bo[0:65, 0:512], [rbo], [R("osb2")])

                                def fin(hp=ph // 2):
                                    bd, rbd = bank()
                                    MM(bd[0:64, 0:512], sel[:], osb2[:, :], True, True, [R("osb2"), RC], [rbd])
                                    ACT(rd2[:, :], bd[0:64, 0:512], AF.Ln, [rbd], [R("rd2")])
                                    ACT(rd2[:, :], rd2[:, :], AF.Exp, [R("rd2")], [R("rd2")], scale=-1.0)
                                    TT("dve", OT2[0:64, hp, c0:c0 + L], osb2[0:64, 0:256], rd2[:, 0:256], ALU.mult,
                                       [R("osb2"), R("rd2")], [R("OT2")])
                                    TT("dve", obf2[:, :], osb2[0:64, 256:512], rd2[:, 256:512], ALU.mult,
                                       [R("osb2"), R("rd2")], [R("obf2")])
                                    DMA("sp", OT2[64:128, hp, c0:c0 + L], obf2[:, :], [R("obf2")], [R("OT2")])
                                deferred[0] = fin
                        pend = (h, PT, rPT) if h < 8 else None
                        if pending and h % 2 == 1:
                            pending.pop(0)()
                    while pending:
                        pending.pop(0)()

                for f_ in prep_steps(0, 0):
                    f_()
                for s_ in range(nseq):
                    nxt = prep_steps(s_ + 1, (s_ + 1) % 2) if s_ + 1 < nseq else []
                    attend_seq(s_, s_ % 2, nxt)
                if deferred[0] is not None:
                    deferred[0]()
                    deferred[0] = None
            P.barrier()
            release(m_long)
            ntb = L // 128
            MW_SLOT = ARENA_BYTES - 9216
            MW0 = alloc_at(MW_SLOT, 4608 * 2, BF16, [4608])
            DMA("pool", MW0, wmgd[l, 0], [], [R("MW", 0)])
            if sample:
                H = L // 2
                Zp = alloc(4 * (H + 1) * 2, BF16, [4, H + 1])
                Zm = alloc(4 * (H + 1) * 2, BF16, [4, H + 1])
                AB = alloc(9 * 4 * 256 * 2, BF16, [9, 4, 256])
                DBs = [alloc(9 * 2 * 512 * 2, BF16, [9, 2, 512]) for _ in range(2)]
                assert mark() <= MW_SLOT
                rZ = R("ZY", ps)
                TT("dve", Zp[:, :, 1:H], ZY[:, :, 1:H], ZY[:, :, L - 1:H:-1], ALU.add, [rZ], [R("Zp")])
                TT("dve", Zm[:, :, 1:H], ZY[:, :, 1:H], ZY[:, :, L - 1:H:-1], ALU.subtract, [rZ], [R("Zm")])
                CP("act", Zp[:, :, 0:1], ZY[:, :, 0:1], [rZ], [R("Zp")])
                CP("act", Zp[:, :, H:H + 1], ZY[:, :, H:H + 1], [rZ], [R("Zp")])
                MEMSET("pool", Zm[:, :, 0:1], 0.0, [R("Zm")])
                for tb in range(8):
                    for gp in range(2):
                        bk, rb = bank()
                        for i in range(2):
                            g = gp * 2 + i
                            MM(bk[:, i * 256:i * 256 + 128], Zp[:, g, tb * 128:(tb + 1) * 128], cscb[:, 0:128], True, True,
                               [R("Zp"), RC], [rb])
                            MM(bk[:, i * 256 + 128:(i + 1) * 256], Zm[:, g, tb * 128:(tb + 1) * 128], cscb[:, 128:256], True, True,
                               [R("Zm"), RC], [rb])
                        CP("act" if (tb + gp) % 2 else "dve", AB[:, tb, gp * 2:gp * 2 + 2, :],
                           bk[:, 0:512].rearrange("p (a b) -> p a b", b=256), [rb], [R("AB")])
                bk, rb = bank()
                for g in range(4):
                    MM(bk[0:1, g * 128:(g + 1) * 128], Zp[:, g, H:H + 1], cscb[:, 0:128], True, True, [R("Zp"), RC], [rb])
                CP("dve", AB[0:1, 8, :, 0:128], bk[0:1, 0:512].rearrange("p (a b) -> p a b", b=128), [rb], [R("AB")])
                for lb in range(L // 512):
                    DB = DBs[lb % 2]
                    rDB = R("DB", lb % 2)
                    DMA("sp", DB, dftsd[lb], [], [rDB])
                    for g in range(4):
                        bk, rb = bank()
                        for tb in range(8):
                            MM(bk[:, 0:512], AB[:, tb, g, 0:128], DB[:, tb, 0, :], tb == 0, False, [R("AB"), rDB], [rb])
                            MM(bk[:, 0:512], AB[:, tb, g, 128:256], DB[:, tb, 1, :], False, False, [R("AB"), rDB], [rb])
                        MM(bk[:, 0:512], AB[0:1, 8, g, 0:128], DB[0:1, 8, 0, :], False, True, [R("AB"), rDB], [rb])
                        CP("act" if g % 2 else "dve", ZY[:, g, lb * 512:(lb + 1) * 512], bk[:, 0:512], [rb], [rZ])
            else:
                AB = alloc(ntb * 4 * 256 * 2, BF16, [ntb, 4, 256])
                if sample:
                    DBs = [alloc(16 * 2 * 512 * 2, BF16, [16, 2, 512]) for _ in range(2)]
                assert mark() <= MW_SLOT
                for s in range(nseq):
                    tb0 = s * L
                    for tb in range(ntb):
                        for gp in range(2):
                            bk, rb = bank()
                            for i in range(2):
                                g = gp * 2 + i
                                MM(bk[:, i * 256:(i + 1) * 256], ZY[:, g, tb0 + tb * 128:tb0 + (tb + 1) * 128], cscb[:], True, True,
                                   [R("ZY", ps), RC], [rb])
                            CP("act" if (tb + gp) % 2 else "dve", AB[:, tb, gp * 2:gp * 2 + 2, :],
                               bk[:, 0:512].rearrange("p (a b) -> p a b", b=256), [rb], [R("AB")])
                    LB = 512 if sample else 256
                    for lb in range(L // LB):
                        if sample:
                            DB = DBs[lb % 2]
                            rDB = R("DB", lb % 2)
                            DMA("sp", DB, dftsd[lb], [], [rDB])
                            dsl = lambda tb, cs: DB[:, tb, cs, :]
                        else:
                            rDB = RC
                            dsl = lambda tb, cs: dftp[:, tb, cs, :]
                        bk, rb = bank()
                        ng = 512 // LB
                        for g in range(4):
                            if g % ng == 0 and g > 0:
                                bk, rb = bank()
                            o0 = (g % ng) * LB
                            for tb in range(ntb):
                                MM(bk[:, o0:o0 + LB], AB[:, tb, g, 0:128], dsl(tb, 0), tb == 0, False, [R("AB"), rDB], [rb])
                                MM(bk[:, o0:o0 + LB], AB[:, tb, g, 128:256], dsl(tb, 1), False, tb == ntb - 1, [R("AB"), rDB], [rb])
                            if g % ng == ng - 1:
                                g0 = g - ng + 1
                                CP("act" if lb % 2 else "dve", ZY[:, g0:g0 + ng, tb0 + lb * LB:tb0 + (lb + 1) * LB],
                                   bk[:, 0:512].rearrange("p (a b) -> p a b", b=LB), [rb], [R("ZY", ps)])
            P.barrier()
            release(m_long)
            MIX = alloc(8 * T * 2, BF16, [8, T])
            WO = alloc(8 * 1024 * 2, BF16, [8, 1024])
            DMA("pool", WO, wod[l], [], [R("WO")])
            m_mix = mark()
            assert mark() + 9216 + 6144 + 4096 <= MW_SLOT
            MWs = [MW0, alloc(4608 * 2, BF16, [4608])]
            G = [alloc(512 * 4, F32, [512]) for _ in range(3)]
            m1 = alloc(512 * 4, F32, [512])
            m2 = alloc(512 * 4, F32, [512])
            for j in range(8):
                MW = MWs[j % 2]
                rMW = R("MW", j % 2)
                if j + 1 < 8:
                    DMA("pool", MWs[(j + 1) % 2], wmgd[l, j + 1], [], [R("MW", (j + 1) % 2)])
                for ti, (tg, n, pcs) in enumerate(tiles):
                    for br in range(3):
                        bk, rb = bank()
                        for kc in range(8):
                            o = (kc * 3 + br) * 128
                            MM(bk[:, 0:n], MW[:, o:o + 128], HT[:, kc, tg:tg + n], kc == 0, kc == 7, [rMW, rHT(ti)], [rb])
                        ACT(G[br][:, 0:n], bk[:, 0:n], AF.Sigmoid, [rb, RC], [R("G", br)], bias=V(l, "bg", br * 8 + j), scale=1.0)
                    ybk = []
                    for bi, src, rsrc in ((0, ZY, R("ZY", ps)), (1, YC, R("YC")), (2, OT2, R("OT2"))):
                        bk, rb = bank()
                        for c in range(4):
                            o = 3072 + bi * 512 + c * 128
                            MM(bk[:, 0:n], MW[:, o:o + 128], src[:, c, tg:tg + n], c == 0, c == 3, [rMW, rsrc], [rb])
                        ybk.append((bk, rb))
                    TT("dve", m1[:, 0:n], ybk[0][0][:, 0:n], G[0][:, 0:n], ALU.mult, [ybk[0][1], R("G", 0)], [R("m1")])
                    TT("dve", m2[:, 0:n], ybk[1][0][:, 0:n], G[1][:, 0:n], ALU.mult, [ybk[1][1], R("G", 1)], [R("m2")])
                    TT("pool", m1[:, 0:n], m1[:, 0:n], m2[:, 0:n], ALU.add, [R("m1"), R("m2")], [R("m1")])
                    TT("dve", m2[:, 0:n], ybk[2][0][:, 0:n], G[2][:, 0:n], ALU.mult, [ybk[2][1], R("G", 2), R("m1")], [R("m2")])
                    TT("pool", MIX[:, j, tg:tg + n], m1[:, 0:n], m2[:, 0:n], ALU.add, [R("m1"), R("m2")], [R("MIX")])
            P.barrier()
            release(m_mix)
            xts = [alloc(8 * TN * 4, F32, [8, TN]) for _ in range(2)]
            sq = alloc(8 * TN * 2, BF16, [8, TN])
            rs = alloc(TN * 4, F32, [TN])
            tmp = alloc(8 * TN * 4, F32, [8, TN])
            for s_ in range(nseq):
                MEMSET("pool", HT[:, :, s_ * (L + 2):s_ * (L + 2) + 1], 0.0, [R("HTall")])
                MEMSET("pool", HT[:, :, s_ * (L + 2) + L + 1:s_ * (L + 2) + L + 2], 0.0, [R("HTall")])
            for ti, (tg, n, pcs) in enumerate(tiles):
                xt = xts[ti % 2]
                rxt = R("xt", ti % 2)
                DMA("sp", xt[:, :, 0:n], xsrc[:, :, tg:tg + n], [R("Y", ps, ti)], [rxt])
                for j in range(8):
                    bk, rb = bank()
                    for kc in range(8):
                        MM(bk[:, 0:n], WO[:, kc, j * 128:(j + 1) * 128], MIX[:, kc, tg:tg + n], kc == 0, kc == 7,
                           [R("WO"), R("MIX")], [rb])
                    STT("dve", xt[:, j, 0:n], bk[:, 0:n], MOD[:, l, ps, 16 + j:17 + j], xt[:, j, 0:n], ALU.mult, ALU.add,
                        [rb, rxt, RMOD], [rxt])
                DMA("sp", yout[:, :, tg:tg + n], xt[:, :, 0:n], [rxt], [R("Y", ps, ti)])
                norm_to_HT(l, xt, rxt, ti, n, [(off, ln_, s_ * (L + 2) + 1 + p0) for (s_, p0, ln_, off) in pcs], 24, (sq, rs, tmp))
            P.barrier()
            ast["off"] = (8 * HTW * 2 + 31) // 32 * 32
            ACTT = alloc(22 * T * 2, BF16, [22, T])
            m_f = mark()
            FUs = [alloc(8 * 256 * 2, BF16, [8, 256]) for _ in range(3)]
            upas = [alloc(HTW * 4, F32, [HTW]) for _ in range(2)]
            upbs = [alloc(HTW * 4, F32, [HTW]) for _ in range(2)]
            Wd = HTW - 2
            ta = alloc(Wd * 4, F32, [Wd])
            tb_ = alloc(Wd * 4, F32, [Wd])
            sa = alloc(Wd * 4, F32, [Wd])
            coltiles = [(c0, min(512, HTW - c0)) for c0 in range(0, HTW, 512)]

            def make_chain(c):
                upa, upb = upas[c % 2], upbs[c % 2]
                rua, rub = R("upa", c % 2), R("upb", c % 2)
                cb = 22 + c
                st = []
                st.append(lambda: ACT(ta[:], upa[:, 1:1 + Wd], AF.Identity, [rua, RC], [R("ta")], bias=V(l, "fb", c), scale=V(l, "fw", 44 + c)))
                st.append(lambda: ACT(tb_[:], upb[:, 1:1 + Wd], AF.Identity, [rub, RC], [R("tb")], bias=V(l, "fb", cb), scale=V(l, "fw", 44 + cb)))
                st.append(lambda: STT("dve", ta[:], upa[:, 0:Wd], V(l, "fw", c), ta[:], ALU.mult, ALU.add, [rua, R("ta"), RC], [R("ta")]))
                st.append(lambda: STT("dve", ta[:], upa[:, 2:2 + Wd], V(l, "fw", 88 + c), ta[:], ALU.mult, ALU.add, [rua, R("ta"), RC], [R("ta")]))
                st.append(lambda: ACT(sa[:], ta[:], AF.Silu, [R("ta")], [R("sa")]))
                st.append(lambda: STT("dve", tb_[:], upb[:, 0:Wd], V(l, "fw", cb), tb_[:], ALU.mult, ALU.add, [rub, R("tb"), RC], [R("tb")]))
                st.append(lambda: STT("dve", tb_[:], upb[:, 2:2 + Wd], V(l, "fw", 88 + cb), tb_[:], ALU.mult, ALU.add, [rub, R("tb"), RC], [R("tb")]))

                def mults():
                    for s_ in range(nseq):
                        j0 = s_ * (L + 2)
                        TT("pool", ACTT[:, c, s_ * L:(s_ + 1) * L], sa[:, j0:j0 + L], tb_[:, j0:j0 + L], ALU.mult,
                           [R("sa"), R("tb")], [R("ACTT")])
                st.append(mults)
                return st

            steps = []
            for c in range(22):
                FU = FUs[c % 3]
                rFU = R("FU", c % 3)
                upa, upb = upas[c % 2], upbs[c % 2]
                rua, rub = R("upa", c % 2), R("upb", c % 2)
                if c == 0:
                    DMA("pool", FU, fud[l, 0], [], [rFU])
                    DMA("pool", FUs[1], fud[l, 1], [], [R("FU", 1)])
                if c + 2 < 22:
                    DMA("pool", FUs[(c + 2) % 3], fud[l, c + 2], [], [R("FU", (c + 2) % 3)])
                per = -(-len(steps) // len(coltiles)) if steps else 0
                for (c0, cn) in coltiles:
                    ba, rba = bank()
                    bb, rbb = bank()
                    for kc in range(8):
                        MM(ba[:, 0:cn], FU[:, kc, 0:128], HT[:, kc, c0:c0 + cn], kc == 0, kc == 7, [rFU, R("HTall")], [rba])
                    for kc in range(8):
                        MM(bb[:, 0:cn], FU[:, kc, 128:256], HT[:, kc, c0:c0 + cn], kc == 0, kc == 7, [rFU, R("HTall")], [rbb])
                    CP("act", upa[:, c0:c0 + cn], ba[:, 0:cn], [rba], [rua])
                    CP("dve", upb[:, c0:c0 + cn], bb[:, 0:cn], [rbb], [rub])
                    for _ in range(per):
                        if steps:
                            steps.pop(0)()
                while steps:
                    steps.pop(0)()
                steps = make_chain(c)
            while steps:
                steps.pop(0)()
            P.barrier()
            release(m_f)
            FDs = [alloc(22 * 128 * 2, BF16, [22, 128]) for _ in range(3)]
            xts = [alloc(8 * TN * 4, F32, [8, TN]) for _ in range(2)]
            sq = alloc(8 * TN * 2, BF16, [8, TN])
            rs = alloc(TN * 4, F32, [TN])
            tmp = alloc(2 * TN * 4, F32, [2, TN])
            fcnt = 0
            til = list(enumerate(tiles))
            nfd = 8 * ((len(til) + 1) // 2)
            for p0_ in range(0, len(til), 2):
                pr = til[p0_:p0_ + 2]
                for k_, (ti, (tg, n, pcs)) in enumerate(pr):
                    DMA("sp", xts[k_][:, :, 0:n], yout[:, :, tg:tg + n], [R("Y", ps, ti)], [R("xt", k_)])
                for j in range(8):
                    FD = FDs[fcnt % 3]
                    rFD = R("FD", fcnt % 3)
                    if fcnt == 0:
                        DMA("pool", FD, fdd[l, 0], [], [rFD])
                        DMA("pool", FDs[1], fdd[l, 1], [], [R("FD", 1)])
                    if fcnt + 2 < nfd:
                        DMA("pool", FDs[(fcnt + 2) % 3], fdd[l, (fcnt + 2) % 8], [], [R("FD", (fcnt + 2) % 3)])
                    fcnt += 1
                    for k_, (ti, (tg, n, pcs)) in enumerate(pr):
                        bk, rb = bank()
                        for c in range(22):
                            MM(bk[:, 0:n], FD[:, c, :], ACTT[:, c, tg:tg + n], c == 0, c == 21, [rFD, R("ACTT")], [rb])
                        STT("dve", xts[k_][:, j, 0:n], bk[:, 0:n], MOD[:, l, ps, 40 + j:41 + j], xts[k_][:, j, 0:n],
                            ALU.mult, ALU.add, [rb, R("xt", k_), RMOD], [R("xt", k_)])
                for k_, (ti, (tg, n, pcs)) in enumerate(pr):
                    DMA("sp", yout[:, :, tg:tg + n], xts[k_][:, :, 0:n], [R("xt", k_)], [R("Y", ps, ti)])
                    if l + 1 < depth:
                        norm_to_HT(l + 1, xts[k_], R("xt", k_), ti, n, [(0, n, tg)], 0, (sq, rs, tmp))
            P.barrier()

    if 0 in passes:
        run_pass(0, 1024, 4, 256, xp, yp, False)
    if 1 in passes:
        run_pass(1, 2048, 1, 2048, xs, ys, True)
    P.counts = {e: len(P.ops[e]) for e in ENGS}
    P.nwaits = {e: sum(len(o.waits) for o in P.ops[e]) for e in ENGS}
    build_program.stats = (P.counts, P.nwaits)
    build_program.P = P
    P.emit()
    P.close()
    return nc, ast["max"]


def _km(W, kc):
    K, N = W.shape
    return np.ascontiguousarray(W.reshape(kc, 128, N).transpose(1, 0, 2))


def _cols(v, n):
    return np.ascontiguousarray(v.reshape(n, 128).T)


def _host_consts():
    c = {}
    c["onesf"] = np.ones((128, 128), np.float32)
    c["ident"] = np.eye(128, dtype=np.float32)
    k = np.arange(128)
    ang = 2 * np.pi * np.outer(k, k) / 128.0
    c["csc"] = np.concatenate([np.cos(ang), -np.sin(ang)], axis=1).astype(np.float32) / np.sqrt(128.0)

    def dft(L):
        m = np.arange(L, dtype=np.float64)
        a = 2 * np.pi * (np.outer(m, m) % L) / L
        return np.cos(a) / np.sqrt(L), np.sin(a) / np.sqrt(L)

    C, S = dft(256)
    dp = np.stack([C.reshape(2, 128, 256), S.reshape(2, 128, 256)], axis=2)
    c["dftp"] = np.ascontiguousarray(dp.transpose(1, 0, 2, 3)).astype(ml_dtypes.bfloat16)
    C, S = dft(2048)
    ds = np.zeros((4, 128, 9, 2, 512), np.float64)
    for tb in range(8):
        ds[:, :, tb, 0, :] = C[tb * 128:(tb + 1) * 128, :].reshape(128, 4, 512).transpose(1, 0, 2)
        ds[:, :, tb, 1, :] = S[tb * 128:(tb + 1) * 128, :].reshape(128, 4, 512).transpose(1, 0, 2)
    ds[:, 0, 8, 0, :] = C[1024, :].reshape(4, 512)
    c["dfts"] = ds.astype(ml_dtypes.bfloat16)
    Ls = 2048
    pos = np.arange(Ls)
    row = (pos // 64).astype(np.float32)
    col = (pos % 64).astype(np.float32)
    half = 16
    inv = (10000.0 ** (-np.arange(0, half, 2, dtype=np.float32) / half)).astype(np.float32)
    rc = np.ones((96, Ls), np.float32)
    rsn = np.zeros((96, Ls), np.float32)
    for axis, pv in enumerate((row, col)):
        a = (pv[None, :] * inv[:, None]).astype(np.float32)
        for hf in range(2):
            r0 = 64 + axis * 16 + hf * 8
            rc[r0:r0 + 8] = np.cos(a)
            rsn[r0:r0 + 8] = np.sin(a)
    c["ropec"] = rc
    c["ropes"] = rsn
    rm = np.zeros((96, 96), np.float32)
    for axis in range(2):
        for f in range(8):
            r1 = 64 + axis * 16 + f
            r2 = r1 + 8
            rm[r2, r1] = -1.0
            rm[r1, r2] = 1.0
    c["rmat"] = rm
    sl = np.zeros((65, 64), np.float32)
    sl[64, :] = 1.0
    c["sel"] = sl
    return c


_CACHE = {}


def kernel(x_prompt, x_sample, cache_ckv, cache_krope, c, c_ctx, ada_w, ada_b, norm1_g, norm2_g,
           w_in, w_gate, b_gate, w_fourier, conv_dw, conv_dw_b, conv_ln_g, conv_ln_b, w_conv_out,
           q_norm_g, w_q_up, kv_norm_g, w_kv_up, qk_q_g, qk_k_g, w_mla_out, w_out,
           ffn_up, ffn_dw, ffn_dw_b, ffn_down):
    f = lambda a: np.asarray(a, dtype=np.float32)
    x_prompt, x_sample, cache_ckv, cache_krope, c, c_ctx = map(f, (x_prompt, x_sample, cache_ckv, cache_krope, c, c_ctx))
    ada_w, ada_b, norm1_g, norm2_g, w_in, w_gate, b_gate = map(f, (ada_w, ada_b, norm1_g, norm2_g, w_in, w_gate, b_gate))
    w_fourier, conv_dw, conv_dw_b, conv_ln_g, conv_ln_b, w_conv_out = map(f, (w_fourier, conv_dw, conv_dw_b, conv_ln_g, conv_ln_b, w_conv_out))
    q_norm_g, w_q_up, kv_norm_g, w_kv_up, qk_q_g, qk_k_g, w_mla_out, w_out = map(f, (q_norm_g, w_q_up, kv_norm_g, w_kv_up, qk_q_g, qk_k_g, w_mla_out, w_out))
    ffn_up, ffn_dw, ffn_dw_b, ffn_down = map(f, (ffn_up, ffn_dw, ffn_dw_b, ffn_down))
    Ld = DEPTH
    if "nc" not in _CACHE:
        _CACHE["nc"] = build_program()[0]
        _CACHE["consts"] = _host_consts()
    nc = _CACHE["nc"]
    consts = _CACHE["consts"]

    sh = dict(consts)
    sh["adaw"] = np.stack([_km(ada_w[l], 8) for l in range(Ld)])
    sh["adab"] = np.stack([_cols(ada_b[l], 48) for l in range(Ld)])
    vecs = np.zeros((Ld, 128, NV), np.float32)
    for l in range(Ld):
        def put(name, arr):
            vecs[l, :arr.shape[0], VOFF[name]:VOFF[name] + arr.shape[1]] = arr
        put("n1g", _cols(norm1_g[l], 8))
        put("n2g", _cols(norm2_g[l], 8))
        put("bg", _cols(b_gate[l], 24))
        put("cw", _cols(conv_dw[l].reshape(-1), 124))
        put("cb", _cols(conv_dw_b[l], 4))
        put("lng", _cols(conv_ln_g[l], 4))
        put("lnb", _cols(conv_ln_b[l], 4))
        put("qng", _cols(q_norm_g[l], 3))
        put("kvg", _cols(kv_norm_g[l], 2))
        put("fw", _cols(ffn_dw[l].reshape(-1), 132))
        put("fb", _cols(ffn_dw_b[l], 44))
        put("qkq", qk_q_g[l].reshape(96, 1))
        put("qkk", qk_k_g[l].reshape(96, 1))
    sh["vec"] = vecs
    win = np.zeros((Ld, 128, 8, WIN_COLS), np.float32)
    for l in range(Ld):
        wk = _km(w_in[l], 8)
        win[l, :, :, 0:2176] = wk[:, :, 0:2176]
        win[l, :, :, 2176 + 64:2176 + 96] = wk[:, :, 2176:2208]
    sh["win"] = win
    wmg = np.zeros((Ld, 8, 128, 4608), np.float32)
    for l in range(Ld):
        g = _km(w_gate[l], 8).reshape(128, 8, 3, 8, 128)
        wf = _km(w_fourier[l], 4).reshape(128, 4, 8, 128)
        wc = _km(w_conv_out[l], 4).reshape(128, 4, 8, 128)
        wm = _km(w_mla_out[l], 4).reshape(128, 4, 8, 128)
        for j in range(8):
            wmg[l, j, :, 0:3072] = g[:, :, :, j, :].reshape(128, 3072)
            wmg[l, j, :, 3072:3584] = wf[:, :, j, :].reshape(128, 512)
            wmg[l, j, :, 3584:4096] = wc[:, :, j, :].reshape(128, 512)
            wmg[l, j, :, 4096:4608] = wm[:, :, j, :].reshape(128, 512)
    sh["wmg"] = wmg
    sh["wq"] = np.stack([_km(w_q_up[l], 3) for l in range(Ld)])
    wkv4 = np.stack([_km(w_kv_up[l], 2) for l in range(Ld)]).reshape(Ld, 128, 2, 8, 128)
    sh["wkn"] = np.ascontiguousarray(wkv4[..., 0:64]).reshape(Ld, 128, 2, 512)
    sh["wkv"] = np.ascontiguousarray(wkv4[..., 64:128]).reshape(Ld, 128, 2, 512)
    sh["wo"] = np.stack([_km(w_out[l], 8) for l in range(Ld)])
    fu = np.zeros((Ld, 22, 128, 8, 256), np.float32)
    for l in range(Ld):
        u = _km(ffn_up[l], 8)
        for cc in range(22):
            fu[l, cc, :, :, 0:128] = u[:, :, cc * 128:(cc + 1) * 128]
            fu[l, cc, :, :, 128:256] = u[:, :, 2816 + cc * 128:2816 + (cc + 1) * 128]
    sh["fu"] = fu
    fd = np.zeros((Ld, 8, 128, 22, 128), np.float32)
    for l in range(Ld):
        d = _km(ffn_down[l], 22).reshape(128, 22, 8, 128)
        fd[l] = d.transpose(2, 0, 1, 3)
    sh["fd"] = fd

    in_maps = []
    for i in range(8):
        b = i // 2
        m = dict(sh)
        xpi = x_prompt[4 * i:4 * i + 4].reshape(1024, 8, 128)
        m["xp"] = np.ascontiguousarray(xpi.transpose(2, 1, 0))
        m["xs"] = np.ascontiguousarray(x_sample[b].reshape(2048, 8, 128).transpose(2, 1, 0))
        cvv = np.stack([c_ctx, c[b]], axis=-1)
        m["cv"] = np.ascontiguousarray(cvv.reshape(8, 128, 2).transpose(1, 0, 2))
        m["cck"] = np.ascontiguousarray(cache_ckv[b].reshape(Ld, 512, 2, 128).transpose(0, 3, 2, 1))
        m["ckr"] = np.ascontiguousarray(cache_krope[b].transpose(0, 2, 1))
        in_maps.append(m)

    if _CACHE.get('prep_only'):
        return in_maps
    res = run_bass_kernel_spmd(nc, in_maps, core_ids=list(range(8)))
    rs = res.results
    y_prompt = np.zeros((32, 256, 1024), np.float32)
    y_sample = np.zeros((4, 2048, 1024), np.float32)
    new_ckv = np.zeros((32, Ld, 256, 256), np.float32)
    new_kr = np.zeros((32, Ld, 256, 32), np.float32)
    for i in range(8):
        r = rs[i]
        y_prompt[4 * i:4 * i + 4] = np.asarray(r["yp"]).transpose(2, 1, 0).reshape(4, 256, 1024)
        if i % 2 == 0:
            y_sample[i // 2] = np.asarray(r["ys"]).transpose(2, 1, 0).reshape(2048, 1024)
        ck = np.asarray(r["ockv"])
        new_ckv[4 * i:4 * i + 4] = ck.transpose(3, 0, 2, 1).reshape(4, 256, Ld, 256).transpose(0, 2, 1, 3)
        kr = np.asarray(r["okr"])
        new_kr[4 * i:4 * i + 4] = kr.transpose(2, 0, 1).reshape(4, 256, Ld, 32).transpose(0, 2, 1, 3)
    return (y_prompt, y_sample, new_ckv, new_kr)
```
